# Optimizing a Trainium2 kernel written in Bass

```python
import math
import jax, jax.numpy as jnp
from jax import lax
import numpy as np

D_MODEL = 2048
BATCH = 8
SEQ = 4096
DEPTH = 1
DEC_BATCH = 8
DEC_SEQ = 16
PAST_LEN = 1024

CHUNK = 64
RET_HEADS = 8
RET_DK = D_MODEL // RET_HEADS
RET_DV = 2 * D_MODEL // RET_HEADS
RET_QK = RET_HEADS * RET_DK
RET_V = RET_HEADS * RET_DV
ROPE_BASE = 10000.0
SSM_INNER = 2 * D_MODEL
SSM_HEAD_DIM = 64
SSM_HEADS = SSM_INNER // SSM_HEAD_DIM
SSM_GROUPS = 8
SSM_HPG = SSM_HEADS // SSM_GROUPS
SSM_STATE = 128
SSM_CONV = 4
SSM_XBC = SSM_INNER + 2 * SSM_GROUPS * SSM_STATE
D_FF = 5632
N_BRANCH = 2
EPS = 1e-6
IN_SPLITS = (RET_QK, RET_QK, RET_V, RET_V, SSM_INNER, SSM_XBC, SSM_HEADS, N_BRANCH * D_MODEL)
IN_TOTAL = sum(IN_SPLITS)

kernel_name = "hybrid_retention_ssd_macaron_step"


def _rmsnorm(x, w):
    xf = x.astype(jnp.float32)
    y = xf * lax.rsqrt(jnp.mean(xf * xf, axis=-1, keepdims=True) + EPS)
    return (y * w.astype(jnp.float32)).astype(x.dtype)


def _swiglu(x, w_gate, w_up, w_down):
    return (jax.nn.silu(x @ w_gate) * (x @ w_up)) @ w_down


def _rope(x, pos):
    half = x.shape[-1] // 2
    inv = ROPE_BASE ** (-jnp.arange(half, dtype=jnp.float32) / half)
    ang = pos[:, None] * inv[None, :]
    cos = jnp.cos(ang)[None, :, None, :]
    sin = jnp.sin(ang)[None, :, None, :]
    xf = x.astype(jnp.float32)
    x1, x2 = xf[..., :half], xf[..., half:]
    return jnp.concatenate([x1 * cos - x2 * sin, x1 * sin + x2 * cos], axis=-1)


def _to_chunks(a, chunk):
    b, l = a.shape[:2]
    return jnp.moveaxis(a.reshape((b, l // chunk, chunk) + a.shape[2:]), 1, 0)


def _from_chunks(a):
    a = jnp.moveaxis(a, 0, 1)
    return a.reshape((a.shape[0], a.shape[1] * a.shape[2]) + a.shape[3:])


def _retention(q, k, v, s0, chunk):
    lg = jnp.log1p(-jnp.exp2(-5.0 - jnp.arange(RET_HEADS, dtype=jnp.float32)))
    idx = jnp.arange(chunk, dtype=jnp.float32)
    diff = idx[:, None] - idx[None, :]
    causal = diff >= 0
    decay_mask = jnp.where(causal[None], jnp.exp(jnp.where(causal, diff, 0.0)[None] * lg[:, None, None]), 0.0)
    inner_decay = jnp.exp((idx[:, None] + 1.0) * lg[None, :])
    state_decay = jnp.exp((chunk - 1.0 - idx[:, None]) * lg[None, :])
    chunk_decay = jnp.exp(chunk * lg)

    def step(s, inp):
        qc, kc, vc = inp
        scores = jnp.einsum('bihd,bjhd->bhij', qc, kc) * decay_mask[None]
        o = (jnp.einsum('bhij,bjhv->bihv', scores, vc)
             + jnp.einsum('bihd,bhdv->bihv', qc, s) * inner_decay[None, :, :, None])
        s_new = (s * chunk_decay[None, :, None, None]
                 + jnp.einsum('bjhd,bjhv->bhdv', kc * state_decay[None, :, :, None], vc))
        return s_new, o

    s, o = lax.scan(step, s0, (_to_chunks(q, chunk), _to_chunks(k, chunk), _to_chunks(v, chunk)))
    return _from_chunks(o), s


def _ssd(x, dt, a, bm, cm, h0, chunk):
    tri = jnp.tril(jnp.ones((chunk, chunk), dtype=bool))[None, :, :, None, None]

    def step(h, inp):
        xc, dtc, bc, cc = inp
        cum = jnp.cumsum(dtc * a[None, None], axis=1)
        seg = cum[:, :, None] - cum[:, None, :]
        lmat = jnp.where(tri, jnp.exp(jnp.where(tri, seg, 0.0)), 0.0)
        xdt = xc * dtc[..., None]
        cb = jnp.einsum('bign,bjgn->bijg', cc, bc)
        y = (jnp.einsum('bijg,bijgh,bjghp->bighp', cb, lmat, xdt)
             + jnp.einsum('bign,bghpn->bighp', cc, h) * jnp.exp(cum)[..., None])
        dend = jnp.exp(cum[:, -1:] - cum)
        h_new = (h * jnp.exp(cum[:, -1])[..., None, None]
                 + jnp.einsum('bjgn,bjgh,bjghp->bghpn', bc, dend, xdt))
        return h_new, y

    h, y = lax.scan(step, h0, (_to_chunks(x, chunk), _to_chunks(dt, chunk),
                               _to_chunks(bm, chunk), _to_chunks(cm, chunk)))
    return _from_chunks(y), h


def _layer(x, s_ret, s_ssm, s_conv, pos0, chunk, p):
    b, l, _ = x.shape
    f32 = jnp.float32
    h = x + 0.5 * _swiglu(_rmsnorm(x, p['norm_ffn1']), p['ffn1_w_gate'], p['ffn1_w_up'], p['ffn1_w_down'])
    n = _rmsnorm(h, p['norm_mix'])
    proj = n @ p['w_in']
    offs = [int(o) for o in np.cumsum(IN_SPLITS)[:-1]]
    q, k, v, g_ret, z, xbc, dt_raw, gates = jnp.split(proj, offs, axis=-1)

    pos = jnp.arange(l, dtype=f32) + pos0
    qh = _rope(q.reshape(b, l, RET_HEADS, RET_DK), pos)
    kh = _rope(k.reshape(b, l, RET_HEADS, RET_DK), pos) * (RET_DK ** -0.5)
    vh = v.reshape(b, l, RET_HEADS, RET_DV).astype(f32)
    o, s_ret_new = _retention(qh, kh, vh, s_ret.astype(f32), chunk)
    mu = jnp.mean(o, axis=-1, keepdims=True)
    var = jnp.mean(jnp.square(o - mu), axis=-1, keepdims=True)
    o = ((o - mu) * lax.rsqrt(var + EPS)).reshape(b, l, RET_V) * p['ret_norm_w'].astype(f32)
    y_ret = (jax.nn.silu(g_ret.astype(f32)) * o).astype(x.dtype) @ p['w_out_ret']

    conv_in = jnp.concatenate([s_conv.astype(xbc.dtype), xbc], axis=1)
    conv = p['conv_b'] + sum(conv_in[:, i:i + l] * p['conv_w'][i] for i in range(SSM_CONV))
    conv = jax.nn.silu(conv)
    conv_new = conv_in[:, -(SSM_CONV - 1):]
    xs, bm, cm = jnp.split(conv, [SSM_INNER, SSM_INNER + SSM_GROUPS * SSM_STATE], axis=-1)
    xs = xs.reshape(b, l, SSM_GROUPS, SSM_HPG, SSM_HEAD_DIM).astype(f32)
    bm = bm.reshape(b, l, SSM_GROUPS, SSM_STATE).astype(f32)
    cm = cm.reshape(b, l, SSM_GROUPS, SSM_STATE).astype(f32)
    dt = jax.nn.softplus(dt_raw.astype(f32) + p['dt_bias'].astype(f32)).reshape(b, l, SSM_GROUPS, SSM_HPG)
    a = -jnp.exp(p['a_log'].astype(f32)).reshape(SSM_GROUPS, SSM_HPG)
    h0 = s_ssm.astype(f32).reshape(b, SSM_GROUPS, SSM_HPG, SSM_HEAD_DIM, SSM_STATE)
    ys, h_ssm = _ssd(xs, dt, a, bm, cm, h0, chunk)
    ys = ys + xs * p['d_skip'].astype(f32).reshape(SSM_GROUPS, SSM_HPG)[..., None]
    ys = ys.reshape(b, l, SSM_INNER) * jax.nn.silu(z.astype(f32))
    yg = ys.reshape(b, l, SSM_GROUPS, SSM_INNER // SSM_GROUPS)
    yg = yg * lax.rsqrt(jnp.mean(yg * yg, axis=-1, keepdims=True) + EPS)
    yg = yg.reshape(b, l, SSM_INNER) * p['ssm_norm_w'].astype(f32)
    y_ssm = yg.astype(x.dtype) @ p['w_out_ssm']

    g_a, g_b = jnp.split(jax.nn.sigmoid(gates + p['b_gate']), 2, axis=-1)
    h = h + (g_a * y_ret + g_b * y_ssm) @ p['w_out']
    h = h + 0.5 * _swiglu(_rmsnorm(h, p['norm_ffn2']), p['ffn2_w_gate'], p['ffn2_w_up'], p['ffn2_w_down'])
    s_ssm_new = h_ssm.reshape(b, SSM_HEADS, SSM_HEAD_DIM, SSM_STATE)
    return h, s_ret_new.astype(x.dtype), s_ssm_new.astype(x.dtype), conv_new


def setup_inputs(seed: int = 0) -> dict:
    key = jax.random.key(seed)
    ks = jax.random.split(key, 32)
    f32 = jnp.float32

    def nrm(k, shape, scale):
        return jax.random.normal(k, shape, f32) * scale

    def gain(k, n):
        return 1.0 + 0.02 * jax.random.normal(k, (n,), f32)

    dt0 = jnp.exp(jax.random.uniform(ks[16], (SSM_HEADS,), f32) * (math.log(0.1) - math.log(1e-3)) + math.log(1e-3))
    return {
        "x_prompt": nrm(ks[0], (BATCH, SEQ, D_MODEL), 1.0),
        "x_sample": nrm(ks[1], (DEC_BATCH, DEC_SEQ, D_MODEL), 1.0),
        "state_ret": nrm(ks[2], (DEC_BATCH, RET_HEADS, RET_DK, RET_DV), 0.5),
        "state_ssm": nrm(ks[3], (DEC_BATCH, SSM_HEADS, SSM_HEAD_DIM, SSM_STATE), 0.5),
        "state_conv": nrm(ks[4], (DEC_BATCH, SSM_CONV - 1, SSM_XBC), 1.0),
        "norm_ffn1": gain(ks[5], D_MODEL),
        "ffn1_w_gate": nrm(ks[6], (D_MODEL, D_FF), D_MODEL ** -0.5),
        "ffn1_w_up": nrm(ks[7], (D_MODEL, D_FF), D_MODEL ** -0.5),
        "ffn1_w_down": nrm(ks[8], (D_FF, D_MODEL), D_FF ** -0.5),
        "norm_mix": gain(ks[9], D_MODEL),
        "w_in": nrm(ks[10], (D_MODEL, IN_TOTAL), D_MODEL ** -0.5),
        "b_gate": nrm(ks[11], (N_BRANCH * D_MODEL,), 0.02),
        "ret_norm_w": gain(ks[12], RET_V),
        "w_out_ret": nrm(ks[13], (RET_V, D_MODEL), RET_V ** -0.5),
        "conv_w": nrm(ks[14], (SSM_CONV, SSM_XBC), SSM_CONV ** -0.5),
        "conv_b": nrm(ks[15], (SSM_XBC,), 0.02),
        "dt_bias": dt0 + jnp.log(-jnp.expm1(-dt0)),
        "a_log": jnp.log(jax.random.uniform(ks[17], (SSM_HEADS,), f32, 1.0, 16.0)),
        "d_skip": gain(ks[18], SSM_HEADS),
        "ssm_norm_w": gain(ks[19], SSM_INNER),
        "w_out_ssm": nrm(ks[20], (SSM_INNER, D_MODEL), SSM_INNER ** -0.5),
        "w_out": nrm(ks[21], (D_MODEL, D_MODEL), D_MODEL ** -0.5),
        "norm_ffn2": gain(ks[22], D_MODEL),
        "ffn2_w_gate": nrm(ks[23], (D_MODEL, D_FF), D_MODEL ** -0.5),
        "ffn2_w_up": nrm(ks[24], (D_MODEL, D_FF), D_MODEL ** -0.5),
        "ffn2_w_down": nrm(ks[25], (D_FF, D_MODEL), D_FF ** -0.5),
        "norm_final": gain(ks[26], D_MODEL),
    }


def reference(x_prompt, x_sample, state_ret, state_ssm, state_conv,
              norm_ffn1, ffn1_w_gate, ffn1_w_up, ffn1_w_down, norm_mix, w_in, b_gate,
              ret_norm_w, w_out_ret, conv_w, conv_b, dt_bias, a_log, d_skip, ssm_norm_w,
              w_out_ssm, w_out, norm_ffn2, ffn2_w_gate, ffn2_w_up, ffn2_w_down, norm_final):
    p = dict(norm_ffn1=norm_ffn1, ffn1_w_gate=ffn1_w_gate, ffn1_w_up=ffn1_w_up, ffn1_w_down=ffn1_w_down,
             norm_mix=norm_mix, w_in=w_in, b_gate=b_gate, ret_norm_w=ret_norm_w, w_out_ret=w_out_ret,
             conv_w=conv_w, conv_b=conv_b, dt_bias=dt_bias, a_log=a_log, d_skip=d_skip,
             ssm_norm_w=ssm_norm_w, w_out_ssm=w_out_ssm, w_out=w_out, norm_ffn2=norm_ffn2,
             ffn2_w_gate=ffn2_w_gate, ffn2_w_up=ffn2_w_up, ffn2_w_down=ffn2_w_down)
    b, l, _ = x_prompt.shape
    ret_p0 = jnp.zeros((b, RET_HEADS, RET_DK, RET_DV), jnp.float32)
    ssm_p0 = jnp.zeros((b, SSM_HEADS, SSM_HEAD_DIM, SSM_STATE), jnp.float32)
    conv_p0 = jnp.zeros((b, SSM_CONV - 1, SSM_XBC), x_prompt.dtype)
    hp, hs = x_prompt, x_sample
    for _ in range(DEPTH):
        hp, ret_p, ssm_p, conv_p = _layer(hp, ret_p0, ssm_p0, conv_p0, 0, min(CHUNK, l), p)
        hs, ret_s, ssm_s, conv_s = _layer(hs, state_ret, state_ssm, state_conv, PAST_LEN, hs.shape[1], p)
    y_prompt = _rmsnorm(hp, norm_final)
    y_sample = _rmsnorm(hs, norm_final)
    return (y_prompt, y_sample, ret_p, ssm_p, conv_p, ret_s, ssm_s, conv_s)
```

```python
import math
from contextlib import ExitStack
import numpy as np
import concourse.bass as bass
import concourse.mybir as mybir
from concourse.bass_utils import run_bass_kernel_spmd

F32 = mybir.dt.float32
BF16 = mybir.dt.bfloat16
AF = mybir.ActivationFunctionType
ALU = mybir.AluOpType

D = 2048
DFF = 5632
NKC = 16
SEQ = 4096
DEC = 16
PAST = 1024
NTOK = SEQ + DEC
EPS = 1e-6
OFF_Q, OFF_K, OFF_V, OFF_G, OFF_Z, OFF_X, OFF_DT, OFF_GA, OFF_GB = 0, 2048, 4096, 8192, 12288, 16384, 22528, 22592, 24640
LG = [math.log1p(-2.0 ** (-5.0 - h)) for h in range(8)]
DBG_NOILV = False
DBG_ONEHB = False
DBG_OLDECL = True
WQ = "sp"
MQ = "sp"
WORDER = ["ffn1_w_gate", "ffn1_w_up", "ffn1_w_down", "w_in", "w_out_ret", "w_out_ssm", "w_out",
          "ffn2_w_gate", "ffn2_w_up", "ffn2_w_down"]


class Res:
    __slots__ = ("name", "last_w", "readers")

    def __init__(self, name=""):
        self.name = name
        self.last_w = None
        self.readers = {}


class Op:
    __slots__ = ("eng", "emit", "deps", "signal", "count", "key", "pos", "is_dma")

    def __init__(self, eng, emit, key):
        self.eng = eng
        self.emit = emit
        self.deps = []
        self.signal = False
        self.count = 0
        self.key = key
        self.is_dma = key is not None
        self.pos = 0


class Prog:
    ENGS = ("pe", "act", "dve", "pool", "sp")

    def __init__(self):
        self.ops = []
        self.by_eng = {e: [] for e in self.ENGS}
        self.seen = {e: {} for e in self.ENGS}
        self.keycnt = {}
        self.cpos = {e: 0 for e in self.ENGS}
        self.last_compute = {}
        self.dma_last = {}

    def _k(self, p):
        return ("k", p.key) if p.is_dma else ("e", p.eng)

    def add(self, eng, emit, reads=(), writes=(), key=None, extra_deps=()):
        op = Op(eng, emit, key)
        idx = len(self.ops)
        self.ops.append(op)
        self.by_eng[eng].append(op)
        if key is not None:
            self.keycnt[key] = self.keycnt.get(key, 0) + 1
            op.pos = self.keycnt[key]
            self.dma_last[key] = idx
        elif emit is not None:
            self.cpos[eng] += 1
            op.pos = self.cpos[eng]
            self.last_compute[eng] = idx
        best = {}

        def cand(d):
            p = self.ops[d]
            k = self._k(p)
            if k not in best or self.ops[best[k]].pos < p.pos:
                best[k] = d

        for r in reads:
            if r.last_w is not None:
                cand(r.last_w)
        for r in writes:
            if r.last_w is not None:
                cand(r.last_w)
            for d in r.readers.values():
                cand(d)
        for d in extra_deps:
            cand(d)
        seen = self.seen[eng]
        for k, d in best.items():
            p = self.ops[d]
            if d == idx:
                continue
            if (not p.is_dma) and p.eng == "pe" and eng == "pe" and not op.is_dma and emit is not None:
                continue
            if seen.get(k, 0) >= p.pos:
                continue
            seen[k] = p.pos
            op.deps.append(d)
            p.signal = True
        if emit is not None:
            k = self._k(op)
            for r in reads:
                r.readers[k] = idx
            for r in writes:
                r.last_w = idx
                r.readers = {}
        return op

    def fence(self, engs=("pe", "act", "dve", "sp")):
        deps = [self.last_compute[e] for e in engs if e in self.last_compute]
        deps += [d for k, d in self.dma_last.items() if not (str(k).startswith("w") or str(k).startswith("cv"))]
        for e in engs:
            self.add(e, None, extra_deps=deps)

    def emit_all(self, nc, final_wait_eng="sp"):
        with ExitStack() as st:
            esem = {e: st.enter_context(nc.semaphore("s_" + e)) for e in self.ENGS}
            ksem = {k: st.enter_context(nc.semaphore("k_" + str(k))) for k in self.keycnt}
            for e in self.ENGS:
                c = 0
                for op in self.by_eng[e]:
                    if op.is_dma:
                        op.count = 16 * op.pos
                    elif op.signal:
                        c += 1
                        op.count = c
            block = st.enter_context(nc.Block())

            def run(e, eng):
                waited = {}
                for op in self.by_eng[e]:
                    for d in op.deps:
                        p = self.ops[d]
                        s = ksem[p.key] if p.is_dma else esem[p.eng]
                        wk = self._k(p)
                        if waited.get(wk, 0) >= p.count:
                            continue
                        waited[wk] = p.count
                        eng.wait_ge(s, p.count)
                    if op.emit is None:
                        continue
                    ins = op.emit(eng)
                    if op.is_dma:
                        ins.then_inc(ksem[op.key], 16)
                    elif op.signal:
                        ins.then_inc(esem[e], 1)
                if e == final_wait_eng:
                    for k, n in self.keycnt.items():
                        eng.wait_ge(ksem[k], 16 * n)

            block.tensor(lambda eng: run("pe", eng))
            block.scalar(lambda eng: run("act", eng))
            block.vector(lambda eng: run("dve", eng))
            block.gpsimd(lambda eng: run("pool", eng))
            block.sync(lambda eng: run("sp", eng))


def I(method, *a, **kw):
    return lambda e: getattr(e, method)(*a, **kw)


def build(tiles, stop=None):
    nc = bass.Bass("TRN2", target_bir_lowering=False)
    P = Prog()

    def din(name, shape, dt=F32):
        return nc.dram_tensor(name, list(shape), dt, kind="ExternalInput").ap()

    def dout(name, shape, dt=F32):
        return nc.dram_tensor(name, list(shape), dt, kind="ExternalOutput").ap()

    xT = din("xT", [D, NTOK])
    W = {}
    for nm, shp in (("ffn1_w_gate", [D, DFF]), ("ffn1_w_up", [D, DFF]), ("ffn1_w_down", [DFF, D]),
                    ("w_in", [D, 26688]), ("w_out_ret", [4096, D]), ("w_out_ssm", [4096, D]), ("w_out", [D, D]),
                    ("ffn2_w_gate", [D, DFF]), ("ffn2_w_up", [D, DFF]), ("ffn2_w_down", [DFF, D])):
        W[nm] = {"f32": din(nm, shp), "shape": shp, "res": [],
                 "bf": nc.dram_tensor(nm + "_bf", list(shp), BF16, kind="Internal").ap()}
    small_d = din("small", [128, 640])
    cst_d = din("cst", [128, 1088])
    rope_d = din("rope", [128, 2, NTOK])
    dqk_d = din("dqk", [8, 128, 2, 528])
    sret_d = din("sret", [8, 256, 512])
    sssm_d = din("sssmT", [128, 4096])
    sconv_d = din("sconv", [128, 144])
    yT = dout("yT", [D, NTOK])
    ret_o = {"p": dout("ret_p", [8, 256, 512]), "s": dout("ret_s", [8, 256, 512])}
    ssm_o = {"p": dout("ssm_pT", [128, 4096]), "s": dout("ssm_sT", [128, 4096])}
    conv_o = {"p": dout("conv_p", [128, 144]), "s": dout("conv_s", [128, 144])}
    r_ret_o = {k: [Res() for _ in range(8)] for k in "ps"}
    r_ssm_o = {k: [Res() for _ in range(8)] for k in "ps"}

    with ExitStack() as st:
        def sb(name, shape, dt):
            return st.enter_context(nc.sbuf_tensor("sb_" + name, list(shape), dt))

        h = sb("h", [128, 16, 512], F32)
        r_h = [Res("h%d" % i) for i in range(16)]
        n = sb("n", [128, 16, 512], BF16)
        r_n = [Res("n%d" % i) for i in range(16)]
        NS = 4
        wsl = [sb("ws%d" % i, [128, 4096], BF16) for i in range(NS)]
        r_ws = [Res("ws%d" % i) for i in range(NS)]
        small = sb("small", [128, 640], F32)
        r_small = Res("small")
        cst = sb("cst", [128, 1088], F32)
        r_cst = Res("cst")
        identb = sb("identb", [128, 128], BF16)
        onesb = sb("onesb", [128, 128], BF16)
        epsc = sb("epsc", [128, 2], F32)
        rope = sb("rope", [128, 2, 512], F32)
        r_rope = Res("rope")
        dqk2 = [sb("dqk%d" % i, [128, 2, 512], F32) for i in range(2)]
        r_dqk2 = [Res(), Res()]
        hist = sb("hist", [128, 48, 3], F32)
        r_hist = [Res() for _ in range(48)]
        abc = sb("abc", [128, 64], F32)
        sqb = [sb("sqb%d" % i, [128, 512], BF16) for i in range(2)]
        r_sqb = [Res(), Res()]
        rs = sb("rs", [128, 512], F32)
        r_rs = Res("rs")
        AW = 26000
        ar = sb("arena", [128, AW], F32)
        pb = [st.enter_context(nc.psum_tensor("pb%d" % i, [128, 512], F32)) for i in range(8)]
        r_pb = [Res("pb%d" % i) for i in range(8)]
        bank_i = [0]

        def bank():
            i = bank_i[0] % 8
            bank_i[0] += 1
            return pb[i], r_pb[i]

        def carve(off, nelem, dt, pat=None, **kw):
            nb = nelem * (4 if dt == F32 else 2)
            assert off % 4 == 0 and nb % 4 == 0 and off + nb <= AW * 4, (off, nb)
            v = ar[:, off // 4:(off + nb) // 4]
            if dt != F32:
                v = v.bitcast(dt)
            if pat:
                v = v.rearrange(pat, **kw)
            return v

        normw = small[:, 0:64].rearrange("p (a k) -> p a k", a=4)
        retw = small[:, 64:96]
        ssmw = small[:, 96:128]
        bgate = small[:, 128:160]
        convw = small[:, 160:352].rearrange("p (c i) -> p c i", i=4)
        convb = small[:, 352:400]
        dtb = small[:, 400:464]
        alog = small[:, 464:528]
        dskip = small[:, 528:592]
        bdmask = cst[:, 0:128]
        bdU = cst[:, 128:256]
        bdones = cst[:, 256:384]
        triloc = cst[:, 384:448]
        identf = cst[:, 448:576]
        onesf = cst[:, 576:704]
        causal = cst[:, 704:832]
        sel64 = cst[:, 832:960]
        sel16 = cst[:, 960:1088]

        P.add(MQ, I("dma_start", out=small[:], in_=small_d), writes=[r_small], key="c0")
        P.add(MQ, I("dma_start", out=cst[:], in_=cst_d), writes=[r_cst], key="c1")
        r_idb = Res("identb")
        P.add("act", I("activation", out=identb[:], in_=identf, func=AF.Copy), reads=[r_cst], writes=[r_idb])
        P.add("dve", I("tensor_copy", onesb[:], onesf), reads=[r_cst], writes=[r_idb])
        P.add("dve", I("memset", epsc[:, 0:1], EPS), writes=[r_idb])
        P.add("dve", I("memset", epsc[:, 1:2], 1.0), writes=[r_idb])
        r_abc = Res("abc")
        P.add("act", I("activation", out=abc[:], in_=alog, func=AF.Exp), reads=[r_small], writes=[r_abc])
        P.add("dve", I("tensor_scalar", abc[:], abc[:], -1.0, None, op0=ALU.mult), reads=[r_abc], writes=[r_abc])

        ws_i = [0]

        def wload(wsrc, k0c, nkc, c0, ncols):
            s = ws_i[0] % NS
            ws_i[0] += 1
            assert nkc * ncols <= 4096
            view = wsl[s][:, 0:nkc * ncols].rearrange("p (k c) -> p k c", k=nkc)
            src = wsrc["bf"][k0c * 128:(k0c + nkc) * 128, c0:c0 + ncols].rearrange("(k p) c -> p k c", p=128)
            P.add(WQ, I("dma_start", out=view, in_=src), reads=wsrc["res"], writes=[r_ws[s]], key="w%d" % s)
            return view, r_ws[s]

        def mm(out, lhsT, rhs, start, stop, rd, wr):
            P.add("pe", I("matmul", out, lhsT, rhs, start=start, stop=stop, skip_group_check=True), reads=rd, writes=wr)

        def proj_fm(wap, c0, ncols_total, T, src, r_src, nk, epilogue, colw=256):
            kpl = 4096 // colw
            nkh = (nk + kpl - 1) // kpl
            for cb in range(0, ncols_total, colw):
                cw = min(colw, ncols_total - cb)
                slabs = []
                for kh in range(nkh):
                    k0 = kh * kpl
                    kn = min(kpl, nk - k0)
                    slabs.append((k0, kn) + wload(wap, k0, kn, c0 + cb, cw))
                for jj in range(0, cw, 128):
                    jw = min(128, cw - jj)
                    ps, r_ps = bank()
                    for (k0, kn, view, r_w) in slabs:
                        for kk in range(kn):
                            kc = k0 + kk
                            mm(ps[0:jw, 0:T], view[:, kk, jj:jj + jw], src(kc), kc == 0, kc == nk - 1,
                               [r_w, r_src[kc]], [r_ps])
                    epilogue((cb + jj) // 128, ps, r_ps)

        def proj_tm(wap, c0, ncols, T, TB, NTB, epilogue):
            view, r_w = wload(wap, 0, 16, c0, ncols)
            for tb in range(NTB):
                ps, r_ps = bank()
                for kc in range(16):
                    mm(ps[0:TB, 0:ncols], n[:, kc, tb * TB:(tb + 1) * TB], view[:, kc, :], kc == 0, kc == 15,
                       [r_w, r_n[kc]], [r_ps])
                epilogue(tb, ps, r_ps)

        def proj_tm_gen(wap, c0, ncols, T, TB, NTB, epilogue):
            view, r_w = wload(wap, 0, 16, c0, ncols)
            for tb in range(NTB):
                ps, r_ps = bank()
                for kc in range(16):
                    mm(ps[0:TB, 0:ncols], n[:, kc, tb * TB:(tb + 1) * TB], view[:, kc, :], kc == 0, kc == 15,
                       [r_w, r_n[kc]], [r_ps])
                epilogue(tb, ps, r_ps)
                yield

        def rmsnorm(T, wi, outf):
            ps, r_ps = bank()
            for kc in range(16):
                b = kc % 2
                P.add("act", I("activation", out=sqb[b][:, 0:T], in_=h[:, kc, 0:T], func=AF.Square),
                      reads=[r_h[kc]], writes=[r_sqb[b]])
                mm(ps[:, 0:T], onesb[:], sqb[b][:, 0:T], kc == 0, kc == 15, [r_sqb[b], r_idb], [r_ps])
            P.add("act", I("activation", out=rs[:, 0:T], in_=ps[:, 0:T], func=AF.Sqrt, bias=epsc[:, 0:1], scale=1.0 / D),
                  reads=[r_ps, r_idb], writes=[r_rs])
            P.add("dve", I("reciprocal", rs[:, 0:T], rs[:, 0:T]), reads=[r_rs], writes=[r_rs])
            for kc in range(16):
                o, r_o = outf(kc)
                P.add("dve", I("scalar_tensor_tensor", out=o, in0=h[:, kc, 0:T], scalar=normw[:, wi, kc:kc + 1],
                                                                          in1=rs[:, 0:T], op0=ALU.mult, op1=ALU.mult),
                      reads=[r_h[kc], r_rs, r_small], writes=[r_o])

        def ffn(T, wg, wu, wd):
            act = carve(0, 44 * 512, BF16, "p (j t) -> p j t", j=44)
            r_act = [Res() for _ in range(44)]
            sgt = [carve(45056 + 2048 * i, 512, F32) for i in range(2)]
            r_sgt = [Res(), Res()]
            nsrc = lambda kc: n[:, kc, 0:T]
            pend = {}

            def ep_gate(j, ps, r_ps):
                b = j % 2
                P.add("act", I("activation", out=sgt[b][:, 0:T], in_=ps[:, 0:T], func=AF.Silu),
                      reads=[r_ps], writes=[r_sgt[b]])

            def ep_up(j, ps, r_ps):
                b = j % 2
                P.add("dve", I("tensor_tensor", out=act[:, j, 0:T], in0=sgt[b][:, 0:T], in1=ps[:, 0:T], op=ALU.mult),
                      reads=[r_ps, r_sgt[b]], writes=[r_act[j]])

            for cb in range(0, DFF, 256):
                proj_fm(wg, cb, 256, T, nsrc, r_n, 16, lambda j, ps, r, cb=cb: ep_gate(cb // 128 + j, ps, r))
                proj_fm(wu, cb, 256, T, nsrc, r_n, 16, lambda j, ps, r, cb=cb: ep_up(cb // 128 + j, ps, r))

            def ep_down(j, ps, r_ps):
                P.add("dve", I("scalar_tensor_tensor", out=h[:, j, 0:T], in0=ps[:, 0:T], scalar=0.5, in1=h[:, j, 0:T],
                                                              op0=ALU.mult, op1=ALU.add),
                      reads=[r_ps, r_h[j]], writes=[r_h[j]])

            proj_fm(wd, 0, D, T, lambda kc: act[:, kc, 0:T], r_act, 44, ep_down)

        converted = [False]
        for (t0, T, C, kind) in tiles:
            TB = min(128, T)
            NTB = T // TB
            CPB = TB // C
            first = (t0 == 0) or kind == "s"
            is_last = (kind == "s") or (t0 + T == SEQ)
            gC = [math.exp(C * LG[hd]) for hd in range(8)]
            P.fence()
            for kq in range(4):
                P.add(MQ, I("dma_start", out=h[:, kq * 4:(kq + 1) * 4, 0:T],
                                                         in_=xT[kq * 512:(kq + 1) * 512, t0:t0 + T].rearrange("(k p) t -> p k t", p=128)),
                      writes=r_h[kq * 4:(kq + 1) * 4], key="x%d" % kq)
            P.add(MQ, I("dma_start", out=rope[:, :, 0:T], in_=rope_d[:, :, t0:t0 + T]), writes=[r_rope], key="c2")
            dqo = 0 if kind == "p" else 512
            if first:
                if kind == "p":
                    P.add("dve", I("memset", hist[:], 0.0), writes=r_hist)
                else:
                    P.add(MQ, I("dma_start", out=hist[:], in_=sconv_d.rearrange("p (c i) -> p c i", i=3)),
                          writes=r_hist, key="c4")
            if not converted[0]:
                converted[0] = True
                prev_cv = []
                for nm in WORDER:
                    R_, C_ = W[nm]["shape"]
                    rp = max(16, ((4 << 20) // (C_ * 4)) // 16 * 16)
                    for r0 in range(0, R_, rp):
                        r1 = min(R_, r0 + rp)
                        rr = Res()
                        W[nm]["res"].append(rr)
                        P.add("pool", I("dma_start", out=W[nm]["bf"][r0:r1, :], in_=W[nm]["f32"][r0:r1, :]), reads=prev_cv[-2:-1], writes=[rr],
                              key="cv%d" % (len(prev_cv) % 2))
                        prev_cv.append(rr)
            rmsnorm(T, 0, lambda kc: (n[:, kc, 0:T], r_n[kc]))
            ffn(T, W["ffn1_w_gate"], W["ffn1_w_up"], W["ffn1_w_down"])
            if stop == "ffn1":
                continue
            P.fence()
            rmsnorm(T, 1, lambda kc: (n[:, kc, 0:T], r_n[kc]))
            nsrc = lambda kc: n[:, kc, 0:T]
            yAB = carve(0, 32 * 512, BF16, "p (j t) -> p j t", j=32)
            r_yAB = [Res() for _ in range(32)]
            mbuf = carve(32768, 16 * 512, BF16, "p (j t) -> p j t", j=16)
            r_m = [Res() for _ in range(16)]
            TOFF = 49152
            cosT = rope[:, 0, 0:T]
            sinT = rope[:, 1, 0:T]

            o = [TOFF]

            def alloc(nelem, dt, pat=None, **kw):
                nb = nelem * (4 if dt == F32 else 2)
                nb = (nb + 31) // 32 * 32
                v = carve(o[0], nb // (4 if dt == F32 else 2), dt)
                v = v[:, 0:nelem]
                if pat:
                    v = v.rearrange(pat, **kw)
                o[0] += nb
                return v

            qh = alloc(2 * 512, BF16, "p (a t) -> p a t", a=2); r_qh = Res()
            kh = alloc(2 * 512, BF16, "p (a t) -> p a t", a=2); r_kh = Res()
            t1 = alloc(512, F32); t2 = alloc(512, F32); t3 = alloc(512, F32)
            r_t1, r_t2, r_t3 = Res(), Res(), Res()
            vtm = alloc(4 * 512, BF16, "p (b c) -> p b c", b=4); r_vtm = Res()
            ktm = alloc(4 * 256, BF16, "p (b c) -> p b c", b=4); r_ktm = Res()
            sg = alloc(4 * 512, F32, "p (a t) -> p a t", a=4); r_sg = [Res() for _ in range(4)]
            oT = alloc(4 * 512, F32, "p (a t) -> p a t", a=4); r_oT = [Res() for _ in range(4)]
            Sf = alloc(2 * 512, F32, "p (a v) -> p a v", a=2); r_Sf = Res()
            Sb = alloc(2 * 512, BF16, "p (a v) -> p a v", a=2); r_Sb = Res()
            pT = [alloc(512, BF16) for _ in range(2)]; r_pT = [Res(), Res()]
            obf = alloc(512, BF16); r_obf = Res()
            mean = alloc(512, F32); msq = alloc(512, F32); r_mean, r_msq = Res(), Res()
            gtmp = [alloc(512, F32) for _ in range(2)]; r_gtmp = [Res(), Res()]

            for hd in range(8):
                dqkb = dqk2[hd % 2]
                r_dq = r_dqk2[hd % 2]
                P.add(MQ, I("dma_start", out=dqkb[:, :, 0:T], in_=dqk_d[hd, :, :, dqo:dqo + T]), writes=[r_dq], key="dq%d" % (hd % 2))
                dq = dqkb[:, 0, 0:T]
                dk = dqkb[:, 1, 0:T]
                gT = math.exp(T * LG[hd])
                if first and kind == "p":
                    P.add("dve", I("memset", Sf[:], 0.0), writes=[r_Sf])
                    P.add("dve", I("memset", Sb[:], 0.0), writes=[r_Sb])
                else:
                    src = (sret_d if first else ret_o[kind])[hd].rearrange("(a p) v -> p a v", p=128)
                    P.add(MQ, I("dma_start", out=Sf[:], in_=src), reads=[r_ret_o[kind][hd]], writes=[r_Sf], key="sr")
                    P.add("act", I("activation", out=Sb[:], in_=Sf[:], func=AF.Copy), reads=[r_Sf], writes=[r_Sb])
                for (off, dst, r_dst, dtab) in ((OFF_Q, qh, r_qh, dq), (OFF_K, kh, r_kh, dk)):
                    pss = []
                    proj_fm(W["w_in"], off + hd * 256, 256, T, nsrc, r_n, 16, lambda j, ps, r: pss.append((ps, r)))
                    (p1, r1), (p2, r2) = pss
                    dve = lambda f, rd, wr: P.add("dve", f, reads=rd, writes=wr)
                    dve(I("tensor_tensor", out=t1[:, 0:T], in0=p1[:, 0:T], in1=cosT, op=ALU.mult), [r1, r_rope], [r_t1])
                    dve(I("tensor_tensor", out=t2[:, 0:T], in0=p2[:, 0:T], in1=sinT, op=ALU.mult), [r2, r_rope], [r_t2])
                    dve(I("tensor_tensor", out=t3[:, 0:T], in0=t1[:, 0:T], in1=t2[:, 0:T], op=ALU.subtract), [r_t1, r_t2], [r_t3])
                    dve(I("tensor_tensor", out=dst[:, 0, 0:T], in0=t3[:, 0:T], in1=dtab, op=ALU.mult), [r_t3, r_dq], [r_dst])
                    dve(I("tensor_tensor", out=t1[:, 0:T], in0=p1[:, 0:T], in1=sinT, op=ALU.mult), [r1, r_rope], [r_t1])
                    dve(I("tensor_tensor", out=t2[:, 0:T], in0=p2[:, 0:T], in1=cosT, op=ALU.mult), [r2, r_rope], [r_t2])
                    dve(I("tensor_tensor", out=t3[:, 0:T], in0=t1[:, 0:T], in1=t2[:, 0:T], op=ALU.add), [r_t1, r_t2], [r_t3])
                    dve(I("tensor_tensor", out=dst[:, 1, 0:T], in0=t3[:, 0:T], in1=dtab, op=ALU.mult), [r_t3, r_dq], [r_dst])
                for half in range(2):
                    def ep_v(tb, ps, r_ps, half=half):
                        P.add("act", I("activation", out=vtm[0:TB, tb, half * 256:(half + 1) * 256], in_=ps[0:TB, 0:256], func=AF.Copy),
                              reads=[r_ps], writes=[r_vtm])
                    proj_tm(W["w_in"], OFF_V + hd * 512 + half * 256, 256, T, TB, NTB, ep_v)
                for tb in range(NTB):
                    ps, r_ps = bank()
                    for a in range(2):
                        mm(ps[0:TB, a * 128:(a + 1) * 128], kh[:, a, tb * TB:(tb + 1) * TB], identb[:], True, True, [r_kh, r_idb], [r_ps])
                    P.add("act", I("activation", out=ktm[0:TB, tb, :], in_=ps[0:TB, 0:256], func=AF.Copy, scale=gT),
                          reads=[r_ps], writes=[r_ktm])
                for bi in range(NTB):
                    isl = slice(bi * TB, (bi + 1) * TB)
                    ps, r_ps = bank()
                    for bj in range(bi + 1):
                        jsl = slice(bj * TB, (bj + 1) * TB)
                        for a in range(2):
                            mm(ps[0:TB, bj * TB:(bj + 1) * TB], kh[:, a, jsl], qh[:, a, isl], (bj == 0 and a == 0), a == 1, [r_kh, r_qh], [r_ps])
                    pb_ = bi % 2
                    if bi > 0:
                        P.add("act", I("activation", out=pT[pb_][0:TB, 0:bi * TB], in_=ps[0:TB, 0:bi * TB], func=AF.Copy),
                              reads=[r_ps], writes=[r_pT[pb_]])
                    P.add("dve", I("tensor_tensor", out=pT[pb_][0:TB, bi * TB:(bi + 1) * TB], in0=ps[0:TB, bi * TB:(bi + 1) * TB],
                                   in1=causal[0:TB, 0:TB], op=ALU.mult), reads=[r_ps, r_cst], writes=[r_pT[pb_]])
                    po_, r_po = bank()
                    for vc in range(4):
                        for bj in range(bi + 1):
                            mm(po_[:, vc * 128:vc * 128 + TB], vtm[0:TB, bj, vc * 128:(vc + 1) * 128], pT[pb_][0:TB, bj * TB:(bj + 1) * TB],
                               (vc == 0 and bj == 0), False, [r_vtm, r_pT[pb_]], [r_po])
                        for a in range(2):
                            mm(po_[:, vc * 128:vc * 128 + TB], Sb[:, a, vc * 128:(vc + 1) * 128], qh[:, a, isl], False, (a == 1), [r_Sb, r_qh], [r_po])
                    for vc in range(4):
                        P.add("act", I("activation", out=oT[:, vc, isl], in_=po_[:, vc * 128:vc * 128 + TB], func=AF.Copy),
                              reads=[r_po], writes=[r_oT[vc]])
                def ep_g(j, ps, r_ps):
                    P.add("act", I("activation", out=sg[:, j, 0:T], in_=ps[:, 0:T], func=AF.Silu), reads=[r_ps], writes=[r_sg[j]])
                for half in range(2):
                    proj_fm(W["w_in"], OFF_G + hd * 512 + half * 256, 256, T, nsrc, r_n, 16,
                            lambda j, ps, r, half=half: ep_g(half * 2 + j, ps, r))
                for a in range(2):
                    pu, r_pu = bank()
                    for tb in range(NTB):
                        mm(pu[:, :], ktm[0:TB, tb, a * 128:(a + 1) * 128], vtm[0:TB, tb, :], tb == 0, tb == NTB - 1, [r_ktm, r_vtm], [r_pu])
                    P.add("dve", I("scalar_tensor_tensor", out=Sf[:, a, :], in0=Sf[:, a, :], scalar=gT, in1=pu[:, :],
                                   op0=ALU.mult, op1=ALU.add), reads=[r_pu, r_Sf], writes=[r_Sf])
                dst = ret_o[kind][hd].rearrange("(a p) v -> p a v", p=128)
                P.add(MQ, I("dma_start", out=dst, in_=Sf[:]), reads=[r_Sf], writes=[r_ret_o[kind][hd]], key="sr")
                psm, r_psm = bank()
                pss2, r_pss2 = bank()
                for vc in range(4):
                    P.add("act", I("activation", out=obf[:, 0:T], in_=oT[:, vc, 0:T], func=AF.Copy), reads=[r_oT[vc]], writes=[r_obf])
                    mm(psm[:, 0:T], onesb[:], obf[:, 0:T], vc == 0, vc == 3, [r_obf, r_idb], [r_psm])
                    b = vc % 2
                    P.add("act", I("activation", out=sqb[b][:, 0:T], in_=oT[:, vc, 0:T], func=AF.Square), reads=[r_oT[vc]], writes=[r_sqb[b]])
                    mm(pss2[:, 0:T], onesb[:], sqb[b][:, 0:T], vc == 0, vc == 3, [r_sqb[b], r_idb], [r_pss2])
                dve = lambda f, rd, wr: P.add("dve", f, reads=rd, writes=wr)
                dve(I("tensor_scalar", mean[:, 0:T], psm[:, 0:T], 1.0 / 512, None, op0=ALU.mult), [r_psm], [r_mean])
                dve(I("tensor_tensor", out=msq[:, 0:T], in0=mean[:, 0:T], in1=mean[:, 0:T], op=ALU.mult), [r_mean], [r_msq])
                dve(I("scalar_tensor_tensor", out=msq[:, 0:T], in0=pss2[:, 0:T], scalar=1.0 / 512, in1=msq[:, 0:T],
                                                                op0=ALU.mult, op1=ALU.subtract), [r_pss2, r_msq], [r_msq])
                P.add("act", I("activation", out=msq[:, 0:T], in_=msq[:, 0:T], func=AF.Sqrt, bias=epsc[:, 0:1], scale=1.0), reads=[r_msq, r_idb], writes=[r_msq])
                dve(I("reciprocal", msq[:, 0:T], msq[:, 0:T]), [r_msq], [r_msq])
                for vc in range(4):
                    j = hd * 4 + vc
                    dve(I("tensor_tensor", out=oT[:, vc, 0:T], in0=oT[:, vc, 0:T], in1=mean[:, 0:T], op=ALU.subtract),
                        [r_oT[vc], r_mean], [r_oT[vc]])
                    dve(I("tensor_tensor", out=oT[:, vc, 0:T], in0=oT[:, vc, 0:T], in1=msq[:, 0:T], op=ALU.mult),
                        [r_oT[vc], r_msq], [r_oT[vc]])
                    dve(I("scalar_tensor_tensor", out=yAB[:, j, 0:T], in0=oT[:, vc, 0:T], scalar=retw[:, j:j + 1], in1=sg[:, vc, 0:T],
                                                                     op0=ALU.mult, op1=ALU.mult),
                        [r_oT[vc], r_sg[vc], r_small], [r_yAB[j]])

            def gate_and_proj(goff, wout, merge):
                for jp in range(0, 16, 2):
                    gl = []
                    proj_fm(W["w_in"], goff + jp * 128, 256, T, nsrc, r_n, 16, lambda j, ps, r: gl.append((ps, r)))
                    for jj, (ps, r_ps) in enumerate(gl):
                        j = jp + jj
                        b = j % 2
                        gcol = (goff - OFF_GA) // 128 + j
                        P.add("act", I("activation", out=gtmp[b][:, 0:T], in_=ps[:, 0:T], func=AF.Sigmoid,
                                                                                bias=bgate[:, gcol:gcol + 1], scale=1.0),
                              reads=[r_ps, r_small], writes=[r_gtmp[b]])
                    proj_fm(wout, jp * 128, 256, T, lambda kc: yAB[:, kc, 0:T], r_yAB, 32, lambda j, ps, r, jp=jp: merge(jp + j, ps, r))

            def merge_a(j, ps, r_ps):
                b = j % 2
                P.add("dve", I("tensor_tensor", out=mbuf[:, j, 0:T], in0=gtmp[b][:, 0:T], in1=ps[:, 0:T], op=ALU.mult),
                      reads=[r_ps, r_gtmp[b]], writes=[r_m[j]])

            gate_and_proj(OFF_GA, W["w_out_ret"], merge_a)
            if stop == "ret":
                continue
            P.fence()

            o[0] = TOFF
            cin = alloc(520, F32); r_cin = Res()
            acc = alloc(512, F32); r_acc = Res()
            xsT = [alloc(512, BF16) for _ in range(2)]; r_xsT = [Res(), Res()]
            xdt = alloc(4 * 512, BF16, "p (b c) -> p b c", b=4); r_xdt = Res()
            xw = alloc(4 * 512, BF16, "p (b c) -> p b c", b=4); r_xw = Res()
            sz = alloc(4 * 512, BF16, "p (b c) -> p b c", b=4); r_sz = Res()
            BT = alloc(512, BF16); CT = alloc(512, BF16); r_BT, r_CT = Res(), Res()
            Btm = alloc(4 * 128, BF16, "p (b c) -> p b c", b=4); r_Btm = Res()
            Hf = alloc(512, F32); r_Hf = Res()
            Hb2 = [alloc(512, BF16) for _ in range(2)]; r_Hb2 = [Res(), Res()]
            hbi = [0]
            Rb = alloc(512, F32); r_Rb = Res()
            Eb = alloc(512, F32); r_Eb = Res()
            MT2 = [alloc(8 * 128, BF16, "p (h i) -> p h i", h=8) for _ in range(2)]; r_MT2 = [Res(), Res()]
            cbm = alloc(64, F32); r_cbm = Res()
            y1 = alloc(512, F32); r_y1 = Res()
            ytm = alloc(4 * 512, F32, "p (b c) -> p b c", b=4); r_ytm = Res()
            dts = alloc(8 * 256, F32, "p (q b c) -> p q b c", q=8, b=4)
            r_dts = [Res() for _ in range(8)]
            DT, DTA, CUM, CL, DEND, ECUM, W2 = range(7)

            ssq = alloc(8, F32); r_ssq = Res()
            junk = y1; r_junk = r_y1
            dve = lambda f, rd, wr: P.add("dve", f, reads=rd, writes=wr)
            actf = lambda f, rd, wr: P.add("act", f, reads=rd, writes=wr)
            selC = sel64 if C == 64 else sel16

            def ep_dt(tb, ps, r_ps):
                dve(I("tensor_tensor", out=dts[0:TB, DT, tb, :], in0=ps[0:TB, 0:64], in1=dtb[0:TB, :], op=ALU.add), [r_ps, r_small], [r_dts[DT]])
            proj_tm(W["w_in"], OFF_DT, 64, T, TB, NTB, ep_dt)
            actf(I("activation", out=dts[0:TB, DT, 0:NTB, :], in_=dts[0:TB, DT, 0:NTB, :], func=AF.Exp), [r_dts[DT]], [r_dts[DT]])
            actf(I("activation", out=dts[0:TB, DT, 0:NTB, :], in_=dts[0:TB, DT, 0:NTB, :], func=AF.Ln, bias=epsc[0:TB, 1:2], scale=1.0), [r_dts[DT]], [r_dts[DT]])
            dve(I("tensor_tensor", out=dts[0:TB, DTA, 0:NTB, :], in0=dts[0:TB, DT, 0:NTB, :],
                                          in1=abc[0:TB, :].unsqueeze(1).to_broadcast([TB, NTB, 64]), op=ALU.mult), [r_dts[DT], r_abc], [r_dts[DTA]])
            for tb in range(NTB):
                pc, r_pc = bank()
                mm(pc[0:TB, 0:64], bdmask[0:TB, 0:TB], dts[0:TB, DTA, tb, :], True, True, [r_cst, r_dts[DTA]], [r_pc])
                mm(pc[0:TB, 64:128], bdones[0:TB, 0:TB], dts[0:TB, DTA, tb, :], True, True, [r_cst, r_dts[DTA]], [r_pc])
                actf(I("activation", out=dts[0:TB, CUM, tb, :], in_=pc[0:TB, 0:64], func=AF.Copy), [r_pc], [r_dts[CUM]])
                actf(I("activation", out=dts[0:TB, ECUM, tb, :], in_=pc[0:TB, 0:64], func=AF.Exp), [r_pc], [r_dts[ECUM]])
                dve(I("tensor_tensor", out=dts[0:TB, DEND, tb, :], in0=pc[0:TB, 64:128], in1=dts[0:TB, CUM, tb, :], op=ALU.subtract),
                    [r_pc, r_dts[CUM]], [r_dts[DEND]])
            actf(I("activation", out=dts[0:TB, DEND, 0:NTB, :], in_=dts[0:TB, DEND, 0:NTB, :], func=AF.Exp), [r_dts[DEND]], [r_dts[DEND]])
            dve(I("tensor_tensor", out=dts[0:TB, W2, 0:NTB, :], in0=dts[0:TB, DEND, 0:NTB, :], in1=dts[0:TB, DT, 0:NTB, :], op=ALU.mult),
                [r_dts[DEND], r_dts[DT]], [r_dts[W2]])

            def conv_chunk(cidx, ps, r_ps, outap, r_out):
                actf(I("activation", out=cin[:, 3:3 + T], in_=ps[:, 0:T], func=AF.Copy), [r_ps], [r_cin])
                dve(I("tensor_copy", cin[:, 0:3], hist[:, cidx, :]), [r_hist[cidx], r_cin], [r_cin])
                dve(I("tensor_scalar", acc[:, 0:T], cin[:, 0:T], convw[:, cidx, 0:1], None, op0=ALU.mult), [r_cin, r_small], [r_acc])
                for i in range(1, 4):
                    dve(I("scalar_tensor_tensor", out=acc[:, 0:T], in0=cin[:, i:i + T], scalar=convw[:, cidx, i:i + 1], in1=acc[:, 0:T],
                                                              op0=ALU.mult, op1=ALU.add), [r_cin, r_acc, r_small], [r_acc])
                dve(I("tensor_copy", hist[:, cidx, :], cin[:, T:T + 3]), [r_cin], [r_hist[cidx]])
                actf(I("activation", out=outap, in_=acc[:, 0:T], func=AF.Silu, bias=convb[:, cidx:cidx + 1], scale=1.0), [r_acc, r_small], [r_out])

            for g in range(8):
                if first and kind == "p":
                    dve(I("memset", Hf[:], 0.0), [], [r_Hf])
                    dve(I("memset", Hb2[hbi[0] % 2][:], 0.0), [], [r_Hb2[hbi[0] % 2]])
                else:
                    src = (sssm_d if first else ssm_o[kind])[:, g * 512:(g + 1) * 512]
                    P.add(MQ, I("dma_start", out=Hf[:], in_=src), reads=[r_ssm_o[kind][g]], writes=[r_Hf], key="ss")
                    actf(I("activation", out=Hb2[hbi[0] % 2][:], in_=Hf[:], func=AF.Copy), [r_Hf], [r_Hb2[hbi[0] % 2]])
                def zgen_f(g=g):
                    for half in range(2):
                        def ep_z(tb, ps, r_ps, half=half):
                            actf(I("activation", out=sz[0:TB, tb, half * 256:(half + 1) * 256], in_=ps[0:TB, 0:256], func=AF.Copy), [r_ps], [r_sz])
                        yield from proj_tm_gen(W["w_in"], OFF_Z + g * 512 + half * 256, 256, T, TB, NTB, ep_z)
                zgen = zgen_f()
                if DBG_NOILV:
                    for _ in zgen:
                        pass
                bl = []
                proj_fm(W["w_in"], OFF_X + 4096 + g * 128, 128, T, nsrc, r_n, 16, lambda j, ps, r: bl.append((ps, r)), colw=128)
                conv_chunk(32 + g, bl[0][0], bl[0][1], BT[:, 0:T], r_BT)
                cl_ = []
                proj_fm(W["w_in"], OFF_X + 5120 + g * 128, 128, T, nsrc, r_n, 16, lambda j, ps, r: cl_.append((ps, r)), colw=128)
                conv_chunk(40 + g, cl_[0][0], cl_[0][1], CT[:, 0:T], r_CT)
                for tb in range(NTB):
                    ps, r_ps = bank()
                    mm(ps[0:TB, 0:128], BT[:, tb * TB:(tb + 1) * TB], identb[:], True, True, [r_BT, r_idb], [r_ps])
                    actf(I("activation", out=Btm[0:TB, tb, :], in_=ps[0:TB, 0:128], func=AF.Copy), [r_ps], [r_Btm])
                xb = [bank() for _ in range(NTB)]
                for xpair in range(2):
                    xl = []
                    proj_fm(W["w_in"], OFF_X + g * 512 + xpair * 256, 256, T, nsrc, r_n, 16, lambda j, ps, r: xl.append((ps, r)))
                    for xi in range(2):
                        xc = xpair * 2 + xi
                        b = xc % 2
                        conv_chunk(g * 4 + xc, xl[xi][0], xl[xi][1], xsT[b][:, 0:T], r_xsT[b])
                        for tb in range(NTB):
                            mm(xb[tb][0][0:TB, xc * 128:(xc + 1) * 128], xsT[b][:, tb * TB:(tb + 1) * TB], identb[:], True, True,
                               [r_xsT[b], r_idb], [xb[tb][1]])
                for tb in range(NTB):
                    xps, r_xps = xb[tb]
                    hs = slice(g * 8, (g + 1) * 8)
                    bc = lambda q: dts[0:TB, q, tb, hs].unsqueeze(2).to_broadcast([TB, 8, 64])
                    x3 = xps[0:TB, :].rearrange("p (h q) -> p h q", h=8)
                    dve(I("tensor_tensor", out=xdt[0:TB, tb, :].rearrange("p (h q) -> p h q", h=8), in0=x3, in1=bc(DT), op=ALU.mult),
                        [r_xps, r_dts[DT]], [r_xdt])
                    dve(I("tensor_tensor", out=xw[0:TB, tb, :].rearrange("p (h q) -> p h q", h=8), in0=x3, in1=bc(W2), op=ALU.mult),
                        [r_xps, r_dts[W2]], [r_xw])
                    dve(I("tensor_tensor", out=ytm[0:TB, tb, :].rearrange("p (h q) -> p h q", h=8), in0=x3,
                                                                in1=dskip[0:TB, hs].unsqueeze(2).to_broadcast([TB, 8, 64]), op=ALU.mult),
                        [r_xps, r_small], [r_ytm])
                if g == 0:
                    dve(I("memset", MT2[0][:], 0.0), [], [r_MT2[0]])
                    dve(I("memset", MT2[1][:], 0.0), [], [r_MT2[1]])
                hs = slice(g * 8, (g + 1) * 8)

                def stage_a(tb):
                    tsl = slice(tb * TB, (tb + 1) * TB)
                    dve(I("tensor_tensor", out=Rb[0:TB, 0:8 * C].rearrange("p (h i) -> p h i", h=8),
                          in0=dts[0:TB, DTA, tb, hs].unsqueeze(2).to_broadcast([TB, 8, C]),
                          in1=triloc[0:TB, 0:C].unsqueeze(1).to_broadcast([TB, 8, C]), op=ALU.mult),
                        [r_dts[DTA], r_cst], [r_Rb])
                    ps, r_ps = bank()
                    mm(ps[0:TB, 0:TB], BT[:, tsl], CT[:, tsl], True, True, [r_BT, r_CT], [r_ps])
                    pseg, r_pseg = bank()
                    mm(pseg[0:TB, 0:8 * C], bdU[0:TB, 0:TB], Rb[0:TB, 0:8 * C], True, True, [r_cst, r_Rb], [r_pseg])
                    for c in range(CPB):
                        p0 = c * C
                        dve(I("tensor_tensor", out=cbm[p0:p0 + C, 0:C], in0=ps[p0:p0 + C, p0:p0 + C], in1=triloc[p0:p0 + C, 0:C], op=ALU.mult),
                            [r_ps, r_cst], [r_cbm])
                    actf(I("activation", out=Eb[0:TB, 0:8 * C], in_=pseg[0:TB, 0:8 * C], func=AF.Exp), [r_pseg], [r_Eb])

                def stage_b(tb):
                    MT = MT2[tb % 2]
                    r_MT = r_MT2[tb % 2]
                    for c in range(CPB):
                        p0 = c * C
                        dve(I("tensor_tensor", out=MT[p0:p0 + C, :, p0:p0 + C],
                              in0=Eb[p0:p0 + C, 0:8 * C].rearrange("p (h i) -> p h i", h=8),
                              in1=cbm[p0:p0 + C, 0:C].unsqueeze(1).to_broadcast([C, 8, C]), op=ALU.mult),
                            [r_Eb, r_cbm], [r_MT])
                    py, r_py = bank()
                    for hh in range(8):
                        mm(py[0:TB, hh * 64:(hh + 1) * 64], MT[0:TB, hh, 0:TB], xdt[0:TB, tb, hh * 64:(hh + 1) * 64], True, True, [r_MT, r_xdt], [r_py])
                    dve(I("tensor_tensor", out=ytm[0:TB, tb, :], in0=ytm[0:TB, tb, :], in1=py[0:TB, :], op=ALU.add),
                        [r_py, r_ytm], [r_ytm])

                def chain_step(tb, c):
                    tsl = slice(tb * TB, (tb + 1) * TB)
                    p0 = c * C
                    csl = slice(p0, p0 + C)
                    ph, r_ph = bank()
                    mm(ph[:, :], Btm[csl, tb, :], xw[csl, tb, :], True, True, [r_Btm, r_xw], [r_ph])
                    pcl, r_pcl = bank()
                    mm(pcl[:, 0:64], selC[csl, :], dts[csl, ECUM, tb, :], True, True, [r_cst, r_dts[ECUM]], [r_pcl])
                    hb_i = hbi[0] % 2
                    pch, r_pch = bank()
                    mm(pch[0:TB, :], CT[:, tsl], Hb2[hb_i][:], True, True, [r_CT, r_Hb2[hb_i]], [r_pch])
                    dve(I("tensor_tensor", out=Hf[:].rearrange("p (h q) -> p h q", h=8), in0=Hf[:].rearrange("p (h q) -> p h q", h=8),
                          in1=pcl[:, hs].unsqueeze(2).to_broadcast([128, 8, 64]), op=ALU.mult), [r_pcl, r_Hf], [r_Hf])
                    dve(I("tensor_tensor", out=Hf[:], in0=Hf[:], in1=ph[:, :], op=ALU.add), [r_ph, r_Hf], [r_Hf])
                    hbi[0] += 1
                    actf(I("activation", out=Hb2[hbi[0] % 2][:], in_=Hf[:], func=AF.Copy), [r_Hf], [r_Hb2[hbi[0] % 2]])
                    dve(I("tensor_tensor", out=y1[csl, :].rearrange("p (h q) -> p h q", h=8),
                          in0=pch[csl, :].rearrange("p (h q) -> p h q", h=8),
                          in1=dts[csl, ECUM, tb, hs].unsqueeze(2).to_broadcast([C, 8, 64]), op=ALU.mult),
                        [r_pch, r_dts[ECUM]], [r_y1])
                    dve(I("tensor_tensor", out=ytm[csl, tb, :], in0=ytm[csl, tb, :], in1=y1[csl, :], op=ALU.add),
                        [r_y1, r_ytm], [r_ytm])
                    next(zgen, None)

                stage_a(0)
                stage_b(0)
                for tb in range(NTB):
                    nxt = tb + 1 < NTB
                    if nxt:
                        stage_a(tb + 1)
                    chain_step(tb, 0)
                    if nxt:
                        stage_b(tb + 1)
                    for c in range(1, CPB):
                        chain_step(tb, c)
                for _ in zgen:
                    pass
                dst = ssm_o[kind][:, g * 512:(g + 1) * 512]
                P.add(MQ, I("dma_start", out=dst, in_=Hf[:]), reads=[r_Hf], writes=[r_ssm_o[kind][g]], key="ss")
                for tb in range(NTB):
                    actf(I("activation", out=sz[0:TB, tb, :], in_=sz[0:TB, tb, :], func=AF.Silu), [r_sz], [r_sz])
                for tb in range(NTB):
                    dve(I("tensor_tensor", out=ytm[0:TB, tb, :], in0=ytm[0:TB, tb, :], in1=sz[0:TB, tb, :], op=ALU.mult),
                        [r_ytm, r_sz], [r_ytm])
                    actf(I("activation", out=junk[0:TB, :], in_=ytm[0:TB, tb, :], func=AF.Square, accum_out=ssq[0:TB, tb:tb + 1]),
                         [r_ytm], [r_junk, r_ssq])
                actf(I("activation", out=ssq[0:TB, 0:NTB], in_=ssq[0:TB, 0:NTB], func=AF.Sqrt, bias=epsc[0:TB, 0:1], scale=1.0 / 512), [r_ssq, r_idb], [r_ssq])
                dve(I("reciprocal", ssq[0:TB, 0:NTB], ssq[0:TB, 0:NTB]), [r_ssq], [r_ssq])
                for tb in range(NTB):
                    dve(I("tensor_scalar", sz[0:TB, tb, :], ytm[0:TB, tb, :], ssq[0:TB, tb:tb + 1], None, op0=ALU.mult),
                        [r_ytm, r_ssq], [r_sz])
                for fc in range(4):
                    ps, r_ps = bank()
                    for tb in range(NTB):
                        mm(ps[:, tb * TB:(tb + 1) * TB], sz[0:TB, tb, fc * 128:(fc + 1) * 128], identb[0:TB, 0:TB], True, True, [r_sz, r_idb], [r_ps])
                    j = g * 4 + fc
                    actf(I("activation", out=yAB[:, j, 0:T], in_=ps[:, 0:T], func=AF.Copy, scale=ssmw[:, j:j + 1]),
                         [r_ps, r_small], [r_yAB[j]])

            def merge_b(j, ps, r_ps):
                b = j % 2
                P.add("dve", I("tensor_tensor", out=gtmp[b][:, 0:T], in0=gtmp[b][:, 0:T], in1=ps[:, 0:T], op=ALU.mult),
                      reads=[r_ps, r_gtmp[b]], writes=[r_gtmp[b]])
                P.add("dve", I("tensor_tensor", out=mbuf[:, j, 0:T], in0=gtmp[b][:, 0:T], in1=mbuf[:, j, 0:T], op=ALU.add),
                      reads=[r_gtmp[b], r_m[j]], writes=[r_m[j]])

            gate_and_proj(OFF_GB, W["w_out_ssm"], merge_b)

            if is_last:
                P.add(MQ, I("dma_start", out=conv_o[kind].rearrange("p (c i) -> p c i", i=3), in_=hist[:]),
                      reads=r_hist, key="st")

            def ep_wo(j, ps, r_ps):
                P.add("dve", I("tensor_tensor", out=h[:, j, 0:T], in0=h[:, j, 0:T], in1=ps[:, 0:T], op=ALU.add),
                      reads=[r_ps, r_h[j]], writes=[r_h[j]])
            proj_fm(W["w_out"], 0, D, T, lambda kc: mbuf[:, kc, 0:T], r_m, 16, ep_wo)
            if stop == "mix":
                continue
            P.fence()
            rmsnorm(T, 2, lambda kc: (n[:, kc, 0:T], r_n[kc]))
            ffn(T, W["ffn2_w_gate"], W["ffn2_w_up"], W["ffn2_w_down"])
            P.fence()
            yo = carve(0, 16 * 512, F32, "p (k t) -> p k t", k=16)
            r_yo = [Res() for _ in range(16)]
            rmsnorm(T, 3, lambda kc: (yo[:, kc, 0:T], r_yo[kc]))
            for kq in range(4):
                P.add(MQ, I("dma_start", out=yT[kq * 512:(kq + 1) * 512, t0:t0 + T].rearrange("(k p) t -> p k t", p=128),
                                                         in_=yo[:, kq * 4:(kq + 1) * 4, 0:T]),
                      reads=r_yo[kq * 4:(kq + 1) * 4], key="y")

        if stop is not None:
            for (t0, T, C, kind) in tiles:
                pass
            P.fence()
            t0, T, C, kind = tiles[-1]
            P.add(MQ, I("dma_start", out=yT[:, t0:t0 + T].rearrange("(k p) t -> p k t", p=128), in_=h[:, :, 0:T]),
                  reads=r_h, key="y")
        P.emit_all(nc)
    return nc


def _consts():
    cst = np.zeros((128, 1088), np.float32)
    j = np.arange(128)[:, None]
    i = np.arange(128)[None, :]
    same = (j // 64) == (i // 64)
    cst[:, 0:128] = (same & (i >= j))
    cst[:, 128:256] = (same & (j > i))
    cst[:, 256:384] = same
    cst[:, 384:448] = ((np.arange(128)[:, None] % 64) <= np.arange(64)[None, :])
    cst[:, 448:576] = np.eye(128)
    cst[:, 576:704] = 1.0
    cst[:, 704:832] = (i >= j)
    cst[63, 832:960] = 1.0
    cst[127, 832:960] = 1.0
    cst[15, 960:1088] = 1.0
    pos = np.concatenate([np.arange(SEQ, dtype=np.float32), np.arange(DEC, dtype=np.float32) + np.float32(PAST)])
    inv = (np.float32(10000.0) ** (-np.arange(128, dtype=np.float32) / np.float32(128))).astype(np.float32)
    ang = (pos[None, :] * inv[:, None]).astype(np.float32)
    rope = np.stack([np.cos(ang), np.sin(ang)], axis=1).astype(np.float32)
    dqk = np.zeros((8, 128, 2, 528), np.float32)
    for hd in range(8):
        for (o_, T_) in ((0, 512), (512, 16)):
            idx = np.arange(T_, dtype=np.float64) + 1.0
            dqk[hd, :, 0, o_:o_ + T_] = np.exp(idx * LG[hd])[None, :]
            dqk[hd, :, 1, o_:o_ + T_] = (np.exp(-idx * LG[hd]) / 16.0)[None, :]
    return cst, rope, dqk


def _fm(v, nch):
    return np.ascontiguousarray(np.asarray(v, np.float32).reshape(nch, 128).T)


TILES = [(SEQ, DEC, DEC, "s")] + [(i * 512, 512, 64, "p") for i in range(8)]
_CACHE = {}


def make_in_maps(inp, cores):
    cst, rope, dqk = _consts()
    small = np.zeros((128, 640), np.float32)
    small[:, 0:16] = _fm(inp["norm_ffn1"], 16)
    small[:, 16:32] = _fm(inp["norm_mix"], 16)
    small[:, 32:48] = _fm(inp["norm_ffn2"], 16)
    small[:, 48:64] = _fm(inp["norm_final"], 16)
    small[:, 64:96] = _fm(inp["ret_norm_w"], 32)
    small[:, 96:128] = _fm(inp["ssm_norm_w"], 32)
    small[:, 128:160] = _fm(inp["b_gate"], 32)
    cw = np.asarray(inp["conv_w"], np.float32)
    small[:, 160:352] = cw.reshape(4, 48, 128).transpose(2, 1, 0).reshape(128, 192)
    small[:, 352:400] = _fm(inp["conv_b"], 48)
    small[:, 400:464] = np.asarray(inp["dt_bias"], np.float32)[None, :]
    small[:, 464:528] = np.asarray(inp["a_log"], np.float32)[None, :]
    small[:, 528:592] = np.asarray(inp["d_skip"], np.float32)[None, :]
    wnames = ["ffn1_w_gate", "ffn1_w_up", "ffn1_w_down", "w_in", "w_out_ret", "w_out_ssm", "w_out",
              "ffn2_w_gate", "ffn2_w_up", "ffn2_w_down"]
    shared = {k: np.ascontiguousarray(np.asarray(inp[k], np.float32)) for k in wnames}
    shared.update(small=small, cst=cst, rope=rope, dqk=dqk)
    maps = []
    for b in cores:
        xa = np.concatenate([np.asarray(inp["x_prompt"][b], np.float32), np.asarray(inp["x_sample"][b], np.float32)], axis=0)
        m = dict(shared)
        m["xT"] = np.ascontiguousarray(xa.T)
        m["sret"] = np.ascontiguousarray(np.asarray(inp["state_ret"][b], np.float32))
        m["sssmT"] = np.ascontiguousarray(np.asarray(inp["state_ssm"][b], np.float32).reshape(4096, 128).T)
        sc = np.asarray(inp["state_conv"][b], np.float32)
        m["sconv"] = np.ascontiguousarray(sc.reshape(3, 48, 128).transpose(2, 1, 0).reshape(128, 144))
        maps.append(m)
    return maps


def assemble(results, nb):
    y = np.stack([r["yT"].T for r in results])
    y_prompt = np.ascontiguousarray(y[:, :SEQ])
    y_sample = np.ascontiguousarray(y[:, SEQ:])

    def ssm(k):
        return np.stack([r[k].T.reshape(64, 64, 128) for r in results])

    def conv(k):
        return np.stack([r[k].reshape(128, 48, 3).transpose(2, 1, 0).reshape(3, 6144) for r in results])

    return (y_prompt, y_sample,
            np.stack([r["ret_p"] for r in results]), ssm("ssm_pT"), conv("conv_p"),
            np.stack([r["ret_s"] for r in results]), ssm("ssm_sT"), conv("conv_s"))


def kernel(**inputs):
    if "nc" not in _CACHE:
        _CACHE["nc"] = build(TILES)
    nc = _CACHE["nc"]
    maps = make_in_maps(inputs, list(range(8)))
    res = run_bass_kernel_spmd(nc, maps, core_ids=list(range(8)))
    return assemble(res.results, 8)
```

```python
import math
from contextlib import ExitStack
import numpy as np
import concourse.bass as bass
import concourse.mybir as mybir
from concourse.bass_utils import run_bass_kernel_spmd

F32 = mybir.dt.float32
BF16 = mybir.dt.bfloat16
AF = mybir.ActivationFunctionType
ALU = mybir.AluOpType

D = 2048
DFF = 5632
NKC = 16
SEQ = 4096
DEC = 16
PAST = 1024
NTOK = SEQ + DEC
EPS = 1e-6
OFF_Q, OFF_K, OFF_V, OFF_G, OFF_Z, OFF_X, OFF_DT, OFF_GA, OFF_GB = 0, 2048, 4096, 8192, 12288, 16384, 22528, 22592, 24640
LG = [math.log1p(-2.0 ** (-5.0 - h)) for h in range(8)]
DBG_NOILV = False
DBG_ONEHB = False
DBG_OLDECL = True
WQ = "sp"
MQ = "sp"
WORDER = ["ffn1_w_gate", "ffn1_w_up", "ffn1_w_down", "w_in", "w_out_ret", "w_out_ssm", "w_out",
          "ffn2_w_gate", "ffn2_w_up", "ffn2_w_down"]


class Res:
    __slots__ = ("name", "last_w", "readers")

    def __init__(self, name=""):
        self.name = name
        self.last_w = None
        self.readers = {}


class Op:
    __slots__ = ("eng", "emit", "deps", "signal", "count", "key", "pos", "is_dma")

    def __init__(self, eng, emit, key):
        self.eng = eng
        self.emit = emit
        self.deps = []
        self.signal = False
        self.count = 0
        self.key = key
        self.is_dma = key is not None
        self.pos = 0


class Prog:
    ENGS = ("pe", "act", "dve", "pool", "sp")

    def __init__(self):
        self.ops = []
        self.by_eng = {e: [] for e in self.ENGS}
        self.seen = {e: {} for e in self.ENGS}
        self.keycnt = {}
        self.cpos = {e: 0 for e in self.ENGS}
        self.last_compute = {}
        self.dma_last = {}

    def _k(self, p):
        return ("k", p.key) if p.is_dma else ("e", p.eng)

    def add(self, eng, emit, reads=(), writes=(), key=None, extra_deps=()):
        op = Op(eng, emit, key)
        idx = len(self.ops)
        self.ops.append(op)
        self.by_eng[eng].append(op)
        if key is not None:
            self.keycnt[key] = self.keycnt.get(key, 0) + 1
            op.pos = self.keycnt[key]
            self.dma_last[key] = idx
        elif emit is not None:
            self.cpos[eng] += 1
            op.pos = self.cpos[eng]
            self.last_compute[eng] = idx
        best = {}

        def cand(d):
            p = self.ops[d]
            k = self._k(p)
            if k not in best or self.ops[best[k]].pos < p.pos:
                best[k] = d

        for r in reads:
            if r.last_w is not None:
                cand(r.last_w)
        for r in writes:
            if r.last_w is not None:
                cand(r.last_w)
            for d in r.readers.values():
                cand(d)
        for d in extra_deps:
            cand(d)
        seen = self.seen[eng]
        for k, d in best.items():
            p = self.ops[d]
            if d == idx:
                continue
            if (not p.is_dma) and p.eng == "pe" and eng == "pe" and not op.is_dma and emit is not None:
                continue
            if seen.get(k, 0) >= p.pos:
                continue
            seen[k] = p.pos
            op.deps.append(d)
            p.signal = True
        if emit is not None:
            k = self._k(op)
            for r in reads:
                r.readers[k] = idx
            for r in writes:
                r.last_w = idx
                r.readers = {}
        return op

    def fence(self, engs=("pe", "act", "dve", "sp")):
        deps = [self.last_compute[e] for e in engs if e in self.last_compute]
        deps += [d for k, d in self.dma_last.items() if not (str(k).startswith("w") or str(k).startswith("cv"))]
        for e in engs:
            self.add(e, None, extra_deps=deps)

    def emit_all(self, nc, final_wait_eng="sp"):
        with ExitStack() as st:
            esem = {e: st.enter_context(nc.semaphore("s_" + e)) for e in self.ENGS}
            ksem = {k: st.enter_context(nc.semaphore("k_" + str(k))) for k in self.keycnt}
            for e in self.ENGS:
                c = 0
                for op in self.by_eng[e]:
                    if op.is_dma:
                        op.count = 16 * op.pos
                    elif op.signal:
                        c += 1
                        op.count = c
            block = st.enter_context(nc.Block())

            def run(e, eng):
                waited = {}
                for op in self.by_eng[e]:
                    for d in op.deps:
                        p = self.ops[d]
                        s = ksem[p.key] if p.is_dma else esem[p.eng]
                        wk = self._k(p)
                        if waited.get(wk, 0) >= p.count:
                            continue
                        waited[wk] = p.count
                        eng.wait_ge(s, p.count)
                    if op.emit is None:
                        continue
                    ins = op.emit(eng)
                    if op.is_dma:
                        ins.then_inc(ksem[op.key], 16)
                    elif op.signal:
                        ins.then_inc(esem[e], 1)
                if e == final_wait_eng:
                    for k, n in self.keycnt.items():
                        eng.wait_ge(ksem[k], 16 * n)

            block.tensor(lambda eng: run("pe", eng))
            block.scalar(lambda eng: run("act", eng))
            block.vector(lambda eng: run("dve", eng))
            block.gpsimd(lambda eng: run("pool", eng))
            block.sync(lambda eng: run("sp", eng))


def I(method, *a, **kw):
    return lambda e: getattr(e, method)(*a, **kw)


def build(tiles, stop=None):
    nc = bass.Bass("TRN2", target_bir_lowering=False)
    P = Prog()

    def din(name, shape, dt=F32):
        return nc.dram_tensor(name, list(shape), dt, kind="ExternalInput").ap()

    def dout(name, shape, dt=F32):
        return nc.dram_tensor(name, list(shape), dt, kind="ExternalOutput").ap()

    xT = din("xT", [D, NTOK])
    W = {}
    for nm, shp in (("ffn1_w_gate", [D, DFF]), ("ffn1_w_up", [D, DFF]), ("ffn1_w_down", [DFF, D]),
                    ("w_in", [D, 26688]), ("w_out_ret", [4096, D]), ("w_out_ssm", [4096, D]), ("w_out", [D, D]),
                    ("ffn2_w_gate", [D, DFF]), ("ffn2_w_up", [D, DFF]), ("ffn2_w_down", [DFF, D])):
        W[nm] = {"f32": din(nm, shp), "shape": shp, "res": [],
                 "bf": nc.dram_tensor(nm + "_bf", list(shp), BF16, kind="Internal").ap()}
    small_d = din("small", [128, 640])
    cst_d = din("cst", [128, 1088])
    rope_d = din("rope", [128, 2, NTOK])
    dqk_d = din("dqk", [8, 128, 2, 528])
    sret_d = din("sret", [8, 256, 512])
    sssm_d = din("sssmT", [128, 4096])
    sconv_d = din("sconv", [128, 144])
    yT = dout("yT", [D, NTOK])
    ret_o = {"p": dout("ret_p", [8, 256, 512]), "s": dout("ret_s", [8, 256, 512])}
    ssm_o = {"p": dout("ssm_pT", [128, 4096]), "s": dout("ssm_sT", [128, 4096])}
    conv_o = {"p": dout("conv_p", [128, 144]), "s": dout("conv_s", [128, 144])}
    r_ret_o = {k: [Res() for _ in range(8)] for k in "ps"}
    r_ssm_o = {k: [Res() for _ in range(8)] for k in "ps"}

    with ExitStack() as st:
        def sb(name, shape, dt):
            return st.enter_context(nc.sbuf_tensor("sb_" + name, list(shape), dt))

        h = sb("h", [128, 16, 512], F32)
        r_h = [Res("h%d" % i) for i in range(16)]
        n = sb("n", [128, 16, 512], BF16)
        r_n = [Res("n%d" % i) for i in range(16)]
        NS = 4
        wsl = [sb("ws%d" % i, [128, 4096], BF16) for i in range(NS)]
        r_ws = [Res("ws%d" % i) for i in range(NS)]
        small = sb("small", [128, 640], F32)
        r_small = Res("small")
        cst = sb("cst", [128, 1088], F32)
        r_cst = Res("cst")
        identb = sb("identb", [128, 128], BF16)
        onesb = sb("onesb", [128, 128], BF16)
        epsc = sb("epsc", [128, 2], F32)
        rope = sb("rope", [128, 2, 512], F32)
        r_rope = Res("rope")
        dqk2 = [sb("dqk%d" % i, [128, 2, 512], F32) for i in range(2)]
        r_dqk2 = [Res(), Res()]
        hist = sb("hist", [128, 48, 3], F32)
        r_hist = [Res() for _ in range(48)]
        abc = sb("abc", [128, 64], F32)
        sqb = [sb("sqb%d" % i, [128, 512], BF16) for i in range(2)]
        r_sqb = [Res(), Res()]
        rs = sb("rs", [128, 512], F32)
        r_rs = Res("rs")
        AW = 26000
        ar = sb("arena", [128, AW], F32)
        pb = [st.enter_context(nc.psum_tensor("pb%d" % i, [128, 512], F32)) for i in range(8)]
        r_pb = [Res("pb%d" % i) for i in range(8)]
        bank_i = [0]

        def bank():
            i = bank_i[0] % 8
            bank_i[0] += 1
            return pb[i], r_pb[i]

        def carve(off, nelem, dt, pat=None, **kw):
            nb = nelem * (4 if dt == F32 else 2)
            assert off % 4 == 0 and nb % 4 == 0 and off + nb <= AW * 4, (off, nb)
            v = ar[:, off // 4:(off + nb) // 4]
            if dt != F32:
                v = v.bitcast(dt)
            if pat:
                v = v.rearrange(pat, **kw)
            return v

        normw = small[:, 0:64].rearrange("p (a k) -> p a k", a=4)
        retw = small[:, 64:96]
        ssmw = small[:, 96:128]
        bgate = small[:, 128:160]
        convw = small[:, 160:352].rearrange("p (c i) -> p c i", i=4)
        convb = small[:, 352:400]
        dtb = small[:, 400:464]
        alog = small[:, 464:528]
        dskip = small[:, 528:592]
        bdmask = cst[:, 0:128]
        bdU = cst[:, 128:256]
        bdones = cst[:, 256:384]
        triloc = cst[:, 384:448]
        identf = cst[:, 448:576]
        onesf = cst[:, 576:704]
        causal = cst[:, 704:832]
        sel64 = cst[:, 832:960]
        sel16 = cst[:, 960:1088]

        P.add(MQ, I("dma_start", out=small[:], in_=small_d), writes=[r_small], key="c0")
        P.add(MQ, I("dma_start", out=cst[:], in_=cst_d), writes=[r_cst], key="c1")
        r_idb = Res("identb")
        P.add("act", I("activation", out=identb[:], in_=identf, func=AF.Copy), reads=[r_cst], writes=[r_idb])
        P.add("dve", I("tensor_copy", onesb[:], onesf), reads=[r_cst], writes=[r_idb])
        P.add("dve", I("memset", epsc[:, 0:1], EPS), writes=[r_idb])
        P.add("dve", I("memset", epsc[:, 1:2], 1.0), writes=[r_idb])
        r_abc = Res("abc")
        P.add("act", I("activation", out=abc[:], in_=alog, func=AF.Exp), reads=[r_small], writes=[r_abc])
        P.add("dve", I("tensor_scalar", abc[:], abc[:], -1.0, None, op0=ALU.mult), reads=[r_abc], writes=[r_abc])

        ws_i = [0]

        def wload(wsrc, k0c, nkc, c0, ncols):
            s = ws_i[0] % NS
            ws_i[0] += 1
            assert nkc * ncols <= 4096
            view = wsl[s][:, 0:nkc * ncols].rearrange("p (k c) -> p k c", k=nkc)
            src = wsrc["bf"][k0c * 128:(k0c + nkc) * 128, c0:c0 + ncols].rearrange("(k p) c -> p k c", p=128)
            P.add(WQ, I("dma_start", out=view, in_=src), reads=wsrc["res"], writes=[r_ws[s]], key="w%d" % s)
            return view, r_ws[s]

        def mm(out, lhsT, rhs, start, stop, rd, wr):
            P.add("pe", I("matmul", out, lhsT, rhs, start=start, stop=stop, skip_group_check=True), reads=rd, writes=wr)

        def proj_fm(wap, c0, ncols_total, T, src, r_src, nk, epilogue, colw=256):
            kpl = 4096 // colw
            nkh = (nk + kpl - 1) // kpl
            for cb in range(0, ncols_total, colw):
                cw = min(colw, ncols_total - cb)
                slabs = []
                for kh in range(nkh):
                    k0 = kh * kpl
                    kn = min(kpl, nk - k0)
                    slabs.append((k0, kn) + wload(wap, k0, kn, c0 + cb, cw))
                for jj in range(0, cw, 128):
                    jw = min(128, cw - jj)
                    ps, r_ps = bank()
                    for (k0, kn, view, r_w) in slabs:
                        for kk in range(kn):
                            kc = k0 + kk
                            mm(ps[0:jw, 0:T], view[:, kk, jj:jj + jw], src(kc), kc == 0, kc == nk - 1,
                               [r_w, r_src[kc]], [r_ps])
                    epilogue((cb + jj) // 128, ps, r_ps)

        def proj_tm(wap, c0, ncols, T, TB, NTB, epilogue):
            view, r_w = wload(wap, 0, 16, c0, ncols)
            for tb in range(NTB):
                ps, r_ps = bank()
                for kc in range(16):
                    mm(ps[0:TB, 0:ncols], n[:, kc, tb * TB:(tb + 1) * TB], view[:, kc, :], kc == 0, kc == 15,
                       [r_w, r_n[kc]], [r_ps])
                epilogue(tb, ps, r_ps)

        def proj_tm_gen(wap, c0, ncols, T, TB, NTB, epilogue):
            view, r_w = wload(wap, 0, 16, c0, ncols)
            for tb in range(NTB):
                ps, r_ps = bank()
                for kc in range(16):
                    mm(ps[0:TB, 0:ncols], n[:, kc, tb * TB:(tb + 1) * TB], view[:, kc, :], kc == 0, kc == 15,
                       [r_w, r_n[kc]], [r_ps])
                epilogue(tb, ps, r_ps)
                yield

        def rmsnorm(T, wi, outf):
            ps, r_ps = bank()
            for kc in range(16):
                b = kc % 2
                P.add("act", I("activation", out=sqb[b][:, 0:T], in_=h[:, kc, 0:T], func=AF.Square),
                      reads=[r_h[kc]], writes=[r_sqb[b]])
                mm(ps[:, 0:T], onesb[:], sqb[b][:, 0:T], kc == 0, kc == 15, [r_sqb[b], r_idb], [r_ps])
            P.add("act", I("activation", out=rs[:, 0:T], in_=ps[:, 0:T], func=AF.Sqrt, bias=epsc[:, 0:1], scale=1.0 / D),
                  reads=[r_ps, r_idb], writes=[r_rs])
            P.add("dve", I("reciprocal", rs[:, 0:T], rs[:, 0:T]), reads=[r_rs], writes=[r_rs])
            for kc in range(16):
                o, r_o = outf(kc)
                P.add("dve", I("scalar_tensor_tensor", out=o, in0=h[:, kc, 0:T], scalar=normw[:, wi, kc:kc + 1],
                                                                          in1=rs[:, 0:T], op0=ALU.mult, op1=ALU.mult),
                      reads=[r_h[kc], r_rs, r_small], writes=[r_o])

        def ffn(T, wg, wu, wd):
            act = carve(0, 44 * 512, BF16, "p (j t) -> p j t", j=44)
            r_act = [Res() for _ in range(44)]
            sgt = [carve(45056 + 2048 * i, 512, F32) for i in range(2)]
            r_sgt = [Res(), Res()]
            nsrc = lambda kc: n[:, kc, 0:T]
            pend = {}

            def ep_gate(j, ps, r_ps):
                b = j % 2
                P.add("act", I("activation", out=sgt[b][:, 0:T], in_=ps[:, 0:T], func=AF.Silu),
                      reads=[r_ps], writes=[r_sgt[b]])

            def ep_up(j, ps, r_ps):
                b = j % 2
                P.add("dve", I("tensor_tensor", out=act[:, j, 0:T], in0=sgt[b][:, 0:T], in1=ps[:, 0:T], op=ALU.mult),
                      reads=[r_ps, r_sgt[b]], writes=[r_act[j]])

            for cb in range(0, DFF, 256):
                proj_fm(wg, cb, 256, T, nsrc, r_n, 16, lambda j, ps, r, cb=cb: ep_gate(cb // 128 + j, ps, r))
                proj_fm(wu, cb, 256, T, nsrc, r_n, 16, lambda j, ps, r, cb=cb: ep_up(cb // 128 + j, ps, r))

            def ep_down(j, ps, r_ps):
                P.add("dve", I("scalar_tensor_tensor", out=h[:, j, 0:T], in0=ps[:, 0:T], scalar=0.5, in1=h[:, j, 0:T],
                                                              op0=ALU.mult, op1=ALU.add),
                      reads=[r_ps, r_h[j]], writes=[r_h[j]])

            proj_fm(wd, 0, D, T, lambda kc: act[:, kc, 0:T], r_act, 44, ep_down)

        converted = [False]
        for (t0, T, C, kind) in tiles:
            TB = min(128, T)
            NTB = T // TB
            CPB = TB // C
            first = (t0 == 0) or kind == "s"
            is_last = (kind == "s") or (t0 + T == SEQ)
            gC = [math.exp(C * LG[hd]) for hd in range(8)]
            P.fence()
            for kq in range(4):
                P.add(MQ, I("dma_start", out=h[:, kq * 4:(kq + 1) * 4, 0:T],
                                                         in_=xT[kq * 512:(kq + 1) * 512, t0:t0 + T].rearrange("(k p) t -> p k t", p=128)),
                      writes=r_h[kq * 4:(kq + 1) * 4], key="x%d" % kq)
            P.add(MQ, I("dma_start", out=rope[:, :, 0:T], in_=rope_d[:, :, t0:t0 + T]), writes=[r_rope], key="c2")
            dqo = 0 if kind == "p" else 512
            if first:
                if kind == "p":
                    P.add("dve", I("memset", hist[:], 0.0), writes=r_hist)
                else:
                    P.add(MQ, I("dma_start", out=hist[:], in_=sconv_d.rearrange("p (c i) -> p c i", i=3)),
                          writes=r_hist, key="c4")
            if not converted[0]:
                converted[0] = True
                prev_cv = []
                for nm in WORDER:
                    R_, C_ = W[nm]["shape"]
                    rp = max(16, ((4 << 20) // (C_ * 4)) // 16 * 16)
                    for r0 in range(0, R_, rp):
                        r1 = min(R_, r0 + rp)
                        rr = Res()
                        W[nm]["res"].append(rr)
                        P.add("pool", I("dma_start", out=W[nm]["bf"][r0:r1, :], in_=W[nm]["f32"][r0:r1, :]), reads=prev_cv[-2:-1], writes=[rr],
                              key="cv%d" % (len(prev_cv) % 2))
                        prev_cv.append(rr)
            rmsnorm(T, 0, lambda kc: (n[:, kc, 0:T], r_n[kc]))
            ffn(T, W["ffn1_w_gate"], W["ffn1_w_up"], W["ffn1_w_down"])
            if stop == "ffn1":
                continue
            P.fence()
            rmsnorm(T, 1, lambda kc: (n[:, kc, 0:T], r_n[kc]))
            nsrc = lambda kc: n[:, kc, 0:T]
            yAB = carve(0, 32 * 512, BF16, "p (j t) -> p j t", j=32)
            r_yAB = [Res() for _ in range(32)]
            mbuf = carve(32768, 16 * 512, BF16, "p (j t) -> p j t", j=16)
            r_m = [Res() for _ in range(16)]
            TOFF = 49152
            cosT = rope[:, 0, 0:T]
            sinT = rope[:, 1, 0:T]

            o = [TOFF]

            def alloc(nelem, dt, pat=None, **kw):
                nb = nelem * (4 if dt == F32 else 2)
                nb = (nb + 31) // 32 * 32
                v = carve(o[0], nb // (4 if dt == F32 else 2), dt)
                v = v[:, 0:nelem]
                if pat:
                    v = v.rearrange(pat, **kw)
                o[0] += nb
                return v

            qh = alloc(2 * 512, BF16, "p (a t) -> p a t", a=2); r_qh = Res()
            kh = alloc(2 * 512, BF16, "p (a t) -> p a t", a=2); r_kh = Res()
            t1 = alloc(512, F32); t2 = alloc(512, F32); t3 = alloc(512, F32)
            r_t1, r_t2, r_t3 = Res(), Res(), Res()
            vtm = alloc(4 * 512, BF16, "p (b c) -> p b c", b=4); r_vtm = Res()
            ktm = alloc(4 * 256, BF16, "p (b c) -> p b c", b=4); r_ktm = Res()
            sg = alloc(4 * 512, F32, "p (a t) -> p a t", a=4); r_sg = [Res() for _ in range(4)]
            oT = alloc(4 * 512, F32, "p (a t) -> p a t", a=4); r_oT = [Res() for _ in range(4)]
            Sf = alloc(2 * 512, F32, "p (a v) -> p a v", a=2); r_Sf = Res()
            Sb = alloc(2 * 512, BF16, "p (a v) -> p a v", a=2); r_Sb = Res()
            pT = [alloc(512, BF16) for _ in range(2)]; r_pT = [Res(), Res()]
            obf = alloc(512, BF16); r_obf = Res()
            mean = alloc(512, F32); msq = alloc(512, F32); r_mean, r_msq = Res(), Res()
            gtmp = [alloc(512, F32) for _ in range(2)]; r_gtmp = [Res(), Res()]

            for hd in range(8):
                dqkb = dqk2[hd % 2]
                r_dq = r_dqk2[hd % 2]
                P.add(MQ, I("dma_start", out=dqkb[:, :, 0:T], in_=dqk_d[hd, :, :, dqo:dqo + T]), writes=[r_dq], key="dq%d" % (hd % 2))
                dq = dqkb[:, 0, 0:T]
                dk = dqkb[:, 1, 0:T]
                gT = math.exp(T * LG[hd])
                if first and kind == "p":
                    P.add("dve", I("memset", Sf[:], 0.0), writes=[r_Sf])
                    P.add("dve", I("memset", Sb[:], 0.0), writes=[r_Sb])
                else:
                    src = (sret_d if first else ret_o[kind])[hd].rearrange("(a p) v -> p a v", p=128)
                    P.add(MQ, I("dma_start", out=Sf[:], in_=src), reads=[r_ret_o[kind][hd]], writes=[r_Sf], key="sr")
                    P.add("act", I("activation", out=Sb[:], in_=Sf[:], func=AF.Copy), reads=[r_Sf], writes=[r_Sb])
                for (off, dst, r_dst, dtab) in ((OFF_Q, qh, r_qh, dq), (OFF_K, kh, r_kh, dk)):
                    pss = []
                    proj_fm(W["w_in"], off + hd * 256, 256, T, nsrc, r_n, 16, lambda j, ps, r: pss.append((ps, r)))
                    (p1, r1), (p2, r2) = pss
                    dve = lambda f, rd, wr: P.add("dve", f, reads=rd, writes=wr)
                    dve(I("tensor_tensor", out=t1[:, 0:T], in0=p1[:, 0:T], in1=cosT, op=ALU.mult), [r1, r_rope], [r_t1])
                    dve(I("tensor_tensor", out=t2[:, 0:T], in0=p2[:, 0:T], in1=sinT, op=ALU.mult), [r2, r_rope], [r_t2])
                    dve(I("tensor_tensor", out=t3[:, 0:T], in0=t1[:, 0:T], in1=t2[:, 0:T], op=ALU.subtract), [r_t1, r_t2], [r_t3])
                    dve(I("tensor_tensor", out=dst[:, 0, 0:T], in0=t3[:, 0:T], in1=dtab, op=ALU.mult), [r_t3, r_dq], [r_dst])
                    dve(I("tensor_tensor", out=t1[:, 0:T], in0=p1[:, 0:T], in1=sinT, op=ALU.mult), [r1, r_rope], [r_t1])
                    dve(I("tensor_tensor", out=t2[:, 0:T], in0=p2[:, 0:T], in1=cosT, op=ALU.mult), [r2, r_rope], [r_t2])
                    dve(I("tensor_tensor", out=t3[:, 0:T], in0=t1[:, 0:T], in1=t2[:, 0:T], op=ALU.add), [r_t1, r_t2], [r_t3])
                    dve(I("tensor_tensor", out=dst[:, 1, 0:T], in0=t3[:, 0:T], in1=dtab, op=ALU.mult), [r_t3, r_dq], [r_dst])
                for half in range(2):
                    def ep_v(tb, ps, r_ps, half=half):
                        P.add("act", I("activation", out=vtm[0:TB, tb, half * 256:(half + 1) * 256], in_=ps[0:TB, 0:256], func=AF.Copy),
                              reads=[r_ps], writes=[r_vtm])
                    proj_tm(W["w_in"], OFF_V + hd * 512 + half * 256, 256, T, TB, NTB, ep_v)
                for tb in range(NTB):
                    ps, r_ps = bank()
                    for a in range(2):
                        mm(ps[0:TB, a * 128:(a + 1) * 128], kh[:, a, tb * TB:(tb + 1) * TB], identb[:], True, True, [r_kh, r_idb], [r_ps])
                    P.add("act", I("activation", out=ktm[0:TB, tb, :], in_=ps[0:TB, 0:256], func=AF.Copy, scale=gT),
                          reads=[r_ps], writes=[r_ktm])
                for bi in range(NTB):
                    isl = slice(bi * TB, (bi + 1) * TB)
                    ps, r_ps = bank()
                    for bj in range(bi + 1):
                        jsl = slice(bj * TB, (bj + 1) * TB)
                        for a in range(2):
                            mm(ps[0:TB, bj * TB:(bj + 1) * TB], kh[:, a, jsl], qh[:, a, isl], (bj == 0 and a == 0), a == 1, [r_kh, r_qh], [r_ps])
                    pb_ = bi % 2
                    if bi > 0:
                        P.add("act", I("activation", out=pT[pb_][0:TB, 0:bi * TB], in_=ps[0:TB, 0:bi * TB], func=AF.Copy),
                              reads=[r_ps], writes=[r_pT[pb_]])
                    P.add("dve", I("tensor_tensor", out=pT[pb_][0:TB, bi * TB:(bi + 1) * TB], in0=ps[0:TB, bi * TB:(bi + 1) * TB],
                                   in1=causal[0:TB, 0:TB], op=ALU.mult), reads=[r_ps, r_cst], writes=[r_pT[pb_]])
                    po_, r_po = bank()
                    for vc in range(4):
                        for bj in range(bi + 1):
                            mm(po_[:, vc * 128:vc * 128 + TB], vtm[0:TB, bj, vc * 128:(vc + 1) * 128], pT[pb_][0:TB, bj * TB:(bj + 1) * TB],
                               (vc == 0 and bj == 0), False, [r_vtm, r_pT[pb_]], [r_po])
                        for a in range(2):
                            mm(po_[:, vc * 128:vc * 128 + TB], Sb[:, a, vc * 128:(vc + 1) * 128], qh[:, a, isl], False, (a == 1), [r_Sb, r_qh], [r_po])
                    for vc in range(4):
                        P.add("act", I("activation", out=oT[:, vc, isl], in_=po_[:, vc * 128:vc * 128 + TB], func=AF.Copy),
                              reads=[r_po], writes=[r_oT[vc]])
                def ep_g(j, ps, r_ps):
                    P.add("act", I("activation", out=sg[:, j, 0:T], in_=ps[:, 0:T], func=AF.Silu), reads=[r_ps], writes=[r_sg[j]])
                for half in range(2):
                    proj_fm(W["w_in"], OFF_G + hd * 512 + half * 256, 256, T, nsrc, r_n, 16,
                            lambda j, ps, r, half=half: ep_g(half * 2 + j, ps, r))
                for a in range(2):
                    pu, r_pu = bank()
                    for tb in range(NTB):
                        mm(pu[:, :], ktm[0:TB, tb, a * 128:(a + 1) * 128], vtm[0:TB, tb, :], tb == 0, tb == NTB - 1, [r_ktm, r_vtm], [r_pu])
                    P.add("dve", I("scalar_tensor_tensor", out=Sf[:, a, :], in0=Sf[:, a, :], scalar=gT, in1=pu[:, :],
                                   op0=ALU.mult, op1=ALU.add), reads=[r_pu, r_Sf], writes=[r_Sf])
                dst = ret_o[kind][hd].rearrange("(a p) v -> p a v", p=128)
                P.add(MQ, I("dma_start", out=dst, in_=Sf[:]), reads=[r_Sf], writes=[r_ret_o[kind][hd]], key="sr")
                psm, r_psm = bank()
                pss2, r_pss2 = bank()
                for vc in range(4):
                    P.add("act", I("activation", out=obf[:, 0:T], in_=oT[:, vc, 0:T], func=AF.Copy), reads=[r_oT[vc]], writes=[r_obf])
                    mm(psm[:, 0:T], onesb[:], obf[:, 0:T], vc == 0, vc == 3, [r_obf, r_idb], [r_psm])
                    b = vc % 2
                    P.add("act", I("activation", out=sqb[b][:, 0:T], in_=oT[:, vc, 0:T], func=AF.Square), reads=[r_oT[vc]], writes=[r_sqb[b]])
                    mm(pss2[:, 0:T], onesb[:], sqb[b][:, 0:T], vc == 0, vc == 3, [r_sqb[b], r_idb], [r_pss2])
                dve = lambda f, rd, wr: P.add("dve", f, reads=rd, writes=wr)
                dve(I("tensor_scalar", mean[:, 0:T], psm[:, 0:T], 1.0 / 512, None, op0=ALU.mult), [r_psm], [r_mean])
                dve(I("tensor_tensor", out=msq[:, 0:T], in0=mean[:, 0:T], in1=mean[:, 0:T], op=ALU.mult), [r_mean], [r_msq])
                dve(I("scalar_tensor_tensor", out=msq[:, 0:T], in0=pss2[:, 0:T], scalar=1.0 / 512, in1=msq[:, 0:T],
                                                                op0=ALU.mult, op1=ALU.subtract), [r_pss2, r_msq], [r_msq])
                P.add("act", I("activation", out=msq[:, 0:T], in_=msq[:, 0:T], func=AF.Sqrt, bias=epsc[:, 0:1], scale=1.0), reads=[r_msq, r_idb], writes=[r_msq])
                dve(I("reciprocal", msq[:, 0:T], msq[:, 0:T]), [r_msq], [r_msq])
                for vc in range(4):
                    j = hd * 4 + vc
                    dve(I("tensor_tensor", out=oT[:, vc, 0:T], in0=oT[:, vc, 0:T], in1=mean[:, 0:T], op=ALU.subtract),
                        [r_oT[vc], r_mean], [r_oT[vc]])
                    dve(I("tensor_tensor", out=oT[:, vc, 0:T], in0=oT[:, vc, 0:T], in1=msq[:, 0:T], op=ALU.mult),
                        [r_oT[vc], r_msq], [r_oT[vc]])
                    dve(I("scalar_tensor_tensor", out=yAB[:, j, 0:T], in0=oT[:, vc, 0:T], scalar=retw[:, j:j + 1], in1=sg[:, vc, 0:T],
                                                                     op0=ALU.mult, op1=ALU.mult),
                        [r_oT[vc], r_sg[vc], r_small], [r_yAB[j]])

            def gate_and_proj(goff, wout, merge):
                for jp in range(0, 16, 2):
                    gl = []
                    proj_fm(W["w_in"], goff + jp * 128, 256, T, nsrc, r_n, 16, lambda j, ps, r: gl.append((ps, r)))
                    for jj, (ps, r_ps) in enumerate(gl):
                        j = jp + jj
                        b = j % 2
                        gcol = (goff - OFF_GA) // 128 + j
                        P.add("act", I("activation", out=gtmp[b][:, 0:T], in_=ps[:, 0:T], func=AF.Sigmoid,
                                                                                bias=bgate[:, gcol:gcol + 1], scale=1.0),
                              reads=[r_ps, r_small], writes=[r_gtmp[b]])
                    proj_fm(wout, jp * 128, 256, T, lambda kc: yAB[:, kc, 0:T], r_yAB, 32, lambda j, ps, r, jp=jp: merge(jp + j, ps, r))

            def merge_a(j, ps, r_ps):
                b = j % 2
                P.add("dve", I("tensor_tensor", out=mbuf[:, j, 0:T], in0=gtmp[b][:, 0:T], in1=ps[:, 0:T], op=ALU.mult),
                      reads=[r_ps, r_gtmp[b]], writes=[r_m[j]])

            gate_and_proj(OFF_GA, W["w_out_ret"], merge_a)
            if stop == "ret":
                continue
            P.fence()

            o[0] = TOFF
            cin = alloc(520, F32); r_cin = Res()
            acc = alloc(512, F32); r_acc = Res()
            xsT = [alloc(512, BF16) for _ in range(2)]; r_xsT = [Res(), Res()]
            xdt = alloc(4 * 512, BF16, "p (b c) -> p b c", b=4); r_xdt = Res()
            xw = alloc(4 * 512, BF16, "p (b c) -> p b c", b=4); r_xw = Res()
            sz = alloc(4 * 512, BF16, "p (b c) -> p b c", b=4); r_sz = Res()
            BT = alloc(512, BF16); CT = alloc(512, BF16); r_BT, r_CT = Res(), Res()
            Btm = alloc(4 * 128, BF16, "p (b c) -> p b c", b=4); r_Btm = Res()
            Hf = alloc(512, F32); r_Hf = Res()
            Hb2 = [alloc(512, BF16) for _ in range(2)]; r_Hb2 = [Res(), Res()]
            hbi = [0]
            Rb = alloc(512, F32); r_Rb = Res()
            Eb = alloc(512, F32); r_Eb = Res()
            MT2 = [alloc(8 * 128, BF16, "p (h i) -> p h i", h=8) for _ in range(2)]; r_MT2 = [Res(), Res()]
            cbm = alloc(64, F32); r_cbm = Res()
            y1 = alloc(512, F32); r_y1 = Res()
            ytm = alloc(4 * 512, F32, "p (b c) -> p b c", b=4); r_ytm = Res()
            dts = alloc(8 * 256, F32, "p (q b c) -> p q b c", q=8, b=4)
            r_dts = [Res() for _ in range(8)]
            DT, DTA, CUM, CL, DEND, ECUM, W2 = range(7)

            ssq = alloc(8, F32); r_ssq = Res()
            junk = y1; r_junk = r_y1
            dve = lambda f, rd, wr: P.add("dve", f, reads=rd, writes=wr)
            actf = lambda f, rd, wr: P.add("act", f, reads=rd, writes=wr)
            selC = sel64 if C == 64 else sel16

            def ep_dt(tb, ps, r_ps):
                dve(I("tensor_tensor", out=dts[0:TB, DT, tb, :], in0=ps[0:TB, 0:64], in1=dtb[0:TB, :], op=ALU.add), [r_ps, r_small], [r_dts[DT]])
            proj_tm(W["w_in"], OFF_DT, 64, T, TB, NTB, ep_dt)
            actf(I("activation", out=dts[0:TB, DT, 0:NTB, :], in_=dts[0:TB, DT, 0:NTB, :], func=AF.Exp), [r_dts[DT]], [r_dts[DT]])
            actf(I("activation", out=dts[0:TB, DT, 0:NTB, :], in_=dts[0:TB, DT, 0:NTB, :], func=AF.Ln, bias=epsc[0:TB, 1:2], scale=1.0), [r_dts[DT]], [r_dts[DT]])
            dve(I("tensor_tensor", out=dts[0:TB, DTA, 0:NTB, :], in0=dts[0:TB, DT, 0:NTB, :],
                                          in1=abc[0:TB, :].unsqueeze(1).to_broadcast([TB, NTB, 64]), op=ALU.mult), [r_dts[DT], r_abc], [r_dts[DTA]])
            for tb in range(NTB):
                pc, r_pc = bank()
                mm(pc[0:TB, 0:64], bdmask[0:TB, 0:TB], dts[0:TB, DTA, tb, :], True, True, [r_cst, r_dts[DTA]], [r_pc])
                mm(pc[0:TB, 64:128], bdones[0:TB, 0:TB], dts[0:TB, DTA, tb, :], True, True, [r_cst, r_dts[DTA]], [r_pc])
                actf(I("activation", out=dts[0:TB, CUM, tb, :], in_=pc[0:TB, 0:64], func=AF.Copy), [r_pc], [r_dts[CUM]])
                actf(I("activation", out=dts[0:TB, ECUM, tb, :], in_=pc[0:TB, 0:64], func=AF.Exp), [r_pc], [r_dts[ECUM]])
                dve(I("tensor_tensor", out=dts[0:TB, DEND, tb, :], in0=pc[0:TB, 64:128], in1=dts[0:TB, CUM, tb, :], op=ALU.subtract),
                    [r_pc, r_dts[CUM]], [r_dts[DEND]])
            actf(I("activation", out=dts[0:TB, DEND, 0:NTB, :], in_=dts[0:TB, DEND, 0:NTB, :], func=AF.Exp), [r_dts[DEND]], [r_dts[DEND]])
            dve(I("tensor_tensor", out=dts[0:TB, W2, 0:NTB, :], in0=dts[0:TB, DEND, 0:NTB, :], in1=dts[0:TB, DT, 0:NTB, :], op=ALU.mult),
                [r_dts[DEND], r_dts[DT]], [r_dts[W2]])

            def conv_chunk(cidx, ps, r_ps, outap, r_out):
                actf(I("activation", out=cin[:, 3:3 + T], in_=ps[:, 0:T], func=AF.Copy), [r_ps], [r_cin])
                dve(I("tensor_copy", cin[:, 0:3], hist[:, cidx, :]), [r_hist[cidx], r_cin], [r_cin])
                dve(I("tensor_scalar", acc[:, 0:T], cin[:, 0:T], convw[:, cidx, 0:1], None, op0=ALU.mult), [r_cin, r_small], [r_acc])
                for i in range(1, 4):
                    dve(I("scalar_tensor_tensor", out=acc[:, 0:T], in0=cin[:, i:i + T], scalar=convw[:, cidx, i:i + 1], in1=acc[:, 0:T],
                                                              op0=ALU.mult, op1=ALU.add), [r_cin, r_acc, r_small], [r_acc])
                dve(I("tensor_copy", hist[:, cidx, :], cin[:, T:T + 3]), [r_cin], [r_hist[cidx]])
                actf(I("activation", out=outap, in_=acc[:, 0:T], func=AF.Silu, bias=convb[:, cidx:cidx + 1], scale=1.0), [r_acc, r_small], [r_out])

            for g in range(8):
                if first and kind == "p":
                    dve(I("memset", Hf[:], 0.0), [], [r_Hf])
                    dve(I("memset", Hb2[hbi[0] % 2][:], 0.0), [], [r_Hb2[hbi[0] % 2]])
                else:
                    src = (sssm_d if first else ssm_o[kind])[:, g * 512:(g + 1) * 512]
                    P.add(MQ, I("dma_start", out=Hf[:], in_=src), reads=[r_ssm_o[kind][g]], writes=[r_Hf], key="ss")
                    actf(I("activation", out=Hb2[hbi[0] % 2][:], in_=Hf[:], func=AF.Copy), [r_Hf], [r_Hb2[hbi[0] % 2]])
                def zgen_f(g=g):
                    for half in range(2):
                        def ep_z(tb, ps, r_ps, half=half):
                            actf(I("activation", out=sz[0:TB, tb, half * 256:(half + 1) * 256], in_=ps[0:TB, 0:256], func=AF.Copy), [r_ps], [r_sz])
                        yield from proj_tm_gen(W["w_in"], OFF_Z + g * 512 + half * 256, 256, T, TB, NTB, ep_z)
                zgen = zgen_f()
                if DBG_NOILV:
                    for _ in zgen:
                        pass
                bl = []
                proj_fm(W["w_in"], OFF_X + 4096 + g * 128, 128, T, nsrc, r_n, 16, lambda j, ps, r: bl.append((ps, r)), colw=128)
                conv_chunk(32 + g, bl[0][0], bl[0][1], BT[:, 0:T], r_BT)
                cl_ = []
                proj_fm(W["w_in"], OFF_X + 5120 + g * 128, 128, T, nsrc, r_n, 16, lambda j, ps, r: cl_.append((ps, r)), colw=128)
                conv_chunk(40 + g, cl_[0][0], cl_[0][1], CT[:, 0:T], r_CT)
                for tb in range(NTB):
                    ps, r_ps = bank()
                    mm(ps[0:TB, 0:128], BT[:, tb * TB:(tb + 1) * TB], identb[:], True, True, [r_BT, r_idb], [r_ps])
                    actf(I("activation", out=Btm[0:TB, tb, :], in_=ps[0:TB, 0:128], func=AF.Copy), [r_ps], [r_Btm])
                xb = [bank() for _ in range(NTB)]
                for xpair in range(2):
                    xl = []
                    proj_fm(W["w_in"], OFF_X + g * 512 + xpair * 256, 256, T, nsrc, r_n, 16, lambda j, ps, r: xl.append((ps, r)))
                    for xi in range(2):
                        xc = xpair * 2 + xi
                        b = xc % 2
                        conv_chunk(g * 4 + xc, xl[xi][0], xl[xi][1], xsT[b][:, 0:T], r_xsT[b])
                        for tb in range(NTB):
                            mm(xb[tb][0][0:TB, xc * 128:(xc + 1) * 128], xsT[b][:, tb * TB:(tb + 1) * TB], identb[:], True, True,
                               [r_xsT[b], r_idb], [xb[tb][1]])
                for tb in range(NTB):
                    xps, r_xps = xb[tb]
                    hs = slice(g * 8, (g + 1) * 8)
                    bc = lambda q: dts[0:TB, q, tb, hs].unsqueeze(2).to_broadcast([TB, 8, 64])
                    x3 = xps[0:TB, :].rearrange("p (h q) -> p h q", h=8)
                    dve(I("tensor_tensor", out=xdt[0:TB, tb, :].rearrange("p (h q) -> p h q", h=8), in0=x3, in1=bc(DT), op=ALU.mult),
                        [r_xps, r_dts[DT]], [r_xdt])
                    dve(I("tensor_tensor", out=xw[0:TB, tb, :].rearrange("p (h q) -> p h q", h=8), in0=x3, in1=bc(W2), op=ALU.mult),
                        [r_xps, r_dts[W2]], [r_xw])
                    dve(I("tensor_tensor", out=ytm[0:TB, tb, :].rearrange("p (h q) -> p h q", h=8), in0=x3,
                                                                in1=dskip[0:TB, hs].unsqueeze(2).to_broadcast([TB, 8, 64]), op=ALU.mult),
                        [r_xps, r_small], [r_ytm])
                if g == 0:
                    dve(I("memset", MT2[0][:], 0.0), [], [r_MT2[0]])
                    dve(I("memset", MT2[1][:], 0.0), [], [r_MT2[1]])
                hs = slice(g * 8, (g + 1) * 8)

                def stage_a(tb):
                    tsl = slice(tb * TB, (tb + 1) * TB)
                    dve(I("tensor_tensor", out=Rb[0:TB, 0:8 * C].rearrange("p (h i) -> p h i", h=8),
                          in0=dts[0:TB, DTA, tb, hs].unsqueeze(2).to_broadcast([TB, 8, C]),
                          in1=triloc[0:TB, 0:C].unsqueeze(1).to_broadcast([TB, 8, C]), op=ALU.mult),
                        [r_dts[DTA], r_cst], [r_Rb])
                    ps, r_ps = bank()
                    mm(ps[0:TB, 0:TB], BT[:, tsl], CT[:, tsl], True, True, [r_BT, r_CT], [r_ps])
                    pseg, r_pseg = bank()
                    mm(pseg[0:TB, 0:8 * C], bdU[0:TB, 0:TB], Rb[0:TB, 0:8 * C], True, True, [r_cst, r_Rb], [r_pseg])
                    for c in range(CPB):
                        p0 = c * C
                        dve(I("tensor_tensor", out=cbm[p0:p0 + C, 0:C], in0=ps[p0:p0 + C, p0:p0 + C], in1=triloc[p0:p0 + C, 0:C], op=ALU.mult),
                            [r_ps, r_cst], [r_cbm])
                    actf(I("activation", out=Eb[0:TB, 0:8 * C], in_=pseg[0:TB, 0:8 * C], func=AF.Exp), [r_pseg], [r_Eb])

                def stage_b(tb):
                    MT = MT2[tb % 2]
                    r_MT = r_MT2[tb % 2]
                    for c in range(CPB):
                        p0 = c * C
                        dve(I("tensor_tensor", out=MT[p0:p0 + C, :, p0:p0 + C],
                              in0=Eb[p0:p0 + C, 0:8 * C].rearrange("p (h i) -> p h i", h=8),
                              in1=cbm[p0:p0 + C, 0:C].unsqueeze(1).to_broadcast([C, 8, C]), op=ALU.mult),
                            [r_Eb, r_cbm], [r_MT])
                    py, r_py = bank()
                    for hh in range(8):
                        mm(py[0:TB, hh * 64:(hh + 1) * 64], MT[0:TB, hh, 0:TB], xdt[0:TB, tb, hh * 64:(hh + 1) * 64], True, True, [r_MT, r_xdt], [r_py])
                    dve(I("tensor_tensor", out=ytm[0:TB, tb, :], in0=ytm[0:TB, tb, :], in1=py[0:TB, :], op=ALU.add),
                        [r_py, r_ytm], [r_ytm])

                def chain_step(tb, c):
                    tsl = slice(tb * TB, (tb + 1) * TB)
                    p0 = c * C
                    csl = slice(p0, p0 + C)
                    ph, r_ph = bank()
                    mm(ph[:, :], Btm[csl, tb, :], xw[csl, tb, :], True, True, [r_Btm, r_xw], [r_ph])
                    pcl, r_pcl = bank()
                    mm(pcl[:, 0:64], selC[csl, :], dts[csl, ECUM, tb, :], True, True, [r_cst, r_dts[ECUM]], [r_pcl])
                    hb_i = hbi[0] % 2
                    pch, r_pch = bank()
                    mm(pch[0:TB, :], CT[:, tsl], Hb2[hb_i][:], True, True, [r_CT, r_Hb2[hb_i]], [r_pch])
                    dve(I("tensor_tensor", out=Hf[:].rearrange("p (h q) -> p h q", h=8), in0=Hf[:].rearrange("p (h q) -> p h q", h=8),
                          in1=pcl[:, hs].unsqueeze(2).to_broadcast([128, 8, 64]), op=ALU.mult), [r_pcl, r_Hf], [r_Hf])
                    dve(I("tensor_tensor", out=Hf[:], in0=Hf[:], in1=ph[:, :], op=ALU.add), [r_ph, r_Hf], [r_Hf])
                    hbi[0] += 1
                    actf(I("activation", out=Hb2[hbi[0] % 2][:], in_=Hf[:], func=AF.Copy), [r_Hf], [r_Hb2[hbi[0] % 2]])
                    dve(I("tensor_tensor", out=y1[csl, :].rearrange("p (h q) -> p h q", h=8),
                          in0=pch[csl, :].rearrange("p (h q) -> p h q", h=8),
                          in1=dts[csl, ECUM, tb, hs].unsqueeze(2).to_broadcast([C, 8, 64]), op=ALU.mult),
                        [r_pch, r_dts[ECUM]], [r_y1])
                    dve(I("tensor_tensor", out=ytm[csl, tb, :], in0=ytm[csl, tb, :], in1=y1[csl, :], op=ALU.add),
                        [r_y1, r_ytm], [r_ytm])
                    next(zgen, None)

                stage_a(0)
                stage_b(0)
                for tb in range(NTB):
                    nxt = tb + 1 < NTB
                    if nxt:
                        stage_a(tb + 1)
                    chain_step(tb, 0)
                    if nxt:
                        stage_b(tb + 1)
                    for c in range(1, CPB):
                        chain_step(tb, c)
                for _ in zgen:
                    pass
                dst = ssm_o[kind][:, g * 512:(g + 1) * 512]
                P.add(MQ, I("dma_start", out=dst, in_=Hf[:]), reads=[r_Hf], writes=[r_ssm_o[kind][g]], key="ss")
                for tb in range(NTB):
                    actf(I("activation", out=sz[0:TB, tb, :], in_=sz[0:TB, tb, :], func=AF.Silu), [r_sz], [r_sz])
                for tb in range(NTB):
                    dve(I("tensor_tensor", out=ytm[0:TB, tb, :], in0=ytm[0:TB, tb, :], in1=sz[0:TB, tb, :], op=ALU.mult),
                        [r_ytm, r_sz], [r_ytm])
                    actf(I("activation", out=junk[0:TB, :], in_=ytm[0:TB, tb, :], func=AF.Square, accum_out=ssq[0:TB, tb:tb + 1]),
                         [r_ytm], [r_junk, r_ssq])
                actf(I("activation", out=ssq[0:TB, 0:NTB], in_=ssq[0:TB, 0:NTB], func=AF.Sqrt, bias=epsc[0:TB, 0:1], scale=1.0 / 512), [r_ssq, r_idb], [r_ssq])
                dve(I("reciprocal", ssq[0:TB, 0:NTB], ssq[0:TB, 0:NTB]), [r_ssq], [r_ssq])
                for tb in range(NTB):
                    dve(I("tensor_scalar", sz[0:TB, tb, :], ytm[0:TB, tb, :], ssq[0:TB, tb:tb + 1], None, op0=ALU.mult),
                        [r_ytm, r_ssq], [r_sz])
                for fc in range(4):
                    ps, r_ps = bank()
                    for tb in range(NTB):
                        mm(ps[:, tb * TB:(tb + 1) * TB], sz[0:TB, tb, fc * 128:(fc + 1) * 128], identb[0:TB, 0:TB], True, True, [r_sz, r_idb], [r_ps])
                    j = g * 4 + fc
                    actf(I("activation", out=yAB[:, j, 0:T], in_=ps[:, 0:T], func=AF.Copy, scale=ssmw[:, j:j + 1]),
                         [r_ps, r_small], [r_yAB[j]])

            def merge_b(j, ps, r_ps):
                b = j % 2
                P.add("dve", I("tensor_tensor", out=gtmp[b][:, 0:T], in0=gtmp[b][:, 0:T], in1=ps[:, 0:T], op=ALU.mult),
                      reads=[r_ps, r_gtmp[b]], writes=[r_gtmp[b]])
                P.add("dve", I("tensor_tensor", out=mbuf[:, j, 0:T], in0=gtmp[b][:, 0:T], in1=mbuf[:, j, 0:T], op=ALU.add),
                      reads=[r_gtmp[b], r_m[j]], writes=[r_m[j]])

            gate_and_proj(OFF_GB, W["w_out_ssm"], merge_b)

            if is_last:
                P.add(MQ, I("dma_start", out=conv_o[kind].rearrange("p (c i) -> p c i", i=3), in_=hist[:]),
                      reads=r_hist, key="st")

            def ep_wo(j, ps, r_ps):
                P.add("dve", I("tensor_tensor", out=h[:, j, 0:T], in0=h[:, j, 0:T], in1=ps[:, 0:T], op=ALU.add),
                      reads=[r_ps, r_h[j]], writes=[r_h[j]])
            proj_fm(W["w_out"], 0, D, T, lambda kc: mbuf[:, kc, 0:T], r_m, 16, ep_wo)
            if stop == "mix":
                continue
            P.fence()
            rmsnorm(T, 2, lambda kc: (n[:, kc, 0:T], r_n[kc]))
            ffn(T, W["ffn2_w_gate"], W["ffn2_w_up"], W["ffn2_w_down"])
            P.fence()
            yo = carve(0, 16 * 512, F32, "p (k t) -> p k t", k=16)
            r_yo = [Res() for _ in range(16)]
            rmsnorm(T, 3, lambda kc: (yo[:, kc, 0:T], r_yo[kc]))
            for kq in range(4):
                P.add(MQ, I("dma_start", out=yT[kq * 512:(kq + 1) * 512, t0:t0 + T].rearrange("(k p) t -> p k t", p=128),
                                                         in_=yo[:, kq * 4:(kq + 1) * 4, 0:T]),
                      reads=r_yo[kq * 4:(kq + 1) * 4], key="y")

        if stop is not None:
            for (t0, T, C, kind) in tiles:
                pass
            P.fence()
            t0, T, C, kind = tiles[-1]
            P.add(MQ, I("dma_start", out=yT[:, t0:t0 + T].rearrange("(k p) t -> p k t", p=128), in_=h[:, :, 0:T]),
                  reads=r_h, key="y")
        P.emit_all(nc)
    return nc


def _consts():
    cst = np.zeros((128, 1088), np.float32)
    j = np.arange(128)[:, None]
    i = np.arange(128)[None, :]
    same = (j // 64) == (i // 64)
    cst[:, 0:128] = (same & (i >= j))
    cst[:, 128:256] = (same & (j > i))
    cst[:, 256:384] = same
    cst[:, 384:448] = ((np.arange(128)[:, None] % 64) <= np.arange(64)[None, :])
    cst[:, 448:576] = np.eye(128)
    cst[:, 576:704] = 1.0
    cst[:, 704:832] = (i >= j)
    cst[63, 832:960] = 1.0
    cst[127, 832:960] = 1.0
    cst[15, 960:1088] = 1.0
    pos = np.concatenate([np.arange(SEQ, dtype=np.float32), np.arange(DEC, dtype=np.float32) + np.float32(PAST)])
    inv = (np.float32(10000.0) ** (-np.arange(128, dtype=np.float32) / np.float32(128))).astype(np.float32)
    ang = (pos[None, :] * inv[:, None]).astype(np.float32)
    rope = np.stack([np.cos(ang), np.sin(ang)], axis=1).astype(np.float32)
    dqk = np.zeros((8, 128, 2, 528), np.float32)
    for hd in range(8):
        for (o_, T_) in ((0, 512), (512, 16)):
            idx = np.arange(T_, dtype=np.float64) + 1.0
            dqk[hd, :, 0, o_:o_ + T_] = np.exp(idx * LG[hd])[None, :]
            dqk[hd, :, 1, o_:o_ + T_] = (np.exp(-idx * LG[hd]) / 16.0)[None, :]
    return cst, rope, dqk


def _fm(v, nch):
    return np.ascontiguousarray(np.asarray(v, np.float32).reshape(nch, 128).T)


TILES = [(i * 512, 512, 64, "p") for i in range(8)] + [(SEQ, DEC, DEC, "s")]
_CACHE = {}


def make_in_maps(inp, cores):
    cst, rope, dqk = _consts()
    small = np.zeros((128, 640), np.float32)
    small[:, 0:16] = _fm(inp["norm_ffn1"], 16)
    small[:, 16:32] = _fm(inp["norm_mix"], 16)
    small[:, 32:48] = _fm(inp["norm_ffn2"], 16)
    small[:, 48:64] = _fm(inp["norm_final"], 16)
    small[:, 64:96] = _fm(inp["ret_norm_w"], 32)
    small[:, 96:128] = _fm(inp["ssm_norm_w"], 32)
    small[:, 128:160] = _fm(inp["b_gate"], 32)
    cw = np.asarray(inp["conv_w"], np.float32)
    small[:, 160:352] = cw.reshape(4, 48, 128).transpose(2, 1, 0).reshape(128, 192)
    small[:, 352:400] = _fm(inp["conv_b"], 48)
    small[:, 400:464] = np.asarray(inp["dt_bias"], np.float32)[None, :]
    small[:, 464:528] = np.asarray(inp["a_log"], np.float32)[None, :]
    small[:, 528:592] = np.asarray(inp["d_skip"], np.float32)[None, :]
    wnames = ["ffn1_w_gate", "ffn1_w_up", "ffn1_w_down", "w_in", "w_out_ret", "w_out_ssm", "w_out",
              "ffn2_w_gate", "ffn2_w_up", "ffn2_w_down"]
    shared = {k: np.ascontiguousarray(np.asarray(inp[k], np.float32)) for k in wnames}
    shared.update(small=small, cst=cst, rope=rope, dqk=dqk)
    maps = []
    for b in cores:
        xa = np.concatenate([np.asarray(inp["x_prompt"][b], np.float32), np.asarray(inp["x_sample"][b], np.float32)], axis=0)
        m = dict(shared)
        m["xT"] = np.ascontiguousarray(xa.T)
        m["sret"] = np.ascontiguousarray(np.asarray(inp["state_ret"][b], np.float32))
        m["sssmT"] = np.ascontiguousarray(np.asarray(inp["state_ssm"][b], np.float32).reshape(4096, 128).T)
        sc = np.asarray(inp["state_conv"][b], np.float32)
        m["sconv"] = np.ascontiguousarray(sc.reshape(3, 48, 128).transpose(2, 1, 0).reshape(128, 144))
        maps.append(m)
    return maps


def assemble(results, nb):
    y = np.stack([r["yT"].T for r in results])
    y_prompt = np.ascontiguousarray(y[:, :SEQ])
    y_sample = np.ascontiguousarray(y[:, SEQ:])

    def ssm(k):
        return np.stack([r[k].T.reshape(64, 64, 128) for r in results])

    def conv(k):
        return np.stack([r[k].reshape(128, 48, 3).transpose(2, 1, 0).reshape(3, 6144) for r in results])

    return (y_prompt, y_sample,
            np.stack([r["ret_p"] for r in results]), ssm("ssm_pT"), conv("conv_p"),
            np.stack([r["ret_s"] for r in results]), ssm("ssm_sT"), conv("conv_s"))


def kernel(**inputs):
    if "nc" not in _CACHE:
        _CACHE["nc"] = build(TILES)
    nc = _CACHE["nc"]
    maps = make_in_maps(inputs, list(range(8)))
    res = run_bass_kernel_spmd(nc, maps, core_ids=list(range(8)))
    return assemble(res.results, 8)
```

```python
import math
from contextlib import ExitStack
import numpy as np
import concourse.bass as bass
import concourse.mybir as mybir
from concourse.bass_utils import run_bass_kernel_spmd

F32 = mybir.dt.float32
BF16 = mybir.dt.bfloat16
AF = mybir.ActivationFunctionType
ALU = mybir.AluOpType

D = 2048
DFF = 5632
NKC = 16
SEQ = 4096
DEC = 16
PAST = 1024
NTOK = SEQ + DEC
EPS = 1e-6
OFF_Q, OFF_K, OFF_V, OFF_G, OFF_Z, OFF_X, OFF_DT, OFF_GA, OFF_GB = 0, 2048, 4096, 8192, 12288, 16384, 22528, 22592, 24640
LG = [math.log1p(-2.0 ** (-5.0 - h)) for h in range(8)]
DBG_NOILV = False
DBG_ONEHB = False
DBG_OLDECL = True
WQ = "sp"
MQ = "sp"
WORDER = ["ffn1_w_gate", "ffn1_w_up", "ffn1_w_down", "w_in", "w_out_ret", "w_out_ssm", "w_out",
          "ffn2_w_gate", "ffn2_w_up", "ffn2_w_down"]


class Res:
    __slots__ = ("name", "last_w", "readers")

    def __init__(self, name=""):
        self.name = name
        self.last_w = None
        self.readers = {}


class Op:
    __slots__ = ("eng", "emit", "deps", "signal", "count", "key", "pos", "is_dma")

    def __init__(self, eng, emit, key):
        self.eng = eng
        self.emit = emit
        self.deps = []
        self.signal = False
        self.count = 0
        self.key = key
        self.is_dma = key is not None
        self.pos = 0


class Prog:
    ENGS = ("pe", "act", "dve", "pool", "sp")

    def __init__(self):
        self.ops = []
        self.by_eng = {e: [] for e in self.ENGS}
        self.seen = {e: {} for e in self.ENGS}
        self.keycnt = {}
        self.cpos = {e: 0 for e in self.ENGS}
        self.last_compute = {}
        self.dma_last = {}

    def _k(self, p):
        return ("k", p.key) if p.is_dma else ("e", p.eng)

    def add(self, eng, emit, reads=(), writes=(), key=None, extra_deps=()):
        op = Op(eng, emit, key)
        idx = len(self.ops)
        self.ops.append(op)
        self.by_eng[eng].append(op)
        if key is not None:
            self.keycnt[key] = self.keycnt.get(key, 0) + 1
            op.pos = self.keycnt[key]
            self.dma_last[key] = idx
        elif emit is not None:
            self.cpos[eng] += 1
            op.pos = self.cpos[eng]
            self.last_compute[eng] = idx
        best = {}

        def cand(d):
            p = self.ops[d]
            k = self._k(p)
            if k not in best or self.ops[best[k]].pos < p.pos:
                best[k] = d

        for r in reads:
            if r.last_w is not None:
                cand(r.last_w)
        for r in writes:
            if r.last_w is not None:
                cand(r.last_w)
            for d in r.readers.values():
                cand(d)
        for d in extra_deps:
            cand(d)
        seen = self.seen[eng]
        for k, d in best.items():
            p = self.ops[d]
            if d == idx:
                continue
            if (not p.is_dma) and p.eng == "pe" and eng == "pe" and not op.is_dma and emit is not None:
                continue
            if seen.get(k, 0) >= p.pos:
                continue
            seen[k] = p.pos
            op.deps.append(d)
            p.signal = True
        if emit is not None:
            k = self._k(op)
            for r in reads:
                r.readers[k] = idx
            for r in writes:
                r.last_w = idx
                r.readers = {}
        return op

    def fence(self, engs=("pe", "act", "dve", "sp")):
        deps = [self.last_compute[e] for e in engs if e in self.last_compute]
        deps += [d for k, d in self.dma_last.items() if not (str(k).startswith("w") or str(k).startswith("cv"))]
        for e in engs:
            self.add(e, None, extra_deps=deps)

    def emit_all(self, nc, final_wait_eng="sp"):
        with ExitStack() as st:
            esem = {e: st.enter_context(nc.semaphore("s_" + e)) for e in self.ENGS}
            ksem = {k: st.enter_context(nc.semaphore("k_" + str(k))) for k in self.keycnt}
            for e in self.ENGS:
                c = 0
                for op in self.by_eng[e]:
                    if op.is_dma:
                        op.count = 16 * op.pos
                    elif op.signal:
                        c += 1
                        op.count = c
            block = st.enter_context(nc.Block())

            def run(e, eng):
                waited = {}
                for op in self.by_eng[e]:
                    for d in op.deps:
                        p = self.ops[d]
                        s = ksem[p.key] if p.is_dma else esem[p.eng]
                        wk = self._k(p)
                        if waited.get(wk, 0) >= p.count:
                            continue
                        waited[wk] = p.count
                        eng.wait_ge(s, p.count)
                    if op.emit is None:
                        continue
                    ins = op.emit(eng)
                    if op.is_dma:
                        ins.then_inc(ksem[op.key], 16)
                    elif op.signal:
                        ins.then_inc(esem[e], 1)
                if e == final_wait_eng:
                    for k, n in self.keycnt.items():
                        eng.wait_ge(ksem[k], 16 * n)

            block.tensor(lambda eng: run("pe", eng))
            block.scalar(lambda eng: run("act", eng))
            block.vector(lambda eng: run("dve", eng))
            block.gpsimd(lambda eng: run("pool", eng))
            block.sync(lambda eng: run("sp", eng))


def I(method, *a, **kw):
    return lambda e: getattr(e, method)(*a, **kw)


def build(tiles, stop=None):
    nc = bass.Bass("TRN2", target_bir_lowering=False)
    P = Prog()

    def din(name, shape, dt=F32):
        return nc.dram_tensor(name, list(shape), dt, kind="ExternalInput").ap()

    def dout(name, shape, dt=F32):
        return nc.dram_tensor(name, list(shape), dt, kind="ExternalOutput").ap()

    xT = din("xT", [D, NTOK])
    W = {}
    for nm, shp in (("ffn1_w_gate", [D, DFF]), ("ffn1_w_up", [D, DFF]), ("ffn1_w_down", [DFF, D]),
                    ("w_in", [D, 26688]), ("w_out_ret", [4096, D]), ("w_out_ssm", [4096, D]), ("w_out", [D, D]),
                    ("ffn2_w_gate", [D, DFF]), ("ffn2_w_up", [D, DFF]), ("ffn2_w_down", [DFF, D])):
        W[nm] = {"f32": din(nm, shp), "shape": shp, "res": [],
                 "bf": nc.dram_tensor(nm + "_bf", list(shp), BF16, kind="Internal").ap()}
    small_d = din("small", [128, 640])
    cst_d = din("cst", [128, 1088])
    rope_d = din("rope", [128, 2, NTOK])
    dqk_d = din("dqk", [8, 128, 2, 528])
    sret_d = din("sret", [8, 256, 512])
    sssm_d = din("sssmT", [128, 4096])
    sconv_d = din("sconv", [128, 144])
    yT = dout("yT", [D, NTOK])
    ret_o = {"p": dout("ret_p", [8, 256, 512]), "s": dout("ret_s", [8, 256, 512])}
    ssm_o = {"p": dout("ssm_pT", [128, 4096]), "s": dout("ssm_sT", [128, 4096])}
    conv_o = {"p": dout("conv_p", [128, 144]), "s": dout("conv_s", [128, 144])}
    r_ret_o = {k: [Res() for _ in range(8)] for k in "ps"}
    r_ssm_o = {k: [Res() for _ in range(8)] for k in "ps"}

    with ExitStack() as st:
        def sb(name, shape, dt):
            return st.enter_context(nc.sbuf_tensor("sb_" + name, list(shape), dt))

        h = sb("h", [128, 16, 512], F32)
        r_h = [Res("h%d" % i) for i in range(16)]
        n = sb("n", [128, 16, 512], BF16)
        r_n = [Res("n%d" % i) for i in range(16)]
        NS = 4
        wsl = [sb("ws%d" % i, [128, 4096], BF16) for i in range(NS)]
        r_ws = [Res("ws%d" % i) for i in range(NS)]
        small = sb("small", [128, 640], F32)
        r_small = Res("small")
        cst = sb("cst", [128, 1088], F32)
        r_cst = Res("cst")
        identb = sb("identb", [128, 128], BF16)
        onesb = sb("onesb", [128, 128], BF16)
        epsc = sb("epsc", [128, 2], F32)
        rope = sb("rope", [128, 2, 512], F32)
        r_rope = Res("rope")
        dqk2 = [sb("dqk%d" % i, [128, 2, 512], F32) for i in range(2)]
        r_dqk2 = [Res(), Res()]
        hist = sb("hist", [128, 48, 3], F32)
        r_hist = [Res() for _ in range(48)]
        abc = sb("abc", [128, 64], F32)
        sqb = [sb("sqb%d" % i, [128, 512], BF16) for i in range(2)]
        r_sqb = [Res(), Res()]
        rs = sb("rs", [128, 512], F32)
        r_rs = Res("rs")
        AW = 26000
        ar = sb("arena", [128, AW], F32)
        pb = [st.enter_context(nc.psum_tensor("pb%d" % i, [128, 512], F32)) for i in range(8)]
        r_pb = [Res("pb%d" % i) for i in range(8)]
        bank_i = [0]

        def bank():
            i = bank_i[0] % 8
            bank_i[0] += 1
            return pb[i], r_pb[i]

        def carve(off, nelem, dt, pat=None, **kw):
            nb = nelem * (4 if dt == F32 else 2)
            assert off % 4 == 0 and nb % 4 == 0 and off + nb <= AW * 4, (off, nb)
            v = ar[:, off // 4:(off + nb) // 4]
            if dt != F32:
                v = v.bitcast(dt)
            if pat:
                v = v.rearrange(pat, **kw)
            return v

        normw = small[:, 0:64].rearrange("p (a k) -> p a k", a=4)
        retw = small[:, 64:96]
        ssmw = small[:, 96:128]
        bgate = small[:, 128:160]
        convw = small[:, 160:352].rearrange("p (c i) -> p c i", i=4)
        convb = small[:, 352:400]
        dtb = small[:, 400:464]
        alog = small[:, 464:528]
        dskip = small[:, 528:592]
        bdmask = cst[:, 0:128]
        bdU = cst[:, 128:256]
        bdones = cst[:, 256:384]
        triloc = cst[:, 384:448]
        identf = cst[:, 448:576]
        onesf = cst[:, 576:704]
        causal = cst[:, 704:832]
        sel64 = cst[:, 832:960]
        sel16 = cst[:, 960:1088]

        P.add(MQ, I("dma_start", out=small[:], in_=small_d), writes=[r_small], key="c0")
        P.add(MQ, I("dma_start", out=cst[:], in_=cst_d), writes=[r_cst], key="c1")
        r_idb = Res("identb")
        P.add("act", I("activation", out=identb[:], in_=identf, func=AF.Copy), reads=[r_cst], writes=[r_idb])
        P.add("dve", I("tensor_copy", onesb[:], onesf), reads=[r_cst], writes=[r_idb])
        P.add("dve", I("memset", epsc[:, 0:1], EPS), writes=[r_idb])
        P.add("dve", I("memset", epsc[:, 1:2], 1.0), writes=[r_idb])
        r_abc = Res("abc")
        P.add("act", I("activation", out=abc[:], in_=alog, func=AF.Exp), reads=[r_small], writes=[r_abc])
        P.add("dve", I("tensor_scalar", abc[:], abc[:], -1.0, None, op0=ALU.mult), reads=[r_abc], writes=[r_abc])

        ws_i = [0]

        wregion = {}
        first_pass = [True]

        def wload(wsrc, k0c, nkc, c0, ncols):
            s = ws_i[0] % NS
            ws_i[0] += 1
            assert nkc * ncols <= 4096
            view = wsl[s][:, 0:nkc * ncols].rearrange("p (k c) -> p k c", k=nkc)
            rk = (id(wsrc), k0c, nkc, c0, ncols)
            bfv = wsrc["bf"][k0c * 128:(k0c + nkc) * 128, c0:c0 + ncols].rearrange("(k p) c -> p k c", p=128)
            if first_pass[0]:
                assert rk not in wregion
                src = wsrc["f32"][k0c * 128:(k0c + nkc) * 128, c0:c0 + ncols].rearrange("(k p) c -> p k c", p=128)
                P.add("pool", I("dma_start", out=view, in_=src), writes=[r_ws[s]], key="w%d" % s)
                rr = Res()
                wregion[rk] = rr
                P.add(WQ, I("dma_start", out=bfv, in_=view), reads=[r_ws[s]], writes=[rr], key="wb%d" % s)
            else:
                P.add(WQ, I("dma_start", out=view, in_=bfv), reads=[wregion[rk]], writes=[r_ws[s]], key="w%d" % s)
            return view, r_ws[s]

        def mm(out, lhsT, rhs, start, stop, rd, wr):
            P.add("pe", I("matmul", out, lhsT, rhs, start=start, stop=stop, skip_group_check=True), reads=rd, writes=wr)

        def proj_fm(wap, c0, ncols_total, T, src, r_src, nk, epilogue, colw=256):
            kpl = 4096 // colw
            nkh = (nk + kpl - 1) // kpl
            for cb in range(0, ncols_total, colw):
                cw = min(colw, ncols_total - cb)
                slabs = []
                for kh in range(nkh):
                    k0 = kh * kpl
                    kn = min(kpl, nk - k0)
                    slabs.append((k0, kn) + wload(wap, k0, kn, c0 + cb, cw))
                for jj in range(0, cw, 128):
                    jw = min(128, cw - jj)
                    ps, r_ps = bank()
                    for (k0, kn, view, r_w) in slabs:
                        for kk in range(kn):
                            kc = k0 + kk
                            mm(ps[0:jw, 0:T], view[:, kk, jj:jj + jw], src(kc), kc == 0, kc == nk - 1,
                               [r_w, r_src[kc]], [r_ps])
                    epilogue((cb + jj) // 128, ps, r_ps)

        def proj_tm(wap, c0, ncols, T, TB, NTB, epilogue):
            view, r_w = wload(wap, 0, 16, c0, ncols)
            for tb in range(NTB):
                ps, r_ps = bank()
                for kc in range(16):
                    mm(ps[0:TB, 0:ncols], n[:, kc, tb * TB:(tb + 1) * TB], view[:, kc, :], kc == 0, kc == 15,
                       [r_w, r_n[kc]], [r_ps])
                epilogue(tb, ps, r_ps)

        def proj_tm_gen(wap, c0, ncols, T, TB, NTB, epilogue):
            view, r_w = wload(wap, 0, 16, c0, ncols)
            for tb in range(NTB):
                ps, r_ps = bank()
                for kc in range(16):
                    mm(ps[0:TB, 0:ncols], n[:, kc, tb * TB:(tb + 1) * TB], view[:, kc, :], kc == 0, kc == 15,
                       [r_w, r_n[kc]], [r_ps])
                epilogue(tb, ps, r_ps)
                yield

        def rmsnorm(T, wi, outf):
            ps, r_ps = bank()
            for kc in range(16):
                b = kc % 2
                P.add("act", I("activation", out=sqb[b][:, 0:T], in_=h[:, kc, 0:T], func=AF.Square),
                      reads=[r_h[kc]], writes=[r_sqb[b]])
                mm(ps[:, 0:T], onesb[:], sqb[b][:, 0:T], kc == 0, kc == 15, [r_sqb[b], r_idb], [r_ps])
            P.add("act", I("activation", out=rs[:, 0:T], in_=ps[:, 0:T], func=AF.Sqrt, bias=epsc[:, 0:1], scale=1.0 / D),
                  reads=[r_ps, r_idb], writes=[r_rs])
            P.add("dve", I("reciprocal", rs[:, 0:T], rs[:, 0:T]), reads=[r_rs], writes=[r_rs])
            for kc in range(16):
                o, r_o = outf(kc)
                P.add("dve", I("scalar_tensor_tensor", out=o, in0=h[:, kc, 0:T], scalar=normw[:, wi, kc:kc + 1],
                                                                          in1=rs[:, 0:T], op0=ALU.mult, op1=ALU.mult),
                      reads=[r_h[kc], r_rs, r_small], writes=[r_o])

        def ffn(T, wg, wu, wd):
            act = carve(0, 44 * 512, BF16, "p (j t) -> p j t", j=44)
            r_act = [Res() for _ in range(44)]
            sgt = [carve(45056 + 2048 * i, 512, F32) for i in range(2)]
            r_sgt = [Res(), Res()]
            nsrc = lambda kc: n[:, kc, 0:T]
            pend = {}

            def ep_gate(j, ps, r_ps):
                b = j % 2
                P.add("act", I("activation", out=sgt[b][:, 0:T], in_=ps[:, 0:T], func=AF.Silu),
                      reads=[r_ps], writes=[r_sgt[b]])

            def ep_up(j, ps, r_ps):
                b = j % 2
                P.add("dve", I("tensor_tensor", out=act[:, j, 0:T], in0=sgt[b][:, 0:T], in1=ps[:, 0:T], op=ALU.mult),
                      reads=[r_ps, r_sgt[b]], writes=[r_act[j]])

            for cb in range(0, DFF, 256):
                proj_fm(wg, cb, 256, T, nsrc, r_n, 16, lambda j, ps, r, cb=cb: ep_gate(cb // 128 + j, ps, r))
                proj_fm(wu, cb, 256, T, nsrc, r_n, 16, lambda j, ps, r, cb=cb: ep_up(cb // 128 + j, ps, r))

            def ep_down(j, ps, r_ps):
                P.add("dve", I("scalar_tensor_tensor", out=h[:, j, 0:T], in0=ps[:, 0:T], scalar=0.5, in1=h[:, j, 0:T],
                                                              op0=ALU.mult, op1=ALU.add),
                      reads=[r_ps, r_h[j]], writes=[r_h[j]])

            proj_fm(wd, 0, D, T, lambda kc: act[:, kc, 0:T], r_act, 44, ep_down)

        converted = [False]
        for (t0, T, C, kind) in tiles:
            TB = min(128, T)
            NTB = T // TB
            CPB = TB // C
            first = (t0 == 0) or kind == "s"
            is_last = (kind == "s") or (t0 + T == SEQ)
            gC = [math.exp(C * LG[hd]) for hd in range(8)]
            P.fence()
            for kq in range(4):
                P.add(MQ, I("dma_start", out=h[:, kq * 4:(kq + 1) * 4, 0:T],
                                                         in_=xT[kq * 512:(kq + 1) * 512, t0:t0 + T].rearrange("(k p) t -> p k t", p=128)),
                      writes=r_h[kq * 4:(kq + 1) * 4], key="x%d" % kq)
            P.add(MQ, I("dma_start", out=rope[:, :, 0:T], in_=rope_d[:, :, t0:t0 + T]), writes=[r_rope], key="c2")
            dqo = 0 if kind == "p" else 512
            if first:
                if kind == "p":
                    P.add("dve", I("memset", hist[:], 0.0), writes=r_hist)
                else:
                    P.add(MQ, I("dma_start", out=hist[:], in_=sconv_d.rearrange("p (c i) -> p c i", i=3)),
                          writes=r_hist, key="c4")
            rmsnorm(T, 0, lambda kc: (n[:, kc, 0:T], r_n[kc]))
            ffn(T, W["ffn1_w_gate"], W["ffn1_w_up"], W["ffn1_w_down"])
            if stop == "ffn1":
                continue
            P.fence()
            rmsnorm(T, 1, lambda kc: (n[:, kc, 0:T], r_n[kc]))
            nsrc = lambda kc: n[:, kc, 0:T]
            yAB = carve(0, 32 * 512, BF16, "p (j t) -> p j t", j=32)
            r_yAB = [Res() for _ in range(32)]
            mbuf = carve(32768, 16 * 512, BF16, "p (j t) -> p j t", j=16)
            r_m = [Res() for _ in range(16)]
            TOFF = 49152
            cosT = rope[:, 0, 0:T]
            sinT = rope[:, 1, 0:T]

            o = [TOFF]

            def alloc(nelem, dt, pat=None, **kw):
                nb = nelem * (4 if dt == F32 else 2)
                nb = (nb + 31) // 32 * 32
                v = carve(o[0], nb // (4 if dt == F32 else 2), dt)
                v = v[:, 0:nelem]
                if pat:
                    v = v.rearrange(pat, **kw)
                o[0] += nb
                return v

            qh = alloc(2 * 512, BF16, "p (a t) -> p a t", a=2); r_qh = Res()
            kh = alloc(2 * 512, BF16, "p (a t) -> p a t", a=2); r_kh = Res()
            t1 = alloc(512, F32); t2 = alloc(512, F32); t3 = alloc(512, F32)
            r_t1, r_t2, r_t3 = Res(), Res(), Res()
            vtm = alloc(4 * 512, BF16, "p (b c) -> p b c", b=4); r_vtm = Res()
            ktm = alloc(4 * 256, BF16, "p (b c) -> p b c", b=4); r_ktm = Res()
            sg = alloc(4 * 512, F32, "p (a t) -> p a t", a=4); r_sg = [Res() for _ in range(4)]
            oT = alloc(4 * 512, F32, "p (a t) -> p a t", a=4); r_oT = [Res() for _ in range(4)]
            Sf = alloc(2 * 512, F32, "p (a v) -> p a v", a=2); r_Sf = Res()
            Sb = alloc(2 * 512, BF16, "p (a v) -> p a v", a=2); r_Sb = Res()
            pT = [alloc(512, BF16) for _ in range(2)]; r_pT = [Res(), Res()]
            obf = alloc(512, BF16); r_obf = Res()
            mean = alloc(512, F32); msq = alloc(512, F32); r_mean, r_msq = Res(), Res()
            gtmp = [alloc(512, F32) for _ in range(2)]; r_gtmp = [Res(), Res()]

            for hd in range(8):
                dqkb = dqk2[hd % 2]
                r_dq = r_dqk2[hd % 2]
                P.add(MQ, I("dma_start", out=dqkb[:, :, 0:T], in_=dqk_d[hd, :, :, dqo:dqo + T]), writes=[r_dq], key="dq%d" % (hd % 2))
                dq = dqkb[:, 0, 0:T]
                dk = dqkb[:, 1, 0:T]
                gT = math.exp(T * LG[hd])
                if first and kind == "p":
                    P.add("dve", I("memset", Sf[:], 0.0), writes=[r_Sf])
                    P.add("dve", I("memset", Sb[:], 0.0), writes=[r_Sb])
                else:
                    src = (sret_d if first else ret_o[kind])[hd].rearrange("(a p) v -> p a v", p=128)
                    P.add(MQ, I("dma_start", out=Sf[:], in_=src), reads=[r_ret_o[kind][hd]], writes=[r_Sf], key="sr")
                    P.add("act", I("activation", out=Sb[:], in_=Sf[:], func=AF.Copy), reads=[r_Sf], writes=[r_Sb])
                for (off, dst, r_dst, dtab) in ((OFF_Q, qh, r_qh, dq), (OFF_K, kh, r_kh, dk)):
                    pss = []
                    proj_fm(W["w_in"], off + hd * 256, 256, T, nsrc, r_n, 16, lambda j, ps, r: pss.append((ps, r)))
                    (p1, r1), (p2, r2) = pss
                    dve = lambda f, rd, wr: P.add("dve", f, reads=rd, writes=wr)
                    dve(I("tensor_tensor", out=t1[:, 0:T], in0=p1[:, 0:T], in1=cosT, op=ALU.mult), [r1, r_rope], [r_t1])
                    dve(I("tensor_tensor", out=t2[:, 0:T], in0=p2[:, 0:T], in1=sinT, op=ALU.mult), [r2, r_rope], [r_t2])
                    dve(I("tensor_tensor", out=t3[:, 0:T], in0=t1[:, 0:T], in1=t2[:, 0:T], op=ALU.subtract), [r_t1, r_t2], [r_t3])
                    dve(I("tensor_tensor", out=dst[:, 0, 0:T], in0=t3[:, 0:T], in1=dtab, op=ALU.mult), [r_t3, r_dq], [r_dst])
                    dve(I("tensor_tensor", out=t1[:, 0:T], in0=p1[:, 0:T], in1=sinT, op=ALU.mult), [r1, r_rope], [r_t1])
                    dve(I("tensor_tensor", out=t2[:, 0:T], in0=p2[:, 0:T], in1=cosT, op=ALU.mult), [r2, r_rope], [r_t2])
                    dve(I("tensor_tensor", out=t3[:, 0:T], in0=t1[:, 0:T], in1=t2[:, 0:T], op=ALU.add), [r_t1, r_t2], [r_t3])
                    dve(I("tensor_tensor", out=dst[:, 1, 0:T], in0=t3[:, 0:T], in1=dtab, op=ALU.mult), [r_t3, r_dq], [r_dst])
                for half in range(2):
                    def ep_v(tb, ps, r_ps, half=half):
                        P.add("act", I("activation", out=vtm[0:TB, tb, half * 256:(half + 1) * 256], in_=ps[0:TB, 0:256], func=AF.Copy),
                              reads=[r_ps], writes=[r_vtm])
                    proj_tm(W["w_in"], OFF_V + hd * 512 + half * 256, 256, T, TB, NTB, ep_v)
                for tb in range(NTB):
                    ps, r_ps = bank()
                    for a in range(2):
                        mm(ps[0:TB, a * 128:(a + 1) * 128], kh[:, a, tb * TB:(tb + 1) * TB], identb[:], True, True, [r_kh, r_idb], [r_ps])
                    P.add("act", I("activation", out=ktm[0:TB, tb, :], in_=ps[0:TB, 0:256], func=AF.Copy, scale=gT),
                          reads=[r_ps], writes=[r_ktm])
                for bi in range(NTB):
                    isl = slice(bi * TB, (bi + 1) * TB)
                    ps, r_ps = bank()
                    for bj in range(bi + 1):
                        jsl = slice(bj * TB, (bj + 1) * TB)
                        for a in range(2):
                            mm(ps[0:TB, bj * TB:(bj + 1) * TB], kh[:, a, jsl], qh[:, a, isl], (bj == 0 and a == 0), a == 1, [r_kh, r_qh], [r_ps])
                    pb_ = bi % 2
                    if bi > 0:
                        P.add("act", I("activation", out=pT[pb_][0:TB, 0:bi * TB], in_=ps[0:TB, 0:bi * TB], func=AF.Copy),
                              reads=[r_ps], writes=[r_pT[pb_]])
                    P.add("dve", I("tensor_tensor", out=pT[pb_][0:TB, bi * TB:(bi + 1) * TB], in0=ps[0:TB, bi * TB:(bi + 1) * TB],
                                   in1=causal[0:TB, 0:TB], op=ALU.mult), reads=[r_ps, r_cst], writes=[r_pT[pb_]])
                    po_, r_po = bank()
                    for vc in range(4):
                        for bj in range(bi + 1):
                            mm(po_[:, vc * 128:vc * 128 + TB], vtm[0:TB, bj, vc * 128:(vc + 1) * 128], pT[pb_][0:TB, bj * TB:(bj + 1) * TB],
                               (vc == 0 and bj == 0), False, [r_vtm, r_pT[pb_]], [r_po])
                        for a in range(2):
                            mm(po_[:, vc * 128:vc * 128 + TB], Sb[:, a, vc * 128:(vc + 1) * 128], qh[:, a, isl], False, (a == 1), [r_Sb, r_qh], [r_po])
                    for vc in range(4):
                        P.add("act", I("activation", out=oT[:, vc, isl], in_=po_[:, vc * 128:vc * 128 + TB], func=AF.Copy),
                              reads=[r_po], writes=[r_oT[vc]])
                def ep_g(j, ps, r_ps):
                    P.add("act", I("activation", out=sg[:, j, 0:T], in_=ps[:, 0:T], func=AF.Silu), reads=[r_ps], writes=[r_sg[j]])
                for half in range(2):
                    proj_fm(W["w_in"], OFF_G + hd * 512 + half * 256, 256, T, nsrc, r_n, 16,
                            lambda j, ps, r, half=half: ep_g(half * 2 + j, ps, r))
                for a in range(2):
                    pu, r_pu = bank()
                    for tb in range(NTB):
                        mm(pu[:, :], ktm[0:TB, tb, a * 128:(a + 1) * 128], vtm[0:TB, tb, :], tb == 0, tb == NTB - 1, [r_ktm, r_vtm], [r_pu])
                    P.add("dve", I("scalar_tensor_tensor", out=Sf[:, a, :], in0=Sf[:, a, :], scalar=gT, in1=pu[:, :],
                                   op0=ALU.mult, op1=ALU.add), reads=[r_pu, r_Sf], writes=[r_Sf])
                dst = ret_o[kind][hd].rearrange("(a p) v -> p a v", p=128)
                P.add(MQ, I("dma_start", out=dst, in_=Sf[:]), reads=[r_Sf], writes=[r_ret_o[kind][hd]], key="sr")
                psm, r_psm = bank()
                pss2, r_pss2 = bank()
                for vc in range(4):
                    P.add("act", I("activation", out=obf[:, 0:T], in_=oT[:, vc, 0:T], func=AF.Copy), reads=[r_oT[vc]], writes=[r_obf])
                    mm(psm[:, 0:T], onesb[:], obf[:, 0:T], vc == 0, vc == 3, [r_obf, r_idb], [r_psm])
                    b = vc % 2
                    P.add("act", I("activation", out=sqb[b][:, 0:T], in_=oT[:, vc, 0:T], func=AF.Square), reads=[r_oT[vc]], writes=[r_sqb[b]])
                    mm(pss2[:, 0:T], onesb[:], sqb[b][:, 0:T], vc == 0, vc == 3, [r_sqb[b], r_idb], [r_pss2])
                dve = lambda f, rd, wr: P.add("dve", f, reads=rd, writes=wr)
                dve(I("tensor_scalar", mean[:, 0:T], psm[:, 0:T], 1.0 / 512, None, op0=ALU.mult), [r_psm], [r_mean])
                dve(I("tensor_tensor", out=msq[:, 0:T], in0=mean[:, 0:T], in1=mean[:, 0:T], op=ALU.mult), [r_mean], [r_msq])
                dve(I("scalar_tensor_tensor", out=msq[:, 0:T], in0=pss2[:, 0:T], scalar=1.0 / 512, in1=msq[:, 0:T],
                                                                op0=ALU.mult, op1=ALU.subtract), [r_pss2, r_msq], [r_msq])
                P.add("act", I("activation", out=msq[:, 0:T], in_=msq[:, 0:T], func=AF.Sqrt, bias=epsc[:, 0:1], scale=1.0), reads=[r_msq, r_idb], writes=[r_msq])
                dve(I("reciprocal", msq[:, 0:T], msq[:, 0:T]), [r_msq], [r_msq])
                for vc in range(4):
                    j = hd * 4 + vc
                    dve(I("tensor_tensor", out=oT[:, vc, 0:T], in0=oT[:, vc, 0:T], in1=mean[:, 0:T], op=ALU.subtract),
                        [r_oT[vc], r_mean], [r_oT[vc]])
                    dve(I("tensor_tensor", out=oT[:, vc, 0:T], in0=oT[:, vc, 0:T], in1=msq[:, 0:T], op=ALU.mult),
                        [r_oT[vc], r_msq], [r_oT[vc]])
                    dve(I("scalar_tensor_tensor", out=yAB[:, j, 0:T], in0=oT[:, vc, 0:T], scalar=retw[:, j:j + 1], in1=sg[:, vc, 0:T],
                                                                     op0=ALU.mult, op1=ALU.mult),
                        [r_oT[vc], r_sg[vc], r_small], [r_yAB[j]])

            def gate_and_proj(goff, wout, merge):
                for jp in range(0, 16, 2):
                    gl = []
                    proj_fm(W["w_in"], goff + jp * 128, 256, T, nsrc, r_n, 16, lambda j, ps, r: gl.append((ps, r)))
                    for jj, (ps, r_ps) in enumerate(gl):
                        j = jp + jj
                        b = j % 2
                        gcol = (goff - OFF_GA) // 128 + j
                        P.add("act", I("activation", out=gtmp[b][:, 0:T], in_=ps[:, 0:T], func=AF.Sigmoid,
                                                                                bias=bgate[:, gcol:gcol + 1], scale=1.0),
                              reads=[r_ps, r_small], writes=[r_gtmp[b]])
                    proj_fm(wout, jp * 128, 256, T, lambda kc: yAB[:, kc, 0:T], r_yAB, 32, lambda j, ps, r, jp=jp: merge(jp + j, ps, r))

            def merge_a(j, ps, r_ps):
                b = j % 2
                P.add("dve", I("tensor_tensor", out=mbuf[:, j, 0:T], in0=gtmp[b][:, 0:T], in1=ps[:, 0:T], op=ALU.mult),
                      reads=[r_ps, r_gtmp[b]], writes=[r_m[j]])

            gate_and_proj(OFF_GA, W["w_out_ret"], merge_a)
            if stop == "ret":
                continue
            P.fence()

            o[0] = TOFF
            cin = alloc(520, F32); r_cin = Res()
            acc = alloc(512, F32); r_acc = Res()
            xsT = [alloc(512, F32) for _ in range(2)]; r_xsT = [Res(), Res()]
            xdt = alloc(4 * 512, BF16, "p (b c) -> p b c", b=4); r_xdt = Res()
            xw = alloc(4 * 512, BF16, "p (b c) -> p b c", b=4); r_xw = Res()
            sz = alloc(4 * 512, BF16, "p (b c) -> p b c", b=4); r_sz = Res()
            BT = alloc(512, BF16); CT = alloc(512, BF16); r_BT, r_CT = Res(), Res()
            Btm = alloc(4 * 128, BF16, "p (b c) -> p b c", b=4); r_Btm = Res()
            Hf = alloc(512, F32); r_Hf = Res()
            Hb2 = [alloc(512, BF16) for _ in range(2)]; r_Hb2 = [Res(), Res()]
            hbi = [0]
            Rb = alloc(512, F32); r_Rb = Res()
            Eb = alloc(512, F32); r_Eb = Res()
            MT2 = [alloc(8 * 128, BF16, "p (h i) -> p h i", h=8) for _ in range(2)]; r_MT2 = [Res(), Res()]
            cbm = alloc(64, F32); r_cbm = Res()
            y1 = alloc(512, F32); r_y1 = Res()
            ytm = alloc(4 * 512, F32, "p (b c) -> p b c", b=4); r_ytm = Res()
            dts = alloc(8 * 256, F32, "p (q b c) -> p q b c", q=8, b=4)
            r_dts = [Res() for _ in range(8)]
            DT, DTA, CUM, CL, DEND, ECUM, W2 = range(7)

            ssq = alloc(8, F32); r_ssq = Res()
            junk = y1; r_junk = r_y1
            dve = lambda f, rd, wr: P.add("dve", f, reads=rd, writes=wr)
            actf = lambda f, rd, wr: P.add("act", f, reads=rd, writes=wr)
            selC = sel64 if C == 64 else sel16

            def ep_dt(tb, ps, r_ps):
                dve(I("tensor_tensor", out=dts[0:TB, DT, tb, :], in0=ps[0:TB, 0:64], in1=dtb[0:TB, :], op=ALU.add), [r_ps, r_small], [r_dts[DT]])
            proj_tm(W["w_in"], OFF_DT, 64, T, TB, NTB, ep_dt)
            actf(I("activation", out=dts[0:TB, DT, 0:NTB, :], in_=dts[0:TB, DT, 0:NTB, :], func=AF.Exp), [r_dts[DT]], [r_dts[DT]])
            actf(I("activation", out=dts[0:TB, DT, 0:NTB, :], in_=dts[0:TB, DT, 0:NTB, :], func=AF.Ln, bias=epsc[0:TB, 1:2], scale=1.0), [r_dts[DT]], [r_dts[DT]])
            dve(I("tensor_tensor", out=dts[0:TB, DTA, 0:NTB, :], in0=dts[0:TB, DT, 0:NTB, :],
                                          in1=abc[0:TB, :].unsqueeze(1).to_broadcast([TB, NTB, 64]), op=ALU.mult), [r_dts[DT], r_abc], [r_dts[DTA]])
            for tb in range(NTB):
                pc, r_pc = bank()
                mm(pc[0:TB, 0:64], bdmask[0:TB, 0:TB], dts[0:TB, DTA, tb, :], True, True, [r_cst, r_dts[DTA]], [r_pc])
                mm(pc[0:TB, 64:128], bdones[0:TB, 0:TB], dts[0:TB, DTA, tb, :], True, True, [r_cst, r_dts[DTA]], [r_pc])
                actf(I("activation", out=dts[0:TB, CUM, tb, :], in_=pc[0:TB, 0:64], func=AF.Copy), [r_pc], [r_dts[CUM]])
                actf(I("activation", out=dts[0:TB, ECUM, tb, :], in_=pc[0:TB, 0:64], func=AF.Exp), [r_pc], [r_dts[ECUM]])
                dve(I("tensor_tensor", out=dts[0:TB, DEND, tb, :], in0=pc[0:TB, 64:128], in1=dts[0:TB, CUM, tb, :], op=ALU.subtract),
                    [r_pc, r_dts[CUM]], [r_dts[DEND]])
            actf(I("activation", out=dts[0:TB, DEND, 0:NTB, :], in_=dts[0:TB, DEND, 0:NTB, :], func=AF.Exp), [r_dts[DEND]], [r_dts[DEND]])
            dve(I("tensor_tensor", out=dts[0:TB, W2, 0:NTB, :], in0=dts[0:TB, DEND, 0:NTB, :], in1=dts[0:TB, DT, 0:NTB, :], op=ALU.mult),
                [r_dts[DEND], r_dts[DT]], [r_dts[W2]])

            def conv_chunk(cidx, ps, r_ps, outap, r_out):
                actf(I("activation", out=cin[:, 3:3 + T], in_=ps[:, 0:T], func=AF.Copy), [r_ps], [r_cin])
                dve(I("tensor_copy", cin[:, 0:3], hist[:, cidx, :]), [r_hist[cidx], r_cin], [r_cin])
                dve(I("tensor_scalar", acc[:, 0:T], cin[:, 0:T], convw[:, cidx, 0:1], None, op0=ALU.mult), [r_cin, r_small], [r_acc])
                for i in range(1, 4):
                    dve(I("scalar_tensor_tensor", out=acc[:, 0:T], in0=cin[:, i:i + T], scalar=convw[:, cidx, i:i + 1], in1=acc[:, 0:T],
                                                              op0=ALU.mult, op1=ALU.add), [r_cin, r_acc, r_small], [r_acc])
                dve(I("tensor_copy", hist[:, cidx, :], cin[:, T:T + 3]), [r_cin], [r_hist[cidx]])
                actf(I("activation", out=outap, in_=acc[:, 0:T], func=AF.Silu, bias=convb[:, cidx:cidx + 1], scale=1.0), [r_acc, r_small], [r_out])

            for g in range(8):
                if first and kind == "p":
                    dve(I("memset", Hf[:], 0.0), [], [r_Hf])
                    dve(I("memset", Hb2[hbi[0] % 2][:], 0.0), [], [r_Hb2[hbi[0] % 2]])
                else:
                    src = (sssm_d if first else ssm_o[kind])[:, g * 512:(g + 1) * 512]
                    P.add(MQ, I("dma_start", out=Hf[:], in_=src), reads=[r_ssm_o[kind][g]], writes=[r_Hf], key="ss")
                    actf(I("activation", out=Hb2[hbi[0] % 2][:], in_=Hf[:], func=AF.Copy), [r_Hf], [r_Hb2[hbi[0] % 2]])
                def zgen_f(g=g):
                    for half in range(2):
                        def ep_z(tb, ps, r_ps, half=half):
                            actf(I("activation", out=sz[0:TB, tb, half * 256:(half + 1) * 256], in_=ps[0:TB, 0:256], func=AF.Copy), [r_ps], [r_sz])
                        yield from proj_tm_gen(W["w_in"], OFF_Z + g * 512 + half * 256, 256, T, TB, NTB, ep_z)
                zgen = zgen_f()
                if DBG_NOILV:
                    for _ in zgen:
                        pass
                bl = []
                proj_fm(W["w_in"], OFF_X + 4096 + g * 128, 128, T, nsrc, r_n, 16, lambda j, ps, r: bl.append((ps, r)), colw=128)
                conv_chunk(32 + g, bl[0][0], bl[0][1], BT[:, 0:T], r_BT)
                cl_ = []
                proj_fm(W["w_in"], OFF_X + 5120 + g * 128, 128, T, nsrc, r_n, 16, lambda j, ps, r: cl_.append((ps, r)), colw=128)
                conv_chunk(40 + g, cl_[0][0], cl_[0][1], CT[:, 0:T], r_CT)
                for tb in range(NTB):
                    ps, r_ps = bank()
                    mm(ps[0:TB, 0:128], BT[:, tb * TB:(tb + 1) * TB], identb[:], True, True, [r_BT, r_idb], [r_ps])
                    actf(I("activation", out=Btm[0:TB, tb, :], in_=ps[0:TB, 0:128], func=AF.Copy), [r_ps], [r_Btm])
                xb = [bank() for _ in range(NTB)]
                for xpair in range(2):
                    xl = []
                    proj_fm(W["w_in"], OFF_X + g * 512 + xpair * 256, 256, T, nsrc, r_n, 16, lambda j, ps, r: xl.append((ps, r)))
                    for xi in range(2):
                        xc = xpair * 2 + xi
                        b = xc % 2
                        conv_chunk(g * 4 + xc, xl[xi][0], xl[xi][1], xsT[b][:, 0:T], r_xsT[b])
                        for tb in range(NTB):
                            mm(xb[tb][0][0:TB, xc * 128:(xc + 1) * 128], xsT[b][:, tb * TB:(tb + 1) * TB], identf, True, True,
                               [r_xsT[b], r_cst], [xb[tb][1]])
                for tb in range(NTB):
                    xps, r_xps = xb[tb]
                    hs = slice(g * 8, (g + 1) * 8)
                    bc = lambda q: dts[0:TB, q, tb, hs].unsqueeze(2).to_broadcast([TB, 8, 64])
                    x3 = xps[0:TB, :].rearrange("p (h q) -> p h q", h=8)
                    dve(I("tensor_tensor", out=xdt[0:TB, tb, :].rearrange("p (h q) -> p h q", h=8), in0=x3, in1=bc(DT), op=ALU.mult),
                        [r_xps, r_dts[DT]], [r_xdt])
                    dve(I("tensor_tensor", out=xw[0:TB, tb, :].rearrange("p (h q) -> p h q", h=8), in0=x3, in1=bc(W2), op=ALU.mult),
                        [r_xps, r_dts[W2]], [r_xw])
                    dve(I("tensor_tensor", out=ytm[0:TB, tb, :].rearrange("p (h q) -> p h q", h=8), in0=x3,
                                                                in1=dskip[0:TB, hs].unsqueeze(2).to_broadcast([TB, 8, 64]), op=ALU.mult),
                        [r_xps, r_small], [r_ytm])
                if g == 0:
                    dve(I("memset", MT2[0][:], 0.0), [], [r_MT2[0]])
                    dve(I("memset", MT2[1][:], 0.0), [], [r_MT2[1]])
                hs = slice(g * 8, (g + 1) * 8)

                def stage_a(tb):
                    tsl = slice(tb * TB, (tb + 1) * TB)
                    dve(I("tensor_tensor", out=Rb[0:TB, 0:8 * C].rearrange("p (h i) -> p h i", h=8),
                          in0=dts[0:TB, DTA, tb, hs].unsqueeze(2).to_broadcast([TB, 8, C]),
                          in1=triloc[0:TB, 0:C].unsqueeze(1).to_broadcast([TB, 8, C]), op=ALU.mult),
                        [r_dts[DTA], r_cst], [r_Rb])
                    ps, r_ps = bank()
                    mm(ps[0:TB, 0:TB], BT[:, tsl], CT[:, tsl], True, True, [r_BT, r_CT], [r_ps])
                    pseg, r_pseg = bank()
                    mm(pseg[0:TB, 0:8 * C], bdU[0:TB, 0:TB], Rb[0:TB, 0:8 * C], True, True, [r_cst, r_Rb], [r_pseg])
                    for c in range(CPB):
                        p0 = c * C
                        dve(I("tensor_tensor", out=cbm[p0:p0 + C, 0:C], in0=ps[p0:p0 + C, p0:p0 + C], in1=triloc[p0:p0 + C, 0:C], op=ALU.mult),
                            [r_ps, r_cst], [r_cbm])
                    actf(I("activation", out=Eb[0:TB, 0:8 * C], in_=pseg[0:TB, 0:8 * C], func=AF.Exp), [r_pseg], [r_Eb])

                def stage_b(tb):
                    MT = MT2[tb % 2]
                    r_MT = r_MT2[tb % 2]
                    for c in range(CPB):
                        p0 = c * C
                        dve(I("tensor_tensor", out=MT[p0:p0 + C, :, p0:p0 + C],
                              in0=Eb[p0:p0 + C, 0:8 * C].rearrange("p (h i) -> p h i", h=8),
                              in1=cbm[p0:p0 + C, 0:C].unsqueeze(1).to_broadcast([C, 8, C]), op=ALU.mult),
                            [r_Eb, r_cbm], [r_MT])
                    py, r_py = bank()
                    for hh in range(8):
                        mm(py[0:TB, hh * 64:(hh + 1) * 64], MT[0:TB, hh, 0:TB], xdt[0:TB, tb, hh * 64:(hh + 1) * 64], True, True, [r_MT, r_xdt], [r_py])
                    dve(I("tensor_tensor", out=ytm[0:TB, tb, :], in0=ytm[0:TB, tb, :], in1=py[0:TB, :], op=ALU.add),
                        [r_py, r_ytm], [r_ytm])

                def chain_step(tb, c):
                    tsl = slice(tb * TB, (tb + 1) * TB)
                    p0 = c * C
                    csl = slice(p0, p0 + C)
                    ph, r_ph = bank()
                    mm(ph[:, :], Btm[csl, tb, :], xw[csl, tb, :], True, True, [r_Btm, r_xw], [r_ph])
                    pcl, r_pcl = bank()
                    mm(pcl[:, 0:64], selC[csl, :], dts[csl, ECUM, tb, :], True, True, [r_cst, r_dts[ECUM]], [r_pcl])
                    hb_i = hbi[0] % 2
                    pch, r_pch = bank()
                    mm(pch[0:TB, :], CT[:, tsl], Hb2[hb_i][:], True, True, [r_CT, r_Hb2[hb_i]], [r_pch])
                    dve(I("tensor_tensor", out=Hf[:].rearrange("p (h q) -> p h q", h=8), in0=Hf[:].rearrange("p (h q) -> p h q", h=8),
                          in1=pcl[:, hs].unsqueeze(2).to_broadcast([128, 8, 64]), op=ALU.mult), [r_pcl, r_Hf], [r_Hf])
                    dve(I("tensor_tensor", out=Hf[:], in0=Hf[:], in1=ph[:, :], op=ALU.add), [r_ph, r_Hf], [r_Hf])
                    hbi[0] += 1
                    actf(I("activation", out=Hb2[hbi[0] % 2][:], in_=Hf[:], func=AF.Copy), [r_Hf], [r_Hb2[hbi[0] % 2]])
                    dve(I("tensor_tensor", out=y1[csl, :].rearrange("p (h q) -> p h q", h=8),
                          in0=pch[csl, :].rearrange("p (h q) -> p h q", h=8),
                          in1=dts[csl, ECUM, tb, hs].unsqueeze(2).to_broadcast([C, 8, 64]), op=ALU.mult),
                        [r_pch, r_dts[ECUM]], [r_y1])
                    dve(I("tensor_tensor", out=ytm[csl, tb, :], in0=ytm[csl, tb, :], in1=y1[csl, :], op=ALU.add),
                        [r_y1, r_ytm], [r_ytm])
                    next(zgen, None)

                stage_a(0)
                stage_b(0)
                for tb in range(NTB):
                    nxt = tb + 1 < NTB
                    if nxt:
                        stage_a(tb + 1)
                    chain_step(tb, 0)
                    if nxt:
                        stage_b(tb + 1)
                    for c in range(1, CPB):
                        chain_step(tb, c)
                for _ in zgen:
                    pass
                dst = ssm_o[kind][:, g * 512:(g + 1) * 512]
                P.add(MQ, I("dma_start", out=dst, in_=Hf[:]), reads=[r_Hf], writes=[r_ssm_o[kind][g]], key="ss")
                for tb in range(NTB):
                    actf(I("activation", out=sz[0:TB, tb, :], in_=sz[0:TB, tb, :], func=AF.Silu), [r_sz], [r_sz])
                for tb in range(NTB):
                    dve(I("tensor_tensor", out=ytm[0:TB, tb, :], in0=ytm[0:TB, tb, :], in1=sz[0:TB, tb, :], op=ALU.mult),
                        [r_ytm, r_sz], [r_ytm])
                    actf(I("activation", out=junk[0:TB, :], in_=ytm[0:TB, tb, :], func=AF.Square, accum_out=ssq[0:TB, tb:tb + 1]),
                         [r_ytm], [r_junk, r_ssq])
                actf(I("activation", out=ssq[0:TB, 0:NTB], in_=ssq[0:TB, 0:NTB], func=AF.Sqrt, bias=epsc[0:TB, 0:1], scale=1.0 / 512), [r_ssq, r_idb], [r_ssq])
                dve(I("reciprocal", ssq[0:TB, 0:NTB], ssq[0:TB, 0:NTB]), [r_ssq], [r_ssq])
                for tb in range(NTB):
                    dve(I("tensor_scalar", ytm[0:TB, tb, :], ytm[0:TB, tb, :], ssq[0:TB, tb:tb + 1], None, op0=ALU.mult),
                        [r_ytm, r_ssq], [r_ytm])
                for fc in range(4):
                    ps, r_ps = bank()
                    for tb in range(NTB):
                        mm(ps[:, tb * TB:(tb + 1) * TB], ytm[0:TB, tb, fc * 128:(fc + 1) * 128], identf[0:TB, 0:TB], True, True, [r_ytm, r_cst], [r_ps])
                    j = g * 4 + fc
                    actf(I("activation", out=yAB[:, j, 0:T], in_=ps[:, 0:T], func=AF.Copy, scale=ssmw[:, j:j + 1]),
                         [r_ps, r_small], [r_yAB[j]])

            def merge_b(j, ps, r_ps):
                b = j % 2
                P.add("dve", I("tensor_tensor", out=gtmp[b][:, 0:T], in0=gtmp[b][:, 0:T], in1=ps[:, 0:T], op=ALU.mult),
                      reads=[r_ps, r_gtmp[b]], writes=[r_gtmp[b]])
                P.add("dve", I("tensor_tensor", out=mbuf[:, j, 0:T], in0=gtmp[b][:, 0:T], in1=mbuf[:, j, 0:T], op=ALU.add),
                      reads=[r_gtmp[b], r_m[j]], writes=[r_m[j]])

            gate_and_proj(OFF_GB, W["w_out_ssm"], merge_b)

            if is_last:
                P.add(MQ, I("dma_start", out=conv_o[kind].rearrange("p (c i) -> p c i", i=3), in_=hist[:]),
                      reads=r_hist, key="st")

            def ep_wo(j, ps, r_ps):
                P.add("dve", I("tensor_tensor", out=h[:, j, 0:T], in0=h[:, j, 0:T], in1=ps[:, 0:T], op=ALU.add),
                      reads=[r_ps, r_h[j]], writes=[r_h[j]])
            proj_fm(W["w_out"], 0, D, T, lambda kc: mbuf[:, kc, 0:T], r_m, 16, ep_wo)
            if stop == "mix":
                continue
            P.fence()
            rmsnorm(T, 2, lambda kc: (n[:, kc, 0:T], r_n[kc]))
            ffn(T, W["ffn2_w_gate"], W["ffn2_w_up"], W["ffn2_w_down"])
            P.fence()
            first_pass[0] = False
            yo = carve(0, 16 * 512, F32, "p (k t) -> p k t", k=16)
            r_yo = [Res() for _ in range(16)]
            rmsnorm(T, 3, lambda kc: (yo[:, kc, 0:T], r_yo[kc]))
            for kq in range(4):
                P.add(MQ, I("dma_start", out=yT[kq * 512:(kq + 1) * 512, t0:t0 + T].rearrange("(k p) t -> p k t", p=128),
                                                         in_=yo[:, kq * 4:(kq + 1) * 4, 0:T]),
                      reads=r_yo[kq * 4:(kq + 1) * 4], key="y")

        if stop is not None:
            for (t0, T, C, kind) in tiles:
                pass
            P.fence()
            t0, T, C, kind = tiles[-1]
            P.add(MQ, I("dma_start", out=yT[:, t0:t0 + T].rearrange("(k p) t -> p k t", p=128), in_=h[:, :, 0:T]),
                  reads=r_h, key="y")
        P.emit_all(nc)
    return nc


def _consts():
    cst = np.zeros((128, 1088), np.float32)
    j = np.arange(128)[:, None]
    i = np.arange(128)[None, :]
    same = (j // 64) == (i // 64)
    cst[:, 0:128] = (same & (i >= j))
    cst[:, 128:256] = (same & (j > i))
    cst[:, 256:384] = same
    cst[:, 384:448] = ((np.arange(128)[:, None] % 64) <= np.arange(64)[None, :])
    cst[:, 448:576] = np.eye(128)
    cst[:, 576:704] = 1.0
    cst[:, 704:832] = (i >= j)
    cst[63, 832:960] = 1.0
    cst[127, 832:960] = 1.0
    cst[15, 960:1088] = 1.0
    pos = np.concatenate([np.arange(SEQ, dtype=np.float32), np.arange(DEC, dtype=np.float32) + np.float32(PAST)])
    inv = (np.float32(10000.0) ** (-np.arange(128, dtype=np.float32) / np.float32(128))).astype(np.float32)
    ang = (pos[None, :] * inv[:, None]).astype(np.float32)
    rope = np.stack([np.cos(ang), np.sin(ang)], axis=1).astype(np.float32)
    dqk = np.zeros((8, 128, 2, 528), np.float32)
    for hd in range(8):
        for (o_, T_) in ((0, 512), (512, 16)):
            idx = np.arange(T_, dtype=np.float64) + 1.0
            dqk[hd, :, 0, o_:o_ + T_] = np.exp(idx * LG[hd])[None, :]
            dqk[hd, :, 1, o_:o_ + T_] = (np.exp(-idx * LG[hd]) / 16.0)[None, :]
    return cst, rope, dqk


def _fm(v, nch):
    return np.ascontiguousarray(np.asarray(v, np.float32).reshape(nch, 128).T)


TILES = [(i * 512, 512, 64, "p") for i in range(8)] + [(SEQ, DEC, DEC, "s")]
_CACHE = {}


def make_in_maps(inp, cores):
    cst, rope, dqk = _consts()
    small = np.zeros((128, 640), np.float32)
    small[:, 0:16] = _fm(inp["norm_ffn1"], 16)
    small[:, 16:32] = _fm(inp["norm_mix"], 16)
    small[:, 32:48] = _fm(inp["norm_ffn2"], 16)
    small[:, 48:64] = _fm(inp["norm_final"], 16)
    small[:, 64:96] = _fm(inp["ret_norm_w"], 32)
    small[:, 96:128] = _fm(inp["ssm_norm_w"], 32)
    small[:, 128:160] = _fm(inp["b_gate"], 32)
    cw = np.asarray(inp["conv_w"], np.float32)
    small[:, 160:352] = cw.reshape(4, 48, 128).transpose(2, 1, 0).reshape(128, 192)
    small[:, 352:400] = _fm(inp["conv_b"], 48)
    small[:, 400:464] = np.asarray(inp["dt_bias"], np.float32)[None, :]
    small[:, 464:528] = np.asarray(inp["a_log"], np.float32)[None, :]
    small[:, 528:592] = np.asarray(inp["d_skip"], np.float32)[None, :]
    wnames = ["ffn1_w_gate", "ffn1_w_up", "ffn1_w_down", "w_in", "w_out_ret", "w_out_ssm", "w_out",
              "ffn2_w_gate", "ffn2_w_up", "ffn2_w_down"]
    shared = {k: np.ascontiguousarray(np.asarray(inp[k], np.float32)) for k in wnames}
    shared.update(small=small, cst=cst, rope=rope, dqk=dqk)
    maps = []
    for b in cores:
        xa = np.concatenate([np.asarray(inp["x_prompt"][b], np.float32), np.asarray(inp["x_sample"][b], np.float32)], axis=0)
        m = dict(shared)
        m["xT"] = np.ascontiguousarray(xa.T)
        m["sret"] = np.ascontiguousarray(np.asarray(inp["state_ret"][b], np.float32))
        m["sssmT"] = np.ascontiguousarray(np.asarray(inp["state_ssm"][b], np.float32).reshape(4096, 128).T)
        sc = np.asarray(inp["state_conv"][b], np.float32)
        m["sconv"] = np.ascontiguousarray(sc.reshape(3, 48, 128).transpose(2, 1, 0).reshape(128, 144))
        maps.append(m)
    return maps


def assemble(results, nb):
    y = np.stack([r["yT"].T for r in results])
    y_prompt = np.ascontiguousarray(y[:, :SEQ])
    y_sample = np.ascontiguousarray(y[:, SEQ:])

    def ssm(k):
        return np.stack([r[k].T.reshape(64, 64, 128) for r in results])

    def conv(k):
        return np.stack([r[k].reshape(128, 48, 3).transpose(2, 1, 0).reshape(3, 6144) for r in results])

    return (y_prompt, y_sample,
            np.stack([r["ret_p"] for r in results]), ssm("ssm_pT"), conv("conv_p"),
            np.stack([r["ret_s"] for r in results]), ssm("ssm_sT"), conv("conv_s"))


def kernel(**inputs):
    if "nc" not in _CACHE:
        _CACHE["nc"] = build(TILES)
    nc = _CACHE["nc"]
    maps = make_in_maps(inputs, list(range(8)))
    res = run_bass_kernel_spmd(nc, maps, core_ids=list(range(8)))
    return assemble(res.results, 8)
```

```python
import math
from contextlib import ExitStack
import numpy as np
import concourse.bass as bass
import concourse.mybir as mybir
from concourse.bass_utils import run_bass_kernel_spmd

F32 = mybir.dt.float32
BF16 = mybir.dt.bfloat16
AF = mybir.ActivationFunctionType
ALU = mybir.AluOpType

D = 2048
DFF = 5632
NKC = 16
SEQ = 4096
DEC = 16
PAST = 1024
NTOK = SEQ + DEC
EPS = 1e-6
OFF_Q, OFF_K, OFF_V, OFF_G, OFF_Z, OFF_X, OFF_DT, OFF_GA, OFF_GB = 0, 2048, 4096, 8192, 12288, 16384, 22528, 22592, 24640
LG = [math.log1p(-2.0 ** (-5.0 - h)) for h in range(8)]
DBG_NOILV = False
DBG_ONEHB = False
DBG_OLDECL = True
WQ = "sp"
MQ = "sp"
WORDER = ["ffn1_w_gate", "ffn1_w_up", "ffn1_w_down", "w_in", "w_out_ret", "w_out_ssm", "w_out",
          "ffn2_w_gate", "ffn2_w_up", "ffn2_w_down"]


class Res:
    __slots__ = ("name", "last_w", "readers")

    def __init__(self, name=""):
        self.name = name
        self.last_w = None
        self.readers = {}


class Op:
    __slots__ = ("eng", "emit", "deps", "signal", "count", "key", "pos", "is_dma")

    def __init__(self, eng, emit, key):
        self.eng = eng
        self.emit = emit
        self.deps = []
        self.signal = False
        self.count = 0
        self.key = key
        self.is_dma = key is not None
        self.pos = 0


class Prog:
    ENGS = ("pe", "act", "dve", "pool", "sp")

    def __init__(self):
        self.ops = []
        self.by_eng = {e: [] for e in self.ENGS}
        self.seen = {e: {} for e in self.ENGS}
        self.keycnt = {}
        self.cpos = {e: 0 for e in self.ENGS}
        self.last_compute = {}
        self.dma_last = {}

    def _k(self, p):
        return ("k", p.key) if p.is_dma else ("e", p.eng)

    def add(self, eng, emit, reads=(), writes=(), key=None, extra_deps=()):
        op = Op(eng, emit, key)
        idx = len(self.ops)
        self.ops.append(op)
        self.by_eng[eng].append(op)
        if key is not None:
            self.keycnt[key] = self.keycnt.get(key, 0) + 1
            op.pos = self.keycnt[key]
            self.dma_last[key] = idx
        elif emit is not None:
            self.cpos[eng] += 1
            op.pos = self.cpos[eng]
            self.last_compute[eng] = idx
        best = {}

        def cand(d):
            p = self.ops[d]
            k = self._k(p)
            if k not in best or self.ops[best[k]].pos < p.pos:
                best[k] = d

        for r in reads:
            if r.last_w is not None:
                cand(r.last_w)
        for r in writes:
            if r.last_w is not None:
                cand(r.last_w)
            for d in r.readers.values():
                cand(d)
        for d in extra_deps:
            cand(d)
        seen = self.seen[eng]
        for k, d in best.items():
            p = self.ops[d]
            if d == idx:
                continue
            if (not p.is_dma) and p.eng == "pe" and eng == "pe" and not op.is_dma and emit is not None:
                continue
            if seen.get(k, 0) >= p.pos:
                continue
            seen[k] = p.pos
            op.deps.append(d)
            p.signal = True
        if emit is not None:
            k = self._k(op)
            for r in reads:
                r.readers[k] = idx
            for r in writes:
                r.last_w = idx
                r.readers = {}
        return op

    def fence(self, engs=("pe", "act", "dve", "sp")):
        deps = [self.last_compute[e] for e in engs if e in self.last_compute]
        deps += [d for k, d in self.dma_last.items() if not (str(k).startswith("w") or str(k).startswith("cv"))]
        for e in engs:
            self.add(e, None, extra_deps=deps)

    def emit_all(self, nc, final_wait_eng="sp"):
        with ExitStack() as st:
            esem = {e: st.enter_context(nc.semaphore("s_" + e)) for e in self.ENGS}
            ksem = {k: st.enter_context(nc.semaphore("k_" + str(k))) for k in self.keycnt}
            for e in self.ENGS:
                c = 0
                for op in self.by_eng[e]:
                    if op.is_dma:
                        op.count = 16 * op.pos
                    elif op.signal:
                        c += 1
                        op.count = c
            block = st.enter_context(nc.Block())

            def run(e, eng):
                waited = {}
                for op in self.by_eng[e]:
                    for d in op.deps:
                        p = self.ops[d]
                        s = ksem[p.key] if p.is_dma else esem[p.eng]
                        wk = self._k(p)
                        if waited.get(wk, 0) >= p.count:
                            continue
                        waited[wk] = p.count
                        eng.wait_ge(s, p.count)
                    if op.emit is None:
                        continue
                    ins = op.emit(eng)
                    if op.is_dma:
                        ins.then_inc(ksem[op.key], 16)
                    elif op.signal:
                        ins.then_inc(esem[e], 1)
                if e == final_wait_eng:
                    for k, n in self.keycnt.items():
                        eng.wait_ge(ksem[k], 16 * n)

            block.tensor(lambda eng: run("pe", eng))
            block.scalar(lambda eng: run("act", eng))
            block.vector(lambda eng: run("dve", eng))
            block.gpsimd(lambda eng: run("pool", eng))
            block.sync(lambda eng: run("sp", eng))


def I(method, *a, **kw):
    return lambda e: getattr(e, method)(*a, **kw)


def build(tiles, stop=None):
    nc = bass.Bass("TRN2", target_bir_lowering=False)
    P = Prog()

    def din(name, shape, dt=F32):
        return nc.dram_tensor(name, list(shape), dt, kind="ExternalInput").ap()

    def dout(name, shape, dt=F32):
        return nc.dram_tensor(name, list(shape), dt, kind="ExternalOutput").ap()

    xT = din("xT", [D, NTOK])
    W = {}
    for nm, shp in (("ffn1_w_gate", [D, DFF]), ("ffn1_w_up", [D, DFF]), ("ffn1_w_down", [DFF, D]),
                    ("w_in", [D, 26688]), ("w_out_ret", [4096, D]), ("w_out_ssm", [4096, D]), ("w_out", [D, D]),
                    ("ffn2_w_gate", [D, DFF]), ("ffn2_w_up", [D, DFF]), ("ffn2_w_down", [DFF, D])):
        W[nm] = {"f32": din(nm, shp), "shape": shp, "res": [],
                 "bf": nc.dram_tensor(nm + "_bf", list(shp), BF16, kind="Internal").ap()}
    small_d = din("small", [128, 640])
    cst_d = din("cst", [128, 1088])
    rope_d = din("rope", [128, 2, NTOK])
    dqk_d = din("dqk", [8, 128, 2, 528])
    sret_d = din("sret", [8, 256, 512])
    sssm_d = din("sssmT", [128, 4096])
    sconv_d = din("sconv", [128, 144])
    yT = dout("yT", [D, NTOK])
    ret_o = {"p": dout("ret_p", [8, 256, 512]), "s": dout("ret_s", [8, 256, 512])}
    ssm_o = {"p": dout("ssm_pT", [128, 4096]), "s": dout("ssm_sT", [128, 4096])}
    conv_o = {"p": dout("conv_p", [128, 144]), "s": dout("conv_s", [128, 144])}
    r_ret_o = {k: [Res() for _ in range(8)] for k in "ps"}
    r_ssm_o = {k: [Res() for _ in range(8)] for k in "ps"}

    with ExitStack() as st:
        def sb(name, shape, dt):
            return st.enter_context(nc.sbuf_tensor("sb_" + name, list(shape), dt))

        h = sb("h", [128, 16, 512], F32)
        r_h = [Res("h%d" % i) for i in range(16)]
        n = sb("n", [128, 16, 512], BF16)
        r_n = [Res("n%d" % i) for i in range(16)]
        NS = 4
        wsl = [sb("ws%d" % i, [128, 4096], BF16) for i in range(NS)]
        r_ws = [Res("ws%d" % i) for i in range(NS)]
        small = sb("small", [128, 640], F32)
        r_small = Res("small")
        cst = sb("cst", [128, 1088], F32)
        r_cst = Res("cst")
        identb = sb("identb", [128, 128], BF16)
        onesb = sb("onesb", [128, 128], BF16)
        epsc = sb("epsc", [128, 2], F32)
        rope = sb("rope", [128, 2, 512], F32)
        r_rope = Res("rope")
        dqk2 = [sb("dqk%d" % i, [128, 2, 512], F32) for i in range(2)]
        r_dqk2 = [Res(), Res()]
        hist = sb("hist", [128, 48, 3], F32)
        r_hist = [Res() for _ in range(48)]
        abc = sb("abc", [128, 64], F32)
        sqb = [sb("sqb%d" % i, [128, 512], BF16) for i in range(2)]
        r_sqb = [Res(), Res()]
        rs = sb("rs", [128, 512], F32)
        r_rs = Res("rs")
        AW = 26000
        ar = sb("arena", [128, AW], F32)
        pb = [st.enter_context(nc.psum_tensor("pb%d" % i, [128, 512], F32)) for i in range(8)]
        r_pb = [Res("pb%d" % i) for i in range(8)]
        bank_i = [0]

        def bank():
            i = bank_i[0] % 8
            bank_i[0] += 1
            return pb[i], r_pb[i]

        def carve(off, nelem, dt, pat=None, **kw):
            nb = nelem * (4 if dt == F32 else 2)
            assert off % 4 == 0 and nb % 4 == 0 and off + nb <= AW * 4, (off, nb)
            v = ar[:, off // 4:(off + nb) // 4]
            if dt != F32:
                v = v.bitcast(dt)
            if pat:
                v = v.rearrange(pat, **kw)
            return v

        normw = small[:, 0:64].rearrange("p (a k) -> p a k", a=4)
        retw = small[:, 64:96]
        ssmw = small[:, 96:128]
        bgate = small[:, 128:160]
        convw = small[:, 160:352].rearrange("p (c i) -> p c i", i=4)
        convb = small[:, 352:400]
        dtb = small[:, 400:464]
        alog = small[:, 464:528]
        dskip = small[:, 528:592]
        bdmask = cst[:, 0:128]
        bdU = cst[:, 128:256]
        bdones = cst[:, 256:384]
        triloc = cst[:, 384:448]
        identf = cst[:, 448:576]
        onesf = cst[:, 576:704]
        causal = cst[:, 704:832]
        sel64 = cst[:, 832:960]
        sel16 = cst[:, 960:1088]

        P.add(MQ, I("dma_start", out=small[:], in_=small_d), writes=[r_small], key="c0")
        P.add(MQ, I("dma_start", out=cst[:], in_=cst_d), writes=[r_cst], key="c1")
        r_idb = Res("identb")
        P.add("act", I("activation", out=identb[:], in_=identf, func=AF.Copy), reads=[r_cst], writes=[r_idb])
        P.add("dve", I("tensor_copy", onesb[:], onesf), reads=[r_cst], writes=[r_idb])
        P.add("dve", I("memset", epsc[:, 0:1], EPS), writes=[r_idb])
        P.add("dve", I("memset", epsc[:, 1:2], 1.0), writes=[r_idb])
        r_abc = Res("abc")
        P.add("act", I("activation", out=abc[:], in_=alog, func=AF.Exp), reads=[r_small], writes=[r_abc])
        P.add("dve", I("tensor_scalar", abc[:], abc[:], -1.0, None, op0=ALU.mult), reads=[r_abc], writes=[r_abc])

        ws_i = [0]

        wregion = {}
        first_pass = [True]

        def wload(wsrc, k0c, nkc, c0, ncols):
            s = ws_i[0] % NS
            ws_i[0] += 1
            assert nkc * ncols <= 4096
            view = wsl[s][:, 0:nkc * ncols].rearrange("p (k c) -> p k c", k=nkc)
            rk = (id(wsrc), k0c, nkc, c0, ncols)
            bfv = wsrc["bf"][k0c * 128:(k0c + nkc) * 128, c0:c0 + ncols].rearrange("(k p) c -> p k c", p=128)
            if first_pass[0]:
                assert rk not in wregion
                src = wsrc["f32"][k0c * 128:(k0c + nkc) * 128, c0:c0 + ncols].rearrange("(k p) c -> p k c", p=128)
                P.add("pool", I("dma_start", out=view, in_=src), writes=[r_ws[s]], key="w%d" % s)
                rr = Res()
                wregion[rk] = rr
                P.add(WQ, I("dma_start", out=bfv, in_=view), reads=[r_ws[s]], writes=[rr], key="wb%d" % s)
            else:
                P.add(WQ, I("dma_start", out=view, in_=bfv), reads=[wregion[rk]], writes=[r_ws[s]], key="w%d" % s)
            return view, r_ws[s]

        def mm(out, lhsT, rhs, start, stop, rd, wr):
            P.add("pe", I("matmul", out, lhsT, rhs, start=start, stop=stop, skip_group_check=True), reads=rd, writes=wr)

        def proj_fm(wap, c0, ncols_total, T, src, r_src, nk, epilogue, colw=256):
            kpl = 4096 // colw
            nkh = (nk + kpl - 1) // kpl
            for cb in range(0, ncols_total, colw):
                cw = min(colw, ncols_total - cb)
                slabs = []
                for kh in range(nkh):
                    k0 = kh * kpl
                    kn = min(kpl, nk - k0)
                    slabs.append((k0, kn) + wload(wap, k0, kn, c0 + cb, cw))
                for jj in range(0, cw, 128):
                    jw = min(128, cw - jj)
                    ps, r_ps = bank()
                    for (k0, kn, view, r_w) in slabs:
                        for kk in range(kn):
                            kc = k0 + kk
                            mm(ps[0:jw, 0:T], view[:, kk, jj:jj + jw], src(kc), kc == 0, kc == nk - 1,
                               [r_w, r_src[kc]], [r_ps])
                    epilogue((cb + jj) // 128, ps, r_ps)

        def proj_tm(wap, c0, ncols, T, TB, NTB, epilogue):
            view, r_w = wload(wap, 0, 16, c0, ncols)
            for tb in range(NTB):
                ps, r_ps = bank()
                for kc in range(16):
                    mm(ps[0:TB, 0:ncols], n[:, kc, tb * TB:(tb + 1) * TB], view[:, kc, :], kc == 0, kc == 15,
                       [r_w, r_n[kc]], [r_ps])
                epilogue(tb, ps, r_ps)

        def proj_tm_gen(wap, c0, ncols, T, TB, NTB, epilogue):
            view, r_w = wload(wap, 0, 16, c0, ncols)
            for tb in range(NTB):
                ps, r_ps = bank()
                for kc in range(16):
                    mm(ps[0:TB, 0:ncols], n[:, kc, tb * TB:(tb + 1) * TB], view[:, kc, :], kc == 0, kc == 15,
                       [r_w, r_n[kc]], [r_ps])
                epilogue(tb, ps, r_ps)
                yield

        def rmsnorm(T, wi, outf):
            ps, r_ps = bank()
            for kc in range(16):
                b = kc % 2
                P.add("act", I("activation", out=sqb[b][:, 0:T], in_=h[:, kc, 0:T], func=AF.Square),
                      reads=[r_h[kc]], writes=[r_sqb[b]])
                mm(ps[:, 0:T], onesb[:], sqb[b][:, 0:T], kc == 0, kc == 15, [r_sqb[b], r_idb], [r_ps])
            P.add("act", I("activation", out=rs[:, 0:T], in_=ps[:, 0:T], func=AF.Sqrt, bias=epsc[:, 0:1], scale=1.0 / D),
                  reads=[r_ps, r_idb], writes=[r_rs])
            P.add("dve", I("reciprocal", rs[:, 0:T], rs[:, 0:T]), reads=[r_rs], writes=[r_rs])
            for kc in range(16):
                o, r_o = outf(kc)
                P.add("dve", I("scalar_tensor_tensor", out=o, in0=h[:, kc, 0:T], scalar=normw[:, wi, kc:kc + 1],
                                                                          in1=rs[:, 0:T], op0=ALU.mult, op1=ALU.mult),
                      reads=[r_h[kc], r_rs, r_small], writes=[r_o])

        def ffn(T, wg, wu, wd):
            act = carve(0, 44 * 512, BF16, "p (j t) -> p j t", j=44)
            r_act = [Res() for _ in range(44)]
            sgt = [carve(45056 + 2048 * i, 512, F32) for i in range(2)]
            r_sgt = [Res(), Res()]
            nsrc = lambda kc: n[:, kc, 0:T]
            pend = {}

            def ep_gate(j, ps, r_ps):
                b = j % 2
                P.add("act", I("activation", out=sgt[b][:, 0:T], in_=ps[:, 0:T], func=AF.Silu),
                      reads=[r_ps], writes=[r_sgt[b]])

            def ep_up(j, ps, r_ps):
                b = j % 2
                P.add("dve", I("tensor_tensor", out=act[:, j, 0:T], in0=sgt[b][:, 0:T], in1=ps[:, 0:T], op=ALU.mult),
                      reads=[r_ps, r_sgt[b]], writes=[r_act[j]])

            for cb in range(0, DFF, 256):
                proj_fm(wg, cb, 256, T, nsrc, r_n, 16, lambda j, ps, r, cb=cb: ep_gate(cb // 128 + j, ps, r))
                proj_fm(wu, cb, 256, T, nsrc, r_n, 16, lambda j, ps, r, cb=cb: ep_up(cb // 128 + j, ps, r))

            def ep_down(j, ps, r_ps):
                P.add("dve", I("scalar_tensor_tensor", out=h[:, j, 0:T], in0=ps[:, 0:T], scalar=0.5, in1=h[:, j, 0:T],
                                                              op0=ALU.mult, op1=ALU.add),
                      reads=[r_ps, r_h[j]], writes=[r_h[j]])

            proj_fm(wd, 0, D, T, lambda kc: act[:, kc, 0:T], r_act, 44, ep_down)

        converted = [False]
        for (t0, T, C, kind) in tiles:
            TB = min(128, T)
            NTB = T // TB
            CPB = TB // C
            first = (t0 == 0) or kind == "s"
            is_last = (kind == "s") or (t0 + T == SEQ)
            gC = [math.exp(C * LG[hd]) for hd in range(8)]
            P.fence()
            for kq in range(4):
                P.add(MQ, I("dma_start", out=h[:, kq * 4:(kq + 1) * 4, 0:T],
                                                         in_=xT[kq * 512:(kq + 1) * 512, t0:t0 + T].rearrange("(k p) t -> p k t", p=128)),
                      writes=r_h[kq * 4:(kq + 1) * 4], key="x%d" % kq)
            P.add(MQ, I("dma_start", out=rope[:, :, 0:T], in_=rope_d[:, :, t0:t0 + T]), writes=[r_rope], key="c2")
            dqo = 0 if kind == "p" else 512
            if first:
                if kind == "p":
                    P.add("dve", I("memset", hist[:], 0.0), writes=r_hist)
                else:
                    P.add(MQ, I("dma_start", out=hist[:], in_=sconv_d.rearrange("p (c i) -> p c i", i=3)),
                          writes=r_hist, key="c4")
            rmsnorm(T, 0, lambda kc: (n[:, kc, 0:T], r_n[kc]))
            ffn(T, W["ffn1_w_gate"], W["ffn1_w_up"], W["ffn1_w_down"])
            if stop == "ffn1":
                continue
            P.fence()
            rmsnorm(T, 1, lambda kc: (n[:, kc, 0:T], r_n[kc]))
            nsrc = lambda kc: n[:, kc, 0:T]
            yAB = carve(0, 32 * 512, BF16, "p (j t) -> p j t", j=32)
            r_yAB = [Res() for _ in range(32)]
            mbuf = carve(32768, 16 * 512, BF16, "p (j t) -> p j t", j=16)
            r_m = [Res() for _ in range(16)]
            TOFF = 49152
            cosT = rope[:, 0, 0:T]
            sinT = rope[:, 1, 0:T]

            o = [TOFF]

            def alloc(nelem, dt, pat=None, **kw):
                nb = nelem * (4 if dt == F32 else 2)
                nb = (nb + 31) // 32 * 32
                v = carve(o[0], nb // (4 if dt == F32 else 2), dt)
                v = v[:, 0:nelem]
                if pat:
                    v = v.rearrange(pat, **kw)
                o[0] += nb
                return v

            qh = alloc(2 * 512, BF16, "p (a t) -> p a t", a=2); r_qh = Res()
            kh = alloc(2 * 512, BF16, "p (a t) -> p a t", a=2); r_kh = Res()
            t1 = alloc(512, F32); t2 = alloc(512, F32); t3 = alloc(512, F32)
            r_t1, r_t2, r_t3 = Res(), Res(), Res()
            vtm = alloc(4 * 512, BF16, "p (b c) -> p b c", b=4); r_vtm = Res()
            ktm = alloc(4 * 256, BF16, "p (b c) -> p b c", b=4); r_ktm = Res()
            sg = alloc(4 * 512, F32, "p (a t) -> p a t", a=4); r_sg = [Res() for _ in range(4)]
            oT = alloc(4 * 512, F32, "p (a t) -> p a t", a=4); r_oT = [Res() for _ in range(4)]
            Sf = alloc(2 * 512, F32, "p (a v) -> p a v", a=2); r_Sf = Res()
            Sb = alloc(2 * 512, BF16, "p (a v) -> p a v", a=2); r_Sb = Res()
            pT = [alloc(512, BF16) for _ in range(2)]; r_pT = [Res(), Res()]
            obf = alloc(512, BF16); r_obf = Res()
            mean = alloc(512, F32); msq = alloc(512, F32); r_mean, r_msq = Res(), Res()
            gtmp = [alloc(512, F32) for _ in range(2)]; r_gtmp = [Res(), Res()]

            for hd in range(8):
                dqkb = dqk2[hd % 2]
                r_dq = r_dqk2[hd % 2]
                P.add(MQ, I("dma_start", out=dqkb[:, :, 0:T], in_=dqk_d[hd, :, :, dqo:dqo + T]), writes=[r_dq], key="dq%d" % (hd % 2))
                dq = dqkb[:, 0, 0:T]
                dk = dqkb[:, 1, 0:T]
                gT = math.exp(T * LG[hd])
                if first and kind == "p":
                    P.add("dve", I("memset", Sf[:], 0.0), writes=[r_Sf])
                    P.add("dve", I("memset", Sb[:], 0.0), writes=[r_Sb])
                else:
                    src = (sret_d if first else ret_o[kind])[hd].rearrange("(a p) v -> p a v", p=128)
                    P.add(MQ, I("dma_start", out=Sf[:], in_=src), reads=[r_ret_o[kind][hd]], writes=[r_Sf], key="sr")
                    P.add("act", I("activation", out=Sb[:], in_=Sf[:], func=AF.Copy), reads=[r_Sf], writes=[r_Sb])
                for (off, dst, r_dst, dtab) in ((OFF_Q, qh, r_qh, dq), (OFF_K, kh, r_kh, dk)):
                    pss = []
                    proj_fm(W["w_in"], off + hd * 256, 256, T, nsrc, r_n, 16, lambda j, ps, r: pss.append((ps, r)))
                    (p1, r1), (p2, r2) = pss
                    dve = lambda f, rd, wr: P.add("dve", f, reads=rd, writes=wr)
                    dve(I("tensor_tensor", out=t1[:, 0:T], in0=p1[:, 0:T], in1=cosT, op=ALU.mult), [r1, r_rope], [r_t1])
                    dve(I("tensor_tensor", out=t2[:, 0:T], in0=p2[:, 0:T], in1=sinT, op=ALU.mult), [r2, r_rope], [r_t2])
                    dve(I("tensor_tensor", out=t3[:, 0:T], in0=t1[:, 0:T], in1=t2[:, 0:T], op=ALU.subtract), [r_t1, r_t2], [r_t3])
                    dve(I("tensor_tensor", out=dst[:, 0, 0:T], in0=t3[:, 0:T], in1=dtab, op=ALU.mult), [r_t3, r_dq], [r_dst])
                    dve(I("tensor_tensor", out=t1[:, 0:T], in0=p1[:, 0:T], in1=sinT, op=ALU.mult), [r1, r_rope], [r_t1])
                    dve(I("tensor_tensor", out=t2[:, 0:T], in0=p2[:, 0:T], in1=cosT, op=ALU.mult), [r2, r_rope], [r_t2])
                    dve(I("tensor_tensor", out=t3[:, 0:T], in0=t1[:, 0:T], in1=t2[:, 0:T], op=ALU.add), [r_t1, r_t2], [r_t3])
                    dve(I("tensor_tensor", out=dst[:, 1, 0:T], in0=t3[:, 0:T], in1=dtab, op=ALU.mult), [r_t3, r_dq], [r_dst])
                for half in range(2):
                    def ep_v(tb, ps, r_ps, half=half):
                        P.add("act", I("activation", out=vtm[0:TB, tb, half * 256:(half + 1) * 256], in_=ps[0:TB, 0:256], func=AF.Copy),
                              reads=[r_ps], writes=[r_vtm])
                    proj_tm(W["w_in"], OFF_V + hd * 512 + half * 256, 256, T, TB, NTB, ep_v)
                for tb in range(NTB):
                    ps, r_ps = bank()
                    for a in range(2):
                        mm(ps[0:TB, a * 128:(a + 1) * 128], kh[:, a, tb * TB:(tb + 1) * TB], identb[:], True, True, [r_kh, r_idb], [r_ps])
                    P.add("act", I("activation", out=ktm[0:TB, tb, :], in_=ps[0:TB, 0:256], func=AF.Copy, scale=gT),
                          reads=[r_ps], writes=[r_ktm])
                for bi in range(NTB):
                    isl = slice(bi * TB, (bi + 1) * TB)
                    ps, r_ps = bank()
                    for bj in range(bi + 1):
                        jsl = slice(bj * TB, (bj + 1) * TB)
                        for a in range(2):
                            mm(ps[0:TB, bj * TB:(bj + 1) * TB], kh[:, a, jsl], qh[:, a, isl], (bj == 0 and a == 0), a == 1, [r_kh, r_qh], [r_ps])
                    pb_ = bi % 2
                    if bi > 0:
                        P.add("act", I("activation", out=pT[pb_][0:TB, 0:bi * TB], in_=ps[0:TB, 0:bi * TB], func=AF.Copy),
                              reads=[r_ps], writes=[r_pT[pb_]])
                    P.add("dve", I("tensor_tensor", out=pT[pb_][0:TB, bi * TB:(bi + 1) * TB], in0=ps[0:TB, bi * TB:(bi + 1) * TB],
                                   in1=causal[0:TB, 0:TB], op=ALU.mult), reads=[r_ps, r_cst], writes=[r_pT[pb_]])
                    po_, r_po = bank()
                    for vc in range(4):
                        for bj in range(bi + 1):
                            mm(po_[:, vc * 128:vc * 128 + TB], vtm[0:TB, bj, vc * 128:(vc + 1) * 128], pT[pb_][0:TB, bj * TB:(bj + 1) * TB],
                               (vc == 0 and bj == 0), False, [r_vtm, r_pT[pb_]], [r_po])
                        for a in range(2):
                            mm(po_[:, vc * 128:vc * 128 + TB], Sb[:, a, vc * 128:(vc + 1) * 128], qh[:, a, isl], False, (a == 1), [r_Sb, r_qh], [r_po])
                    for vc in range(4):
                        P.add("act", I("activation", out=oT[:, vc, isl], in_=po_[:, vc * 128:vc * 128 + TB], func=AF.Copy),
                              reads=[r_po], writes=[r_oT[vc]])
                def ep_g(j, ps, r_ps):
                    P.add("act", I("activation", out=sg[:, j, 0:T], in_=ps[:, 0:T], func=AF.Silu), reads=[r_ps], writes=[r_sg[j]])
                for half in range(2):
                    proj_fm(W["w_in"], OFF_G + hd * 512 + half * 256, 256, T, nsrc, r_n, 16,
                            lambda j, ps, r, half=half: ep_g(half * 2 + j, ps, r))
                for a in range(2):
                    pu, r_pu = bank()
                    for tb in range(NTB):
                        mm(pu[:, :], ktm[0:TB, tb, a * 128:(a + 1) * 128], vtm[0:TB, tb, :], tb == 0, tb == NTB - 1, [r_ktm, r_vtm], [r_pu])
                    P.add("dve", I("scalar_tensor_tensor", out=Sf[:, a, :], in0=Sf[:, a, :], scalar=gT, in1=pu[:, :],
                                   op0=ALU.mult, op1=ALU.add), reads=[r_pu, r_Sf], writes=[r_Sf])
                dst = ret_o[kind][hd].rearrange("(a p) v -> p a v", p=128)
                P.add(MQ, I("dma_start", out=dst, in_=Sf[:]), reads=[r_Sf], writes=[r_ret_o[kind][hd]], key="sr")
                psm, r_psm = bank()
                pss2, r_pss2 = bank()
                for vc in range(4):
                    P.add("act", I("activation", out=obf[:, 0:T], in_=oT[:, vc, 0:T], func=AF.Copy), reads=[r_oT[vc]], writes=[r_obf])
                    mm(psm[:, 0:T], onesb[:], obf[:, 0:T], vc == 0, vc == 3, [r_obf, r_idb], [r_psm])
                    b = vc % 2
                    P.add("act", I("activation", out=sqb[b][:, 0:T], in_=oT[:, vc, 0:T], func=AF.Square), reads=[r_oT[vc]], writes=[r_sqb[b]])
                    mm(pss2[:, 0:T], onesb[:], sqb[b][:, 0:T], vc == 0, vc == 3, [r_sqb[b], r_idb], [r_pss2])
                dve = lambda f, rd, wr: P.add("dve", f, reads=rd, writes=wr)
                dve(I("tensor_scalar", mean[:, 0:T], psm[:, 0:T], 1.0 / 512, None, op0=ALU.mult), [r_psm], [r_mean])
                dve(I("tensor_tensor", out=msq[:, 0:T], in0=mean[:, 0:T], in1=mean[:, 0:T], op=ALU.mult), [r_mean], [r_msq])
                dve(I("scalar_tensor_tensor", out=msq[:, 0:T], in0=pss2[:, 0:T], scalar=1.0 / 512, in1=msq[:, 0:T],
                                                                op0=ALU.mult, op1=ALU.subtract), [r_pss2, r_msq], [r_msq])
                P.add("act", I("activation", out=msq[:, 0:T], in_=msq[:, 0:T], func=AF.Sqrt, bias=epsc[:, 0:1], scale=1.0), reads=[r_msq, r_idb], writes=[r_msq])
                dve(I("reciprocal", msq[:, 0:T], msq[:, 0:T]), [r_msq], [r_msq])
                for vc in range(4):
                    j = hd * 4 + vc
                    dve(I("tensor_tensor", out=oT[:, vc, 0:T], in0=oT[:, vc, 0:T], in1=mean[:, 0:T], op=ALU.subtract),
                        [r_oT[vc], r_mean], [r_oT[vc]])
                    dve(I("tensor_tensor", out=oT[:, vc, 0:T], in0=oT[:, vc, 0:T], in1=msq[:, 0:T], op=ALU.mult),
                        [r_oT[vc], r_msq], [r_oT[vc]])
                    dve(I("scalar_tensor_tensor", out=yAB[:, j, 0:T], in0=oT[:, vc, 0:T], scalar=retw[:, j:j + 1], in1=sg[:, vc, 0:T],
                                                                     op0=ALU.mult, op1=ALU.mult),
                        [r_oT[vc], r_sg[vc], r_small], [r_yAB[j]])

            def gate_and_proj(goff, wout, merge):
                for jp in range(0, 16, 2):
                    gl = []
                    proj_fm(W["w_in"], goff + jp * 128, 256, T, nsrc, r_n, 16, lambda j, ps, r: gl.append((ps, r)))
                    for jj, (ps, r_ps) in enumerate(gl):
                        j = jp + jj
                        b = j % 2
                        gcol = (goff - OFF_GA) // 128 + j
                        P.add("act", I("activation", out=gtmp[b][:, 0:T], in_=ps[:, 0:T], func=AF.Sigmoid,
                                                                                bias=bgate[:, gcol:gcol + 1], scale=1.0),
                              reads=[r_ps, r_small], writes=[r_gtmp[b]])
                    proj_fm(wout, jp * 128, 256, T, lambda kc: yAB[:, kc, 0:T], r_yAB, 32, lambda j, ps, r, jp=jp: merge(jp + j, ps, r))

            def merge_a(j, ps, r_ps):
                b = j % 2
                P.add("dve", I("tensor_tensor", out=mbuf[:, j, 0:T], in0=gtmp[b][:, 0:T], in1=ps[:, 0:T], op=ALU.mult),
                      reads=[r_ps, r_gtmp[b]], writes=[r_m[j]])

            gate_and_proj(OFF_GA, W["w_out_ret"], merge_a)
            if stop == "ret":
                continue
            P.fence()

            o[0] = TOFF
            cin = alloc(520, F32); r_cin = Res()
            acc = alloc(512, F32); r_acc = Res()
            xsT = [alloc(512, BF16) for _ in range(2)]; r_xsT = [Res(), Res()]
            xdt = alloc(4 * 512, BF16, "p (b c) -> p b c", b=4); r_xdt = Res()
            xw = alloc(4 * 512, BF16, "p (b c) -> p b c", b=4); r_xw = Res()
            sz = alloc(4 * 512, BF16, "p (b c) -> p b c", b=4); r_sz = Res()
            BT = alloc(512, BF16); CT = alloc(512, BF16); r_BT, r_CT = Res(), Res()
            Btm = alloc(4 * 128, BF16, "p (b c) -> p b c", b=4); r_Btm = Res()
            Hf = alloc(512, F32); r_Hf = Res()
            Hb2 = [alloc(512, BF16) for _ in range(2)]; r_Hb2 = [Res(), Res()]
            hbi = [0]
            Rb = alloc(512, F32); r_Rb = Res()
            Eb = alloc(512, F32); r_Eb = Res()
            MT2 = [alloc(8 * 128, BF16, "p (h i) -> p h i", h=8) for _ in range(2)]; r_MT2 = [Res(), Res()]
            cbm = alloc(64, F32); r_cbm = Res()
            y1 = alloc(512, F32); r_y1 = Res()
            ytm = alloc(4 * 512, F32, "p (b c) -> p b c", b=4); r_ytm = Res()
            dts = alloc(8 * 256, F32, "p (q b c) -> p q b c", q=8, b=4)
            r_dts = [Res() for _ in range(8)]
            DT, DTA, CUM, CL, DEND, ECUM, W2 = range(7)

            ssq = alloc(8, F32); r_ssq = Res()
            junk = y1; r_junk = r_y1
            dve = lambda f, rd, wr: P.add("dve", f, reads=rd, writes=wr)
            actf = lambda f, rd, wr: P.add("act", f, reads=rd, writes=wr)
            selC = sel64 if C == 64 else sel16

            def ep_dt(tb, ps, r_ps):
                dve(I("tensor_tensor", out=dts[0:TB, DT, tb, :], in0=ps[0:TB, 0:64], in1=dtb[0:TB, :], op=ALU.add), [r_ps, r_small], [r_dts[DT]])
            proj_tm(W["w_in"], OFF_DT, 64, T, TB, NTB, ep_dt)
            actf(I("activation", out=dts[0:TB, DT, 0:NTB, :], in_=dts[0:TB, DT, 0:NTB, :], func=AF.Exp), [r_dts[DT]], [r_dts[DT]])
            actf(I("activation", out=dts[0:TB, DT, 0:NTB, :], in_=dts[0:TB, DT, 0:NTB, :], func=AF.Ln, bias=epsc[0:TB, 1:2], scale=1.0), [r_dts[DT]], [r_dts[DT]])
            dve(I("tensor_tensor", out=dts[0:TB, DTA, 0:NTB, :], in0=dts[0:TB, DT, 0:NTB, :],
                                          in1=abc[0:TB, :].unsqueeze(1).to_broadcast([TB, NTB, 64]), op=ALU.mult), [r_dts[DT], r_abc], [r_dts[DTA]])
            for tb in range(NTB):
                pc, r_pc = bank()
                mm(pc[0:TB, 0:64], bdmask[0:TB, 0:TB], dts[0:TB, DTA, tb, :], True, True, [r_cst, r_dts[DTA]], [r_pc])
                mm(pc[0:TB, 64:128], bdones[0:TB, 0:TB], dts[0:TB, DTA, tb, :], True, True, [r_cst, r_dts[DTA]], [r_pc])
                actf(I("activation", out=dts[0:TB, CUM, tb, :], in_=pc[0:TB, 0:64], func=AF.Copy), [r_pc], [r_dts[CUM]])
                actf(I("activation", out=dts[0:TB, ECUM, tb, :], in_=pc[0:TB, 0:64], func=AF.Exp), [r_pc], [r_dts[ECUM]])
                dve(I("tensor_tensor", out=dts[0:TB, DEND, tb, :], in0=pc[0:TB, 64:128], in1=dts[0:TB, CUM, tb, :], op=ALU.subtract),
                    [r_pc, r_dts[CUM]], [r_dts[DEND]])
            actf(I("activation", out=dts[0:TB, DEND, 0:NTB, :], in_=dts[0:TB, DEND, 0:NTB, :], func=AF.Exp), [r_dts[DEND]], [r_dts[DEND]])
            dve(I("tensor_tensor", out=dts[0:TB, W2, 0:NTB, :], in0=dts[0:TB, DEND, 0:NTB, :], in1=dts[0:TB, DT, 0:NTB, :], op=ALU.mult),
                [r_dts[DEND], r_dts[DT]], [r_dts[W2]])

            def conv_chunk(cidx, ps, r_ps, outap, r_out):
                actf(I("activation", out=cin[:, 3:3 + T], in_=ps[:, 0:T], func=AF.Copy), [r_ps], [r_cin])
                dve(I("tensor_copy", cin[:, 0:3], hist[:, cidx, :]), [r_hist[cidx], r_cin], [r_cin])
                dve(I("tensor_scalar", acc[:, 0:T], cin[:, 0:T], convw[:, cidx, 0:1], None, op0=ALU.mult), [r_cin, r_small], [r_acc])
                for i in range(1, 4):
                    dve(I("scalar_tensor_tensor", out=acc[:, 0:T], in0=cin[:, i:i + T], scalar=convw[:, cidx, i:i + 1], in1=acc[:, 0:T],
                                                              op0=ALU.mult, op1=ALU.add), [r_cin, r_acc, r_small], [r_acc])
                dve(I("tensor_copy", hist[:, cidx, :], cin[:, T:T + 3]), [r_cin], [r_hist[cidx]])
                actf(I("activation", out=outap, in_=acc[:, 0:T], func=AF.Silu, bias=convb[:, cidx:cidx + 1], scale=1.0), [r_acc, r_small], [r_out])

            for g in range(8):
                if first and kind == "p":
                    dve(I("memset", Hf[:], 0.0), [], [r_Hf])
                    dve(I("memset", Hb2[hbi[0] % 2][:], 0.0), [], [r_Hb2[hbi[0] % 2]])
                else:
                    src = (sssm_d if first else ssm_o[kind])[:, g * 512:(g + 1) * 512]
                    P.add(MQ, I("dma_start", out=Hf[:], in_=src), reads=[r_ssm_o[kind][g]], writes=[r_Hf], key="ss")
                    actf(I("activation", out=Hb2[hbi[0] % 2][:], in_=Hf[:], func=AF.Copy), [r_Hf], [r_Hb2[hbi[0] % 2]])
                def zgen_f(g=g):
                    for half in range(2):
                        def ep_z(tb, ps, r_ps, half=half):
                            actf(I("activation", out=sz[0:TB, tb, half * 256:(half + 1) * 256], in_=ps[0:TB, 0:256], func=AF.Copy), [r_ps], [r_sz])
                        yield from proj_tm_gen(W["w_in"], OFF_Z + g * 512 + half * 256, 256, T, TB, NTB, ep_z)
                zgen = zgen_f()
                if DBG_NOILV:
                    for _ in zgen:
                        pass
                bl = []
                proj_fm(W["w_in"], OFF_X + 4096 + g * 128, 128, T, nsrc, r_n, 16, lambda j, ps, r: bl.append((ps, r)), colw=128)
                conv_chunk(32 + g, bl[0][0], bl[0][1], BT[:, 0:T], r_BT)
                cl_ = []
                proj_fm(W["w_in"], OFF_X + 5120 + g * 128, 128, T, nsrc, r_n, 16, lambda j, ps, r: cl_.append((ps, r)), colw=128)
                conv_chunk(40 + g, cl_[0][0], cl_[0][1], CT[:, 0:T], r_CT)
                for tb in range(NTB):
                    ps, r_ps = bank()
                    mm(ps[0:TB, 0:128], BT[:, tb * TB:(tb + 1) * TB], identb[:], True, True, [r_BT, r_idb], [r_ps])
                    actf(I("activation", out=Btm[0:TB, tb, :], in_=ps[0:TB, 0:128], func=AF.Copy), [r_ps], [r_Btm])
                xb = [bank() for _ in range(NTB)]
                for xpair in range(2):
                    xl = []
                    proj_fm(W["w_in"], OFF_X + g * 512 + xpair * 256, 256, T, nsrc, r_n, 16, lambda j, ps, r: xl.append((ps, r)))
                    for xi in range(2):
                        xc = xpair * 2 + xi
                        b = xc % 2
                        conv_chunk(g * 4 + xc, xl[xi][0], xl[xi][1], xsT[b][:, 0:T], r_xsT[b])
                        for tb in range(NTB):
                            mm(xb[tb][0][0:TB, xc * 128:(xc + 1) * 128], xsT[b][:, tb * TB:(tb + 1) * TB], identb[:], True, True,
                               [r_xsT[b], r_idb], [xb[tb][1]])
                for tb in range(NTB):
                    xps, r_xps = xb[tb]
                    hs = slice(g * 8, (g + 1) * 8)
                    bc = lambda q: dts[0:TB, q, tb, hs].unsqueeze(2).to_broadcast([TB, 8, 64])
                    x3 = xps[0:TB, :].rearrange("p (h q) -> p h q", h=8)
                    dve(I("tensor_tensor", out=xdt[0:TB, tb, :].rearrange("p (h q) -> p h q", h=8), in0=x3, in1=bc(DT), op=ALU.mult),
                        [r_xps, r_dts[DT]], [r_xdt])
                    dve(I("tensor_tensor", out=xw[0:TB, tb, :].rearrange("p (h q) -> p h q", h=8), in0=x3, in1=bc(W2), op=ALU.mult),
                        [r_xps, r_dts[W2]], [r_xw])
                    dve(I("tensor_tensor", out=ytm[0:TB, tb, :].rearrange("p (h q) -> p h q", h=8), in0=x3,
                                                                in1=dskip[0:TB, hs].unsqueeze(2).to_broadcast([TB, 8, 64]), op=ALU.mult),
                        [r_xps, r_small], [r_ytm])
                if g == 0:
                    dve(I("memset", MT2[0][:], 0.0), [], [r_MT2[0]])
                    dve(I("memset", MT2[1][:], 0.0), [], [r_MT2[1]])
                hs = slice(g * 8, (g + 1) * 8)

                def stage_a(tb):
                    tsl = slice(tb * TB, (tb + 1) * TB)
                    dve(I("tensor_tensor", out=Rb[0:TB, 0:8 * C].rearrange("p (h i) -> p h i", h=8),
                          in0=dts[0:TB, DTA, tb, hs].unsqueeze(2).to_broadcast([TB, 8, C]),
                          in1=triloc[0:TB, 0:C].unsqueeze(1).to_broadcast([TB, 8, C]), op=ALU.mult),
                        [r_dts[DTA], r_cst], [r_Rb])
                    ps, r_ps = bank()
                    mm(ps[0:TB, 0:TB], BT[:, tsl], CT[:, tsl], True, True, [r_BT, r_CT], [r_ps])
                    pseg, r_pseg = bank()
                    mm(pseg[0:TB, 0:8 * C], bdU[0:TB, 0:TB], Rb[0:TB, 0:8 * C], True, True, [r_cst, r_Rb], [r_pseg])
                    for c in range(CPB):
                        p0 = c * C
                        dve(I("tensor_tensor", out=cbm[p0:p0 + C, 0:C], in0=ps[p0:p0 + C, p0:p0 + C], in1=triloc[p0:p0 + C, 0:C], op=ALU.mult),
                            [r_ps, r_cst], [r_cbm])
                    actf(I("activation", out=Eb[0:TB, 0:8 * C], in_=pseg[0:TB, 0:8 * C], func=AF.Exp), [r_pseg], [r_Eb])

                def stage_b(tb):
                    MT = MT2[tb % 2]
                    r_MT = r_MT2[tb % 2]
                    for c in range(CPB):
                        p0 = c * C
                        dve(I("tensor_tensor", out=MT[p0:p0 + C, :, p0:p0 + C],
                              in0=Eb[p0:p0 + C, 0:8 * C].rearrange("p (h i) -> p h i", h=8),
                              in1=cbm[p0:p0 + C, 0:C].unsqueeze(1).to_broadcast([C, 8, C]), op=ALU.mult),
                            [r_Eb, r_cbm], [r_MT])
                    py, r_py = bank()
                    for hh in range(8):
                        mm(py[0:TB, hh * 64:(hh + 1) * 64], MT[0:TB, hh, 0:TB], xdt[0:TB, tb, hh * 64:(hh + 1) * 64], True, True, [r_MT, r_xdt], [r_py])
                    dve(I("tensor_tensor", out=ytm[0:TB, tb, :], in0=ytm[0:TB, tb, :], in1=py[0:TB, :], op=ALU.add),
                        [r_py, r_ytm], [r_ytm])

                def chain_step(tb, c):
                    tsl = slice(tb * TB, (tb + 1) * TB)
                    p0 = c * C
                    csl = slice(p0, p0 + C)
                    ph, r_ph = bank()
                    mm(ph[:, :], Btm[csl, tb, :], xw[csl, tb, :], True, True, [r_Btm, r_xw], [r_ph])
                    pcl, r_pcl = bank()
                    mm(pcl[:, 0:64], selC[csl, :], dts[csl, ECUM, tb, :], True, True, [r_cst, r_dts[ECUM]], [r_pcl])
                    hb_i = hbi[0] % 2
                    pch, r_pch = bank()
                    mm(pch[0:TB, :], CT[:, tsl], Hb2[hb_i][:], True, True, [r_CT, r_Hb2[hb_i]], [r_pch])
                    dve(I("tensor_tensor", out=Hf[:].rearrange("p (h q) -> p h q", h=8), in0=Hf[:].rearrange("p (h q) -> p h q", h=8),
                          in1=pcl[:, hs].unsqueeze(2).to_broadcast([128, 8, 64]), op=ALU.mult), [r_pcl, r_Hf], [r_Hf])
                    dve(I("tensor_tensor", out=Hf[:], in0=Hf[:], in1=ph[:, :], op=ALU.add), [r_ph, r_Hf], [r_Hf])
                    hbi[0] += 1
                    actf(I("activation", out=Hb2[hbi[0] % 2][:], in_=Hf[:], func=AF.Copy), [r_Hf], [r_Hb2[hbi[0] % 2]])
                    dve(I("tensor_tensor", out=y1[csl, :].rearrange("p (h q) -> p h q", h=8),
                          in0=pch[csl, :].rearrange("p (h q) -> p h q", h=8),
                          in1=dts[csl, ECUM, tb, hs].unsqueeze(2).to_broadcast([C, 8, 64]), op=ALU.mult),
                        [r_pch, r_dts[ECUM]], [r_y1])
                    dve(I("tensor_tensor", out=ytm[csl, tb, :], in0=ytm[csl, tb, :], in1=y1[csl, :], op=ALU.add),
                        [r_y1, r_ytm], [r_ytm])
                    next(zgen, None)

                stage_a(0)
                stage_b(0)
                for tb in range(NTB):
                    nxt = tb + 1 < NTB
                    if nxt:
                        stage_a(tb + 1)
                    chain_step(tb, 0)
                    if nxt:
                        stage_b(tb + 1)
                    for c in range(1, CPB):
                        chain_step(tb, c)
                for _ in zgen:
                    pass
                dst = ssm_o[kind][:, g * 512:(g + 1) * 512]
                P.add(MQ, I("dma_start", out=dst, in_=Hf[:]), reads=[r_Hf], writes=[r_ssm_o[kind][g]], key="ss")
                for tb in range(NTB):
                    actf(I("activation", out=sz[0:TB, tb, :], in_=sz[0:TB, tb, :], func=AF.Silu), [r_sz], [r_sz])
                for tb in range(NTB):
                    dve(I("tensor_tensor", out=ytm[0:TB, tb, :], in0=ytm[0:TB, tb, :], in1=sz[0:TB, tb, :], op=ALU.mult),
                        [r_ytm, r_sz], [r_ytm])
                    actf(I("activation", out=junk[0:TB, :], in_=ytm[0:TB, tb, :], func=AF.Square, accum_out=ssq[0:TB, tb:tb + 1]),
                         [r_ytm], [r_junk, r_ssq])
                actf(I("activation", out=ssq[0:TB, 0:NTB], in_=ssq[0:TB, 0:NTB], func=AF.Sqrt, bias=epsc[0:TB, 0:1], scale=1.0 / 512), [r_ssq, r_idb], [r_ssq])
                dve(I("reciprocal", ssq[0:TB, 0:NTB], ssq[0:TB, 0:NTB]), [r_ssq], [r_ssq])
                for tb in range(NTB):
                    dve(I("tensor_scalar", sz[0:TB, tb, :], ytm[0:TB, tb, :], ssq[0:TB, tb:tb + 1], None, op0=ALU.mult),
                        [r_ytm, r_ssq], [r_sz])
                for fc in range(4):
                    ps, r_ps = bank()
                    for tb in range(NTB):
                        mm(ps[:, tb * TB:(tb + 1) * TB], sz[0:TB, tb, fc * 128:(fc + 1) * 128], identb[0:TB, 0:TB], True, True, [r_sz, r_idb], [r_ps])
                    j = g * 4 + fc
                    actf(I("activation", out=yAB[:, j, 0:T], in_=ps[:, 0:T], func=AF.Copy, scale=ssmw[:, j:j + 1]),
                         [r_ps, r_small], [r_yAB[j]])

            def merge_b(j, ps, r_ps):
                b = j % 2
                P.add("dve", I("tensor_tensor", out=gtmp[b][:, 0:T], in0=gtmp[b][:, 0:T], in1=ps[:, 0:T], op=ALU.mult),
                      reads=[r_ps, r_gtmp[b]], writes=[r_gtmp[b]])
                P.add("dve", I("tensor_tensor", out=mbuf[:, j, 0:T], in0=gtmp[b][:, 0:T], in1=mbuf[:, j, 0:T], op=ALU.add),
                      reads=[r_gtmp[b], r_m[j]], writes=[r_m[j]])

            gate_and_proj(OFF_GB, W["w_out_ssm"], merge_b)

            if is_last:
                P.add(MQ, I("dma_start", out=conv_o[kind].rearrange("p (c i) -> p c i", i=3), in_=hist[:]),
                      reads=r_hist, key="st")

            def ep_wo(j, ps, r_ps):
                P.add("dve", I("tensor_tensor", out=h[:, j, 0:T], in0=h[:, j, 0:T], in1=ps[:, 0:T], op=ALU.add),
                      reads=[r_ps, r_h[j]], writes=[r_h[j]])
            proj_fm(W["w_out"], 0, D, T, lambda kc: mbuf[:, kc, 0:T], r_m, 16, ep_wo)
            if stop == "mix":
                continue
            P.fence()
            rmsnorm(T, 2, lambda kc: (n[:, kc, 0:T], r_n[kc]))
            ffn(T, W["ffn2_w_gate"], W["ffn2_w_up"], W["ffn2_w_down"])
            P.fence()
            first_pass[0] = False
            yo = carve(0, 16 * 512, F32, "p (k t) -> p k t", k=16)
            r_yo = [Res() for _ in range(16)]
            rmsnorm(T, 3, lambda kc: (yo[:, kc, 0:T], r_yo[kc]))
            for kq in range(4):
                P.add(MQ, I("dma_start", out=yT[kq * 512:(kq + 1) * 512, t0:t0 + T].rearrange("(k p) t -> p k t", p=128),
                                                         in_=yo[:, kq * 4:(kq + 1) * 4, 0:T]),
                      reads=r_yo[kq * 4:(kq + 1) * 4], key="y")

        if stop is not None:
            for (t0, T, C, kind) in tiles:
                pass
            P.fence()
            t0, T, C, kind = tiles[-1]
            P.add(MQ, I("dma_start", out=yT[:, t0:t0 + T].rearrange("(k p) t -> p k t", p=128), in_=h[:, :, 0:T]),
                  reads=r_h, key="y")
        P.emit_all(nc)
    return nc


def _consts():
    cst = np.zeros((128, 1088), np.float32)
    j = np.arange(128)[:, None]
    i = np.arange(128)[None, :]
    same = (j // 64) == (i // 64)
    cst[:, 0:128] = (same & (i >= j))
    cst[:, 128:256] = (same & (j > i))
    cst[:, 256:384] = same
    cst[:, 384:448] = ((np.arange(128)[:, None] % 64) <= np.arange(64)[None, :])
    cst[:, 448:576] = np.eye(128)
    cst[:, 576:704] = 1.0
    cst[:, 704:832] = (i >= j)
    cst[63, 832:960] = 1.0
    cst[127, 832:960] = 1.0
    cst[15, 960:1088] = 1.0
    pos = np.concatenate([np.arange(SEQ, dtype=np.float32), np.arange(DEC, dtype=np.float32) + np.float32(PAST)])
    inv = (np.float32(10000.0) ** (-np.arange(128, dtype=np.float32) / np.float32(128))).astype(np.float32)
    ang = (pos[None, :] * inv[:, None]).astype(np.float32)
    rope = np.stack([np.cos(ang), np.sin(ang)], axis=1).astype(np.float32)
    dqk = np.zeros((8, 128, 2, 528), np.float32)
    for hd in range(8):
        for (o_, T_) in ((0, 512), (512, 16)):
            idx = np.arange(T_, dtype=np.float64) + 1.0
            dqk[hd, :, 0, o_:o_ + T_] = np.exp(idx * LG[hd])[None, :]
            dqk[hd, :, 1, o_:o_ + T_] = (np.exp(-idx * LG[hd]) / 16.0)[None, :]
    return cst, rope, dqk


def _fm(v, nch):
    return np.ascontiguousarray(np.asarray(v, np.float32).reshape(nch, 128).T)


TILES = [(i * 512, 512, 64, "p") for i in range(8)] + [(SEQ, DEC, DEC, "s")]
_CACHE = {}


def make_in_maps(inp, cores):
    cst, rope, dqk = _consts()
    small = np.zeros((128, 640), np.float32)
    small[:, 0:16] = _fm(inp["norm_ffn1"], 16)
    small[:, 16:32] = _fm(inp["norm_mix"], 16)
    small[:, 32:48] = _fm(inp["norm_ffn2"], 16)
    small[:, 48:64] = _fm(inp["norm_final"], 16)
    small[:, 64:96] = _fm(inp["ret_norm_w"], 32)
    small[:, 96:128] = _fm(inp["ssm_norm_w"], 32)
    small[:, 128:160] = _fm(inp["b_gate"], 32)
    cw = np.asarray(inp["conv_w"], np.float32)
    small[:, 160:352] = cw.reshape(4, 48, 128).transpose(2, 1, 0).reshape(128, 192)
    small[:, 352:400] = _fm(inp["conv_b"], 48)
    small[:, 400:464] = np.asarray(inp["dt_bias"], np.float32)[None, :]
    small[:, 464:528] = np.asarray(inp["a_log"], np.float32)[None, :]
    small[:, 528:592] = np.asarray(inp["d_skip"], np.float32)[None, :]
    wnames = ["ffn1_w_gate", "ffn1_w_up", "ffn1_w_down", "w_in", "w_out_ret", "w_out_ssm", "w_out",
              "ffn2_w_gate", "ffn2_w_up", "ffn2_w_down"]
    shared = {k: np.ascontiguousarray(np.asarray(inp[k], np.float32)) for k in wnames}
    shared.update(small=small, cst=cst, rope=rope, dqk=dqk)
    maps = []
    for b in cores:
        xa = np.concatenate([np.asarray(inp["x_prompt"][b], np.float32), np.asarray(inp["x_sample"][b], np.float32)], axis=0)
        m = dict(shared)
        m["xT"] = np.ascontiguousarray(xa.T)
        m["sret"] = np.ascontiguousarray(np.asarray(inp["state_ret"][b], np.float32))
        m["sssmT"] = np.ascontiguousarray(np.asarray(inp["state_ssm"][b], np.float32).reshape(4096, 128).T)
        sc = np.asarray(inp["state_conv"][b], np.float32)
        m["sconv"] = np.ascontiguousarray(sc.reshape(3, 48, 128).transpose(2, 1, 0).reshape(128, 144))
        maps.append(m)
    return maps


def assemble(results, nb):
    y = np.stack([r["yT"].T for r in results])
    y_prompt = np.ascontiguousarray(y[:, :SEQ])
    y_sample = np.ascontiguousarray(y[:, SEQ:])

    def ssm(k):
        return np.stack([r[k].T.reshape(64, 64, 128) for r in results])

    def conv(k):
        return np.stack([r[k].reshape(128, 48, 3).transpose(2, 1, 0).reshape(3, 6144) for r in results])

    return (y_prompt, y_sample,
            np.stack([r["ret_p"] for r in results]), ssm("ssm_pT"), conv("conv_p"),
            np.stack([r["ret_s"] for r in results]), ssm("ssm_sT"), conv("conv_s"))


def kernel(**inputs):
    if "nc" not in _CACHE:
        _CACHE["nc"] = build(TILES)
    nc = _CACHE["nc"]
    maps = make_in_maps(inputs, list(range(8)))
    res = run_bass_kernel_spmd(nc, maps, core_ids=list(range(8)))
    return assemble(res.results, 8)
```

```python
import math
from contextlib import ExitStack
import numpy as np
import concourse.bass as bass
import concourse.mybir as mybir
from concourse.bass_utils import run_bass_kernel_spmd

F32 = mybir.dt.float32
BF16 = mybir.dt.bfloat16
AF = mybir.ActivationFunctionType
ALU = mybir.AluOpType

D = 2048
DFF = 5632
NKC = 16
SEQ = 4096
DEC = 16
PAST = 1024
NTOK = SEQ + DEC
EPS = 1e-6
OFF_Q, OFF_K, OFF_V, OFF_G, OFF_Z, OFF_X, OFF_DT, OFF_GA, OFF_GB = 0, 2048, 4096, 8192, 12288, 16384, 22528, 22592, 24640
LG = [math.log1p(-2.0 ** (-5.0 - h)) for h in range(8)]
DBG_NOILV = False
DBG_ONEHB = False
DBG_OLDECL = True
WQ = "sp"
MQ = "pool"
WORDER = ["ffn1_w_gate", "ffn1_w_up", "ffn1_w_down", "w_in", "w_out_ret", "w_out_ssm", "w_out",
          "ffn2_w_gate", "ffn2_w_up", "ffn2_w_down"]


class Res:
    __slots__ = ("name", "last_w", "readers")

    def __init__(self, name=""):
        self.name = name
        self.last_w = None
        self.readers = {}


class Op:
    __slots__ = ("eng", "emit", "deps", "signal", "count", "key", "pos", "is_dma")

    def __init__(self, eng, emit, key):
        self.eng = eng
        self.emit = emit
        self.deps = []
        self.signal = False
        self.count = 0
        self.key = key
        self.is_dma = key is not None
        self.pos = 0


class Prog:
    ENGS = ("pe", "act", "dve", "pool", "sp")

    def __init__(self):
        self.ops = []
        self.by_eng = {e: [] for e in self.ENGS}
        self.seen = {e: {} for e in self.ENGS}
        self.keycnt = {}
        self.cpos = {e: 0 for e in self.ENGS}
        self.last_compute = {}
        self.dma_last = {}

    def _k(self, p):
        return ("k", p.key) if p.is_dma else ("e", p.eng)

    def add(self, eng, emit, reads=(), writes=(), key=None, extra_deps=()):
        op = Op(eng, emit, key)
        idx = len(self.ops)
        self.ops.append(op)
        self.by_eng[eng].append(op)
        if key is not None:
            self.keycnt[key] = self.keycnt.get(key, 0) + 1
            op.pos = self.keycnt[key]
            self.dma_last[key] = idx
        elif emit is not None:
            self.cpos[eng] += 1
            op.pos = self.cpos[eng]
            self.last_compute[eng] = idx
        best = {}

        def cand(d):
            p = self.ops[d]
            k = self._k(p)
            if k not in best or self.ops[best[k]].pos < p.pos:
                best[k] = d

        for r in reads:
            if r.last_w is not None:
                cand(r.last_w)
        for r in writes:
            if r.last_w is not None:
                cand(r.last_w)
            for d in r.readers.values():
                cand(d)
        for d in extra_deps:
            cand(d)
        seen = self.seen[eng]
        for k, d in best.items():
            p = self.ops[d]
            if d == idx:
                continue
            if (not p.is_dma) and p.eng == "pe" and eng == "pe" and not op.is_dma and emit is not None:
                continue
            if seen.get(k, 0) >= p.pos:
                continue
            seen[k] = p.pos
            op.deps.append(d)
            p.signal = True
        if emit is not None:
            k = self._k(op)
            for r in reads:
                r.readers[k] = idx
            for r in writes:
                r.last_w = idx
                r.readers = {}
        return op

    def fence(self, engs=("pe", "act", "dve", "pool")):
        deps = [self.last_compute[e] for e in engs if e in self.last_compute]
        deps += [d for k, d in self.dma_last.items() if not (str(k).startswith("w") or str(k).startswith("cv"))]
        for e in engs:
            self.add(e, None, extra_deps=deps)

    def emit_all(self, nc, final_wait_eng="pool"):
        with ExitStack() as st:
            esem = {e: st.enter_context(nc.semaphore("s_" + e)) for e in self.ENGS}
            ksem = {k: st.enter_context(nc.semaphore("k_" + str(k))) for k in self.keycnt}
            for e in self.ENGS:
                c = 0
                for op in self.by_eng[e]:
                    if op.is_dma:
                        op.count = 16 * op.pos
                    elif op.signal:
                        c += 1
                        op.count = c
            block = st.enter_context(nc.Block())

            def run(e, eng):
                waited = {}
                for op in self.by_eng[e]:
                    for d in op.deps:
                        p = self.ops[d]
                        s = ksem[p.key] if p.is_dma else esem[p.eng]
                        wk = self._k(p)
                        if waited.get(wk, 0) >= p.count:
                            continue
                        waited[wk] = p.count
                        eng.wait_ge(s, p.count)
                    if op.emit is None:
                        continue
                    ins = op.emit(eng)
                    if op.is_dma:
                        ins.then_inc(ksem[op.key], 16)
                    elif op.signal:
                        ins.then_inc(esem[e], 1)
                if e == final_wait_eng:
                    for k, n in self.keycnt.items():
                        eng.wait_ge(ksem[k], 16 * n)

            block.tensor(lambda eng: run("pe", eng))
            block.scalar(lambda eng: run("act", eng))
            block.vector(lambda eng: run("dve", eng))
            block.gpsimd(lambda eng: run("pool", eng))
            block.sync(lambda eng: run("sp", eng))


def I(method, *a, **kw):
    return lambda e: getattr(e, method)(*a, **kw)


def build(tiles, stop=None):
    nc = bass.Bass("TRN2", target_bir_lowering=False)
    P = Prog()

    def din(name, shape, dt=F32):
        return nc.dram_tensor(name, list(shape), dt, kind="ExternalInput").ap()

    def dout(name, shape, dt=F32):
        return nc.dram_tensor(name, list(shape), dt, kind="ExternalOutput").ap()

    xT = din("xT", [D, NTOK])
    W = {}
    for nm, shp in (("ffn1_w_gate", [D, DFF]), ("ffn1_w_up", [D, DFF]), ("ffn1_w_down", [DFF, D]),
                    ("w_in", [D, 26688]), ("w_out_ret", [4096, D]), ("w_out_ssm", [4096, D]), ("w_out", [D, D]),
                    ("ffn2_w_gate", [D, DFF]), ("ffn2_w_up", [D, DFF]), ("ffn2_w_down", [DFF, D])):
        W[nm] = {"f32": din(nm, shp), "shape": shp, "res": [],
                 "bf": nc.dram_tensor(nm + "_bf", list(shp), BF16, kind="Internal").ap()}
    small_d = din("small", [128, 640])
    cst_d = din("cst", [128, 1088])
    rope_d = din("rope", [128, 2, NTOK])
    dqk_d = din("dqk", [8, 128, 2, 528])
    sret_d = din("sret", [8, 256, 512])
    sssm_d = din("sssmT", [128, 4096])
    sconv_d = din("sconv", [128, 144])
    yT = dout("yT", [D, NTOK])
    ret_o = {"p": dout("ret_p", [8, 256, 512]), "s": dout("ret_s", [8, 256, 512])}
    ssm_o = {"p": dout("ssm_pT", [128, 4096]), "s": dout("ssm_sT", [128, 4096])}
    conv_o = {"p": dout("conv_p", [128, 144]), "s": dout("conv_s", [128, 144])}
    r_ret_o = {k: [Res() for _ in range(8)] for k in "ps"}
    r_ssm_o = {k: [Res() for _ in range(8)] for k in "ps"}

    with ExitStack() as st:
        def sb(name, shape, dt):
            return st.enter_context(nc.sbuf_tensor("sb_" + name, list(shape), dt))

        h = sb("h", [128, 16, 512], F32)
        r_h = [Res("h%d" % i) for i in range(16)]
        n = sb("n", [128, 16, 512], BF16)
        r_n = [Res("n%d" % i) for i in range(16)]
        NS = 4
        wsl = [sb("ws%d" % i, [128, 4096], BF16) for i in range(NS)]
        r_ws = [Res("ws%d" % i) for i in range(NS)]
        small = sb("small", [128, 640], F32)
        r_small = Res("small")
        cst = sb("cst", [128, 1088], F32)
        r_cst = Res("cst")
        identb = sb("identb", [128, 128], BF16)
        onesb = sb("onesb", [128, 128], BF16)
        epsc = sb("epsc", [128, 2], F32)
        rope = sb("rope", [128, 2, 512], F32)
        r_rope = Res("rope")
        dqk2 = [sb("dqk%d" % i, [128, 2, 512], F32) for i in range(2)]
        r_dqk2 = [Res(), Res()]
        hist = sb("hist", [128, 48, 3], F32)
        r_hist = [Res() for _ in range(48)]
        abc = sb("abc", [128, 64], F32)
        sqb = [sb("sqb%d" % i, [128, 512], BF16) for i in range(2)]
        r_sqb = [Res(), Res()]
        rs = sb("rs", [128, 512], F32)
        r_rs = Res("rs")
        AW = 26000
        ar = sb("arena", [128, AW], F32)
        pb = [st.enter_context(nc.psum_tensor("pb%d" % i, [128, 512], F32)) for i in range(8)]
        r_pb = [Res("pb%d" % i) for i in range(8)]
        bank_i = [0]

        def bank():
            i = bank_i[0] % 8
            bank_i[0] += 1
            return pb[i], r_pb[i]

        def carve(off, nelem, dt, pat=None, **kw):
            nb = nelem * (4 if dt == F32 else 2)
            assert off % 4 == 0 and nb % 4 == 0 and off + nb <= AW * 4, (off, nb)
            v = ar[:, off // 4:(off + nb) // 4]
            if dt != F32:
                v = v.bitcast(dt)
            if pat:
                v = v.rearrange(pat, **kw)
            return v

        normw = small[:, 0:64].rearrange("p (a k) -> p a k", a=4)
        retw = small[:, 64:96]
        ssmw = small[:, 96:128]
        bgate = small[:, 128:160]
        convw = small[:, 160:352].rearrange("p (c i) -> p c i", i=4)
        convb = small[:, 352:400]
        dtb = small[:, 400:464]
        alog = small[:, 464:528]
        dskip = small[:, 528:592]
        bdmask = cst[:, 0:128]
        bdU = cst[:, 128:256]
        bdones = cst[:, 256:384]
        triloc = cst[:, 384:448]
        identf = cst[:, 448:576]
        onesf = cst[:, 576:704]
        causal = cst[:, 704:832]
        sel64 = cst[:, 832:960]
        sel16 = cst[:, 960:1088]

        P.add(MQ, I("dma_start", out=small[:], in_=small_d), writes=[r_small], key="c0")
        P.add(MQ, I("dma_start", out=cst[:], in_=cst_d), writes=[r_cst], key="c1")
        r_idb = Res("identb")
        P.add("act", I("activation", out=identb[:], in_=identf, func=AF.Copy), reads=[r_cst], writes=[r_idb])
        P.add("dve", I("tensor_copy", onesb[:], onesf), reads=[r_cst], writes=[r_idb])
        P.add("dve", I("memset", epsc[:, 0:1], EPS), writes=[r_idb])
        P.add("dve", I("memset", epsc[:, 1:2], 1.0), writes=[r_idb])
        r_abc = Res("abc")
        P.add("act", I("activation", out=abc[:], in_=alog, func=AF.Exp), reads=[r_small], writes=[r_abc])
        P.add("dve", I("tensor_scalar", abc[:], abc[:], -1.0, None, op0=ALU.mult), reads=[r_abc], writes=[r_abc])

        ws_i = [0]

        wregion = {}
        first_pass = [True]

        def wload(wsrc, k0c, nkc, c0, ncols):
            s = ws_i[0] % NS
            ws_i[0] += 1
            assert nkc * ncols <= 4096
            view = wsl[s][:, 0:nkc * ncols].rearrange("p (k c) -> p k c", k=nkc)
            rk = (id(wsrc), k0c, nkc, c0, ncols)
            bfv = wsrc["bf"][k0c * 128:(k0c + nkc) * 128, c0:c0 + ncols].rearrange("(k p) c -> p k c", p=128)
            if first_pass[0]:
                assert rk not in wregion
                src = wsrc["f32"][k0c * 128:(k0c + nkc) * 128, c0:c0 + ncols].rearrange("(k p) c -> p k c", p=128)
                P.add("pool", I("dma_start", out=view, in_=src), writes=[r_ws[s]], key="w%d" % s)
                rr = Res()
                wregion[rk] = rr
                P.add(WQ, I("dma_start", out=bfv, in_=view), reads=[r_ws[s]], writes=[rr], key="wb%d" % s)
            else:
                P.add(WQ, I("dma_start", out=view, in_=bfv), reads=[wregion[rk]], writes=[r_ws[s]], key="w%d" % s)
            return view, r_ws[s]

        def mm(out, lhsT, rhs, start, stop, rd, wr):
            P.add("pe", I("matmul", out, lhsT, rhs, start=start, stop=stop, skip_group_check=True), reads=rd, writes=wr)

        def proj_fm(wap, c0, ncols_total, T, src, r_src, nk, epilogue, colw=256):
            kpl = 4096 // colw
            nkh = (nk + kpl - 1) // kpl
            for cb in range(0, ncols_total, colw):
                cw = min(colw, ncols_total - cb)
                slabs = []
                for kh in range(nkh):
                    k0 = kh * kpl
                    kn = min(kpl, nk - k0)
                    slabs.append((k0, kn) + wload(wap, k0, kn, c0 + cb, cw))
                for jj in range(0, cw, 128):
                    jw = min(128, cw - jj)
                    ps, r_ps = bank()
                    for (k0, kn, view, r_w) in slabs:
                        for kk in range(kn):
                            kc = k0 + kk
                            mm(ps[0:jw, 0:T], view[:, kk, jj:jj + jw], src(kc), kc == 0, kc == nk - 1,
                               [r_w, r_src[kc]], [r_ps])
                    epilogue((cb + jj) // 128, ps, r_ps)

        def proj_tm(wap, c0, ncols, T, TB, NTB, epilogue):
            view, r_w = wload(wap, 0, 16, c0, ncols)
            for tb in range(NTB):
                ps, r_ps = bank()
                for kc in range(16):
                    mm(ps[0:TB, 0:ncols], n[:, kc, tb * TB:(tb + 1) * TB], view[:, kc, :], kc == 0, kc == 15,
                       [r_w, r_n[kc]], [r_ps])
                epilogue(tb, ps, r_ps)

        def proj_tm_gen(wap, c0, ncols, T, TB, NTB, epilogue):
            view, r_w = wload(wap, 0, 16, c0, ncols)
            for tb in range(NTB):
                ps, r_ps = bank()
                for kc in range(16):
                    mm(ps[0:TB, 0:ncols], n[:, kc, tb * TB:(tb + 1) * TB], view[:, kc, :], kc == 0, kc == 15,
                       [r_w, r_n[kc]], [r_ps])
                epilogue(tb, ps, r_ps)
                yield

        def rmsnorm(T, wi, outf):
            ps, r_ps = bank()
            for kc in range(16):
                b = kc % 2
                P.add("act", I("activation", out=sqb[b][:, 0:T], in_=h[:, kc, 0:T], func=AF.Square),
                      reads=[r_h[kc]], writes=[r_sqb[b]])
                mm(ps[:, 0:T], onesb[:], sqb[b][:, 0:T], kc == 0, kc == 15, [r_sqb[b], r_idb], [r_ps])
            P.add("act", I("activation", out=rs[:, 0:T], in_=ps[:, 0:T], func=AF.Sqrt, bias=epsc[:, 0:1], scale=1.0 / D),
                  reads=[r_ps, r_idb], writes=[r_rs])
            P.add("dve", I("reciprocal", rs[:, 0:T], rs[:, 0:T]), reads=[r_rs], writes=[r_rs])
            for kc in range(16):
                o, r_o = outf(kc)
                P.add("dve", I("scalar_tensor_tensor", out=o, in0=h[:, kc, 0:T], scalar=normw[:, wi, kc:kc + 1],
                                                                          in1=rs[:, 0:T], op0=ALU.mult, op1=ALU.mult),
                      reads=[r_h[kc], r_rs, r_small], writes=[r_o])

        def ffn(T, wg, wu, wd):
            act = carve(0, 44 * 512, BF16, "p (j t) -> p j t", j=44)
            r_act = [Res() for _ in range(44)]
            sgt = [carve(45056 + 2048 * i, 512, F32) for i in range(2)]
            r_sgt = [Res(), Res()]
            nsrc = lambda kc: n[:, kc, 0:T]
            pend = {}

            def ep_gate(j, ps, r_ps):
                b = j % 2
                P.add("act", I("activation", out=sgt[b][:, 0:T], in_=ps[:, 0:T], func=AF.Silu),
                      reads=[r_ps], writes=[r_sgt[b]])

            def ep_up(j, ps, r_ps):
                b = j % 2
                P.add("dve", I("tensor_tensor", out=act[:, j, 0:T], in0=sgt[b][:, 0:T], in1=ps[:, 0:T], op=ALU.mult),
                      reads=[r_ps, r_sgt[b]], writes=[r_act[j]])

            for cb in range(0, DFF, 256):
                proj_fm(wg, cb, 256, T, nsrc, r_n, 16, lambda j, ps, r, cb=cb: ep_gate(cb // 128 + j, ps, r))
                proj_fm(wu, cb, 256, T, nsrc, r_n, 16, lambda j, ps, r, cb=cb: ep_up(cb // 128 + j, ps, r))

            def ep_down(j, ps, r_ps):
                P.add("dve", I("scalar_tensor_tensor", out=h[:, j, 0:T], in0=ps[:, 0:T], scalar=0.5, in1=h[:, j, 0:T],
                                                              op0=ALU.mult, op1=ALU.add),
                      reads=[r_ps, r_h[j]], writes=[r_h[j]])

            proj_fm(wd, 0, D, T, lambda kc: act[:, kc, 0:T], r_act, 44, ep_down)

        converted = [False]
        for (t0, T, C, kind) in tiles:
            TB = min(128, T)
            NTB = T // TB
            CPB = TB // C
            first = (t0 == 0) or kind == "s"
            is_last = (kind == "s") or (t0 + T == SEQ)
            gC = [math.exp(C * LG[hd]) for hd in range(8)]
            P.fence()
            for kq in range(4):
                P.add(MQ, I("dma_start", out=h[:, kq * 4:(kq + 1) * 4, 0:T],
                                                         in_=xT[kq * 512:(kq + 1) * 512, t0:t0 + T].rearrange("(k p) t -> p k t", p=128)),
                      writes=r_h[kq * 4:(kq + 1) * 4], key="x%d" % kq)
            P.add(MQ, I("dma_start", out=rope[:, :, 0:T], in_=rope_d[:, :, t0:t0 + T]), writes=[r_rope], key="c2")
            dqo = 0 if kind == "p" else 512
            if first:
                if kind == "p":
                    P.add("dve", I("memset", hist[:], 0.0), writes=r_hist)
                else:
                    P.add(MQ, I("dma_start", out=hist[:], in_=sconv_d.rearrange("p (c i) -> p c i", i=3)),
                          writes=r_hist, key="c4")
            rmsnorm(T, 0, lambda kc: (n[:, kc, 0:T], r_n[kc]))
            ffn(T, W["ffn1_w_gate"], W["ffn1_w_up"], W["ffn1_w_down"])
            if stop == "ffn1":
                continue
            P.fence()
            rmsnorm(T, 1, lambda kc: (n[:, kc, 0:T], r_n[kc]))
            nsrc = lambda kc: n[:, kc, 0:T]
            yAB = carve(0, 32 * 512, BF16, "p (j t) -> p j t", j=32)
            r_yAB = [Res() for _ in range(32)]
            mbuf = carve(32768, 16 * 512, BF16, "p (j t) -> p j t", j=16)
            r_m = [Res() for _ in range(16)]
            TOFF = 49152
            cosT = rope[:, 0, 0:T]
            sinT = rope[:, 1, 0:T]

            o = [TOFF]

            def alloc(nelem, dt, pat=None, **kw):
                nb = nelem * (4 if dt == F32 else 2)
                nb = (nb + 31) // 32 * 32
                v = carve(o[0], nb // (4 if dt == F32 else 2), dt)
                v = v[:, 0:nelem]
                if pat:
                    v = v.rearrange(pat, **kw)
                o[0] += nb
                return v

            qh = alloc(2 * 512, BF16, "p (a t) -> p a t", a=2); r_qh = Res()
            kh = alloc(2 * 512, BF16, "p (a t) -> p a t", a=2); r_kh = Res()
            t1 = alloc(512, F32); t2 = alloc(512, F32); t3 = alloc(512, F32)
            r_t1, r_t2, r_t3 = Res(), Res(), Res()
            vtm = alloc(4 * 512, BF16, "p (b c) -> p b c", b=4); r_vtm = Res()
            ktm = alloc(4 * 256, BF16, "p (b c) -> p b c", b=4); r_ktm = Res()
            sg = alloc(4 * 512, F32, "p (a t) -> p a t", a=4); r_sg = [Res() for _ in range(4)]
            oT = alloc(4 * 512, F32, "p (a t) -> p a t", a=4); r_oT = [Res() for _ in range(4)]
            Sf = alloc(2 * 512, F32, "p (a v) -> p a v", a=2); r_Sf = Res()
            Sb = alloc(2 * 512, BF16, "p (a v) -> p a v", a=2); r_Sb = Res()
            pT = [alloc(512, BF16) for _ in range(2)]; r_pT = [Res(), Res()]
            obf = alloc(512, BF16); r_obf = Res()
            mean = alloc(512, F32); msq = alloc(512, F32); r_mean, r_msq = Res(), Res()
            gtmp = [alloc(512, F32) for _ in range(2)]; r_gtmp = [Res(), Res()]

            for hd in range(8):
                dqkb = dqk2[hd % 2]
                r_dq = r_dqk2[hd % 2]
                P.add(MQ, I("dma_start", out=dqkb[:, :, 0:T], in_=dqk_d[hd, :, :, dqo:dqo + T]), writes=[r_dq], key="dq%d" % (hd % 2))
                dq = dqkb[:, 0, 0:T]
                dk = dqkb[:, 1, 0:T]
                gT = math.exp(T * LG[hd])
                if first and kind == "p":
                    P.add("dve", I("memset", Sf[:], 0.0), writes=[r_Sf])
                    P.add("dve", I("memset", Sb[:], 0.0), writes=[r_Sb])
                else:
                    src = (sret_d if first else ret_o[kind])[hd].rearrange("(a p) v -> p a v", p=128)
                    P.add(MQ, I("dma_start", out=Sf[:], in_=src), reads=[r_ret_o[kind][hd]], writes=[r_Sf], key="sr")
                    P.add("act", I("activation", out=Sb[:], in_=Sf[:], func=AF.Copy), reads=[r_Sf], writes=[r_Sb])
                for (off, dst, r_dst, dtab) in ((OFF_Q, qh, r_qh, dq), (OFF_K, kh, r_kh, dk)):
                    pss = []
                    proj_fm(W["w_in"], off + hd * 256, 256, T, nsrc, r_n, 16, lambda j, ps, r: pss.append((ps, r)))
                    (p1, r1), (p2, r2) = pss
                    dve = lambda f, rd, wr: P.add("dve", f, reads=rd, writes=wr)
                    dve(I("tensor_tensor", out=t1[:, 0:T], in0=p1[:, 0:T], in1=cosT, op=ALU.mult), [r1, r_rope], [r_t1])
                    dve(I("tensor_tensor", out=t2[:, 0:T], in0=p2[:, 0:T], in1=sinT, op=ALU.mult), [r2, r_rope], [r_t2])
                    dve(I("tensor_tensor", out=t3[:, 0:T], in0=t1[:, 0:T], in1=t2[:, 0:T], op=ALU.subtract), [r_t1, r_t2], [r_t3])
                    dve(I("tensor_tensor", out=dst[:, 0, 0:T], in0=t3[:, 0:T], in1=dtab, op=ALU.mult), [r_t3, r_dq], [r_dst])
                    dve(I("tensor_tensor", out=t1[:, 0:T], in0=p1[:, 0:T], in1=sinT, op=ALU.mult), [r1, r_rope], [r_t1])
                    dve(I("tensor_tensor", out=t2[:, 0:T], in0=p2[:, 0:T], in1=cosT, op=ALU.mult), [r2, r_rope], [r_t2])
                    dve(I("tensor_tensor", out=t3[:, 0:T], in0=t1[:, 0:T], in1=t2[:, 0:T], op=ALU.add), [r_t1, r_t2], [r_t3])
                    dve(I("tensor_tensor", out=dst[:, 1, 0:T], in0=t3[:, 0:T], in1=dtab, op=ALU.mult), [r_t3, r_dq], [r_dst])
                for half in range(2):
                    def ep_v(tb, ps, r_ps, half=half):
                        P.add("act", I("activation", out=vtm[0:TB, tb, half * 256:(half + 1) * 256], in_=ps[0:TB, 0:256], func=AF.Copy),
                              reads=[r_ps], writes=[r_vtm])
                    proj_tm(W["w_in"], OFF_V + hd * 512 + half * 256, 256, T, TB, NTB, ep_v)
                for tb in range(NTB):
                    ps, r_ps = bank()
                    for a in range(2):
                        mm(ps[0:TB, a * 128:(a + 1) * 128], kh[:, a, tb * TB:(tb + 1) * TB], identb[:], True, True, [r_kh, r_idb], [r_ps])
                    P.add("act", I("activation", out=ktm[0:TB, tb, :], in_=ps[0:TB, 0:256], func=AF.Copy, scale=gT),
                          reads=[r_ps], writes=[r_ktm])
                for bi in range(NTB):
                    isl = slice(bi * TB, (bi + 1) * TB)
                    ps, r_ps = bank()
                    for bj in range(bi + 1):
                        jsl = slice(bj * TB, (bj + 1) * TB)
                        for a in range(2):
                            mm(ps[0:TB, bj * TB:(bj + 1) * TB], kh[:, a, jsl], qh[:, a, isl], (bj == 0 and a == 0), a == 1, [r_kh, r_qh], [r_ps])
                    pb_ = bi % 2
                    if bi > 0:
                        P.add("act", I("activation", out=pT[pb_][0:TB, 0:bi * TB], in_=ps[0:TB, 0:bi * TB], func=AF.Copy),
                              reads=[r_ps], writes=[r_pT[pb_]])
                    P.add("dve", I("tensor_tensor", out=pT[pb_][0:TB, bi * TB:(bi + 1) * TB], in0=ps[0:TB, bi * TB:(bi + 1) * TB],
                                   in1=causal[0:TB, 0:TB], op=ALU.mult), reads=[r_ps, r_cst], writes=[r_pT[pb_]])
                    po_, r_po = bank()
                    for vc in range(4):
                        for bj in range(bi + 1):
                            mm(po_[:, vc * 128:vc * 128 + TB], vtm[0:TB, bj, vc * 128:(vc + 1) * 128], pT[pb_][0:TB, bj * TB:(bj + 1) * TB],
                               (vc == 0 and bj == 0), False, [r_vtm, r_pT[pb_]], [r_po])
                        for a in range(2):
                            mm(po_[:, vc * 128:vc * 128 + TB], Sb[:, a, vc * 128:(vc + 1) * 128], qh[:, a, isl], False, (a == 1), [r_Sb, r_qh], [r_po])
                    for vc in range(4):
                        P.add("act", I("activation", out=oT[:, vc, isl], in_=po_[:, vc * 128:vc * 128 + TB], func=AF.Copy),
                              reads=[r_po], writes=[r_oT[vc]])
                def ep_g(j, ps, r_ps):
                    P.add("act", I("activation", out=sg[:, j, 0:T], in_=ps[:, 0:T], func=AF.Silu), reads=[r_ps], writes=[r_sg[j]])
                for half in range(2):
                    proj_fm(W["w_in"], OFF_G + hd * 512 + half * 256, 256, T, nsrc, r_n, 16,
                            lambda j, ps, r, half=half: ep_g(half * 2 + j, ps, r))
                for a in range(2):
                    pu, r_pu = bank()
                    for tb in range(NTB):
                        mm(pu[:, :], ktm[0:TB, tb, a * 128:(a + 1) * 128], vtm[0:TB, tb, :], tb == 0, tb == NTB - 1, [r_ktm, r_vtm], [r_pu])
                    P.add("dve", I("scalar_tensor_tensor", out=Sf[:, a, :], in0=Sf[:, a, :], scalar=gT, in1=pu[:, :],
                                   op0=ALU.mult, op1=ALU.add), reads=[r_pu, r_Sf], writes=[r_Sf])
                dst = ret_o[kind][hd].rearrange("(a p) v -> p a v", p=128)
                P.add(MQ, I("dma_start", out=dst, in_=Sf[:]), reads=[r_Sf], writes=[r_ret_o[kind][hd]], key="sr")
                psm, r_psm = bank()
                pss2, r_pss2 = bank()
                for vc in range(4):
                    P.add("act", I("activation", out=obf[:, 0:T], in_=oT[:, vc, 0:T], func=AF.Copy), reads=[r_oT[vc]], writes=[r_obf])
                    mm(psm[:, 0:T], onesb[:], obf[:, 0:T], vc == 0, vc == 3, [r_obf, r_idb], [r_psm])
                    b = vc % 2
                    P.add("act", I("activation", out=sqb[b][:, 0:T], in_=oT[:, vc, 0:T], func=AF.Square), reads=[r_oT[vc]], writes=[r_sqb[b]])
                    mm(pss2[:, 0:T], onesb[:], sqb[b][:, 0:T], vc == 0, vc == 3, [r_sqb[b], r_idb], [r_pss2])
                dve = lambda f, rd, wr: P.add("dve", f, reads=rd, writes=wr)
                dve(I("tensor_scalar", mean[:, 0:T], psm[:, 0:T], 1.0 / 512, None, op0=ALU.mult), [r_psm], [r_mean])
                dve(I("tensor_tensor", out=msq[:, 0:T], in0=mean[:, 0:T], in1=mean[:, 0:T], op=ALU.mult), [r_mean], [r_msq])
                dve(I("scalar_tensor_tensor", out=msq[:, 0:T], in0=pss2[:, 0:T], scalar=1.0 / 512, in1=msq[:, 0:T],
                                                                op0=ALU.mult, op1=ALU.subtract), [r_pss2, r_msq], [r_msq])
                P.add("act", I("activation", out=msq[:, 0:T], in_=msq[:, 0:T], func=AF.Sqrt, bias=epsc[:, 0:1], scale=1.0), reads=[r_msq, r_idb], writes=[r_msq])
                dve(I("reciprocal", msq[:, 0:T], msq[:, 0:T]), [r_msq], [r_msq])
                for vc in range(4):
                    j = hd * 4 + vc
                    dve(I("tensor_tensor", out=oT[:, vc, 0:T], in0=oT[:, vc, 0:T], in1=mean[:, 0:T], op=ALU.subtract),
                        [r_oT[vc], r_mean], [r_oT[vc]])
                    dve(I("tensor_tensor", out=oT[:, vc, 0:T], in0=oT[:, vc, 0:T], in1=msq[:, 0:T], op=ALU.mult),
                        [r_oT[vc], r_msq], [r_oT[vc]])
                    dve(I("scalar_tensor_tensor", out=yAB[:, j, 0:T], in0=oT[:, vc, 0:T], scalar=retw[:, j:j + 1], in1=sg[:, vc, 0:T],
                                                                     op0=ALU.mult, op1=ALU.mult),
                        [r_oT[vc], r_sg[vc], r_small], [r_yAB[j]])

            def gate_and_proj(goff, wout, merge):
                for jp in range(0, 16, 2):
                    gl = []
                    proj_fm(W["w_in"], goff + jp * 128, 256, T, nsrc, r_n, 16, lambda j, ps, r: gl.append((ps, r)))
                    for jj, (ps, r_ps) in enumerate(gl):
                        j = jp + jj
                        b = j % 2
                        gcol = (goff - OFF_GA) // 128 + j
                        P.add("act", I("activation", out=gtmp[b][:, 0:T], in_=ps[:, 0:T], func=AF.Sigmoid,
                                                                                bias=bgate[:, gcol:gcol + 1], scale=1.0),
                              reads=[r_ps, r_small], writes=[r_gtmp[b]])
                    proj_fm(wout, jp * 128, 256, T, lambda kc: yAB[:, kc, 0:T], r_yAB, 32, lambda j, ps, r, jp=jp: merge(jp + j, ps, r))

            def merge_a(j, ps, r_ps):
                b = j % 2
                P.add("dve", I("tensor_tensor", out=mbuf[:, j, 0:T], in0=gtmp[b][:, 0:T], in1=ps[:, 0:T], op=ALU.mult),
                      reads=[r_ps, r_gtmp[b]], writes=[r_m[j]])

            gate_and_proj(OFF_GA, W["w_out_ret"], merge_a)
            if stop == "ret":
                continue
            P.fence()

            o[0] = TOFF
            cin = alloc(520, F32); r_cin = Res()
            acc = alloc(512, F32); r_acc = Res()
            xsT = [alloc(512, BF16) for _ in range(2)]; r_xsT = [Res(), Res()]
            xdt = alloc(4 * 512, BF16, "p (b c) -> p b c", b=4); r_xdt = Res()
            xw = alloc(4 * 512, BF16, "p (b c) -> p b c", b=4); r_xw = Res()
            sz = alloc(4 * 512, BF16, "p (b c) -> p b c", b=4); r_sz = Res()
            BT = alloc(512, BF16); CT = alloc(512, BF16); r_BT, r_CT = Res(), Res()
            Btm = alloc(4 * 128, BF16, "p (b c) -> p b c", b=4); r_Btm = Res()
            Hf = alloc(512, F32); r_Hf = Res()
            Hb2 = [alloc(512, BF16) for _ in range(2)]; r_Hb2 = [Res(), Res()]
            hbi = [0]
            Rb = alloc(512, F32); r_Rb = Res()
            Eb = alloc(512, F32); r_Eb = Res()
            MT2 = [alloc(8 * 128, BF16, "p (h i) -> p h i", h=8) for _ in range(2)]; r_MT2 = [Res(), Res()]
            cbm = alloc(64, F32); r_cbm = Res()
            y1 = alloc(512, F32); r_y1 = Res()
            ytm = alloc(4 * 512, F32, "p (b c) -> p b c", b=4); r_ytm = Res()
            dts = alloc(8 * 256, F32, "p (q b c) -> p q b c", q=8, b=4)
            r_dts = [Res() for _ in range(8)]
            DT, DTA, CUM, CL, DEND, ECUM, W2 = range(7)

            ssq = alloc(8, F32); r_ssq = Res()
            junk = y1; r_junk = r_y1
            dve = lambda f, rd, wr: P.add("dve", f, reads=rd, writes=wr)
            actf = lambda f, rd, wr: P.add("act", f, reads=rd, writes=wr)
            selC = sel64 if C == 64 else sel16

            def ep_dt(tb, ps, r_ps):
                dve(I("tensor_tensor", out=dts[0:TB, DT, tb, :], in0=ps[0:TB, 0:64], in1=dtb[0:TB, :], op=ALU.add), [r_ps, r_small], [r_dts[DT]])
            proj_tm(W["w_in"], OFF_DT, 64, T, TB, NTB, ep_dt)
            actf(I("activation", out=dts[0:TB, DT, 0:NTB, :], in_=dts[0:TB, DT, 0:NTB, :], func=AF.Exp), [r_dts[DT]], [r_dts[DT]])
            actf(I("activation", out=dts[0:TB, DT, 0:NTB, :], in_=dts[0:TB, DT, 0:NTB, :], func=AF.Ln, bias=epsc[0:TB, 1:2], scale=1.0), [r_dts[DT]], [r_dts[DT]])
            dve(I("tensor_tensor", out=dts[0:TB, DTA, 0:NTB, :], in0=dts[0:TB, DT, 0:NTB, :],
                                          in1=abc[0:TB, :].unsqueeze(1).to_broadcast([TB, NTB, 64]), op=ALU.mult), [r_dts[DT], r_abc], [r_dts[DTA]])
            for tb in range(NTB):
                pc, r_pc = bank()
                mm(pc[0:TB, 0:64], bdmask[0:TB, 0:TB], dts[0:TB, DTA, tb, :], True, True, [r_cst, r_dts[DTA]], [r_pc])
                mm(pc[0:TB, 64:128], bdones[0:TB, 0:TB], dts[0:TB, DTA, tb, :], True, True, [r_cst, r_dts[DTA]], [r_pc])
                actf(I("activation", out=dts[0:TB, CUM, tb, :], in_=pc[0:TB, 0:64], func=AF.Copy), [r_pc], [r_dts[CUM]])
                actf(I("activation", out=dts[0:TB, ECUM, tb, :], in_=pc[0:TB, 0:64], func=AF.Exp), [r_pc], [r_dts[ECUM]])
                dve(I("tensor_tensor", out=dts[0:TB, DEND, tb, :], in0=pc[0:TB, 64:128], in1=dts[0:TB, CUM, tb, :], op=ALU.subtract),
                    [r_pc, r_dts[CUM]], [r_dts[DEND]])
            actf(I("activation", out=dts[0:TB, DEND, 0:NTB, :], in_=dts[0:TB, DEND, 0:NTB, :], func=AF.Exp), [r_dts[DEND]], [r_dts[DEND]])
            dve(I("tensor_tensor", out=dts[0:TB, W2, 0:NTB, :], in0=dts[0:TB, DEND, 0:NTB, :], in1=dts[0:TB, DT, 0:NTB, :], op=ALU.mult),
                [r_dts[DEND], r_dts[DT]], [r_dts[W2]])

            def conv_chunk(cidx, ps, r_ps, outap, r_out):
                actf(I("activation", out=cin[:, 3:3 + T], in_=ps[:, 0:T], func=AF.Copy), [r_ps], [r_cin])
                dve(I("tensor_copy", cin[:, 0:3], hist[:, cidx, :]), [r_hist[cidx], r_cin], [r_cin])
                dve(I("tensor_scalar", acc[:, 0:T], cin[:, 0:T], convw[:, cidx, 0:1], None, op0=ALU.mult), [r_cin, r_small], [r_acc])
                for i in range(1, 4):
                    dve(I("scalar_tensor_tensor", out=acc[:, 0:T], in0=cin[:, i:i + T], scalar=convw[:, cidx, i:i + 1], in1=acc[:, 0:T],
                                                              op0=ALU.mult, op1=ALU.add), [r_cin, r_acc, r_small], [r_acc])
                dve(I("tensor_copy", hist[:, cidx, :], cin[:, T:T + 3]), [r_cin], [r_hist[cidx]])
                actf(I("activation", out=outap, in_=acc[:, 0:T], func=AF.Silu, bias=convb[:, cidx:cidx + 1], scale=1.0), [r_acc, r_small], [r_out])

            for g in range(8):
                if first and kind == "p":
                    dve(I("memset", Hf[:], 0.0), [], [r_Hf])
                    dve(I("memset", Hb2[hbi[0] % 2][:], 0.0), [], [r_Hb2[hbi[0] % 2]])
                else:
                    src = (sssm_d if first else ssm_o[kind])[:, g * 512:(g + 1) * 512]
                    P.add(MQ, I("dma_start", out=Hf[:], in_=src), reads=[r_ssm_o[kind][g]], writes=[r_Hf], key="ss")
                    actf(I("activation", out=Hb2[hbi[0] % 2][:], in_=Hf[:], func=AF.Copy), [r_Hf], [r_Hb2[hbi[0] % 2]])
                def zgen_f(g=g):
                    for half in range(2):
                        def ep_z(tb, ps, r_ps, half=half):
                            actf(I("activation", out=sz[0:TB, tb, half * 256:(half + 1) * 256], in_=ps[0:TB, 0:256], func=AF.Copy), [r_ps], [r_sz])
                        yield from proj_tm_gen(W["w_in"], OFF_Z + g * 512 + half * 256, 256, T, TB, NTB, ep_z)
                zgen = zgen_f()
                if DBG_NOILV:
                    for _ in zgen:
                        pass
                bl = []
                proj_fm(W["w_in"], OFF_X + 4096 + g * 128, 128, T, nsrc, r_n, 16, lambda j, ps, r: bl.append((ps, r)), colw=128)
                conv_chunk(32 + g, bl[0][0], bl[0][1], BT[:, 0:T], r_BT)
                cl_ = []
                proj_fm(W["w_in"], OFF_X + 5120 + g * 128, 128, T, nsrc, r_n, 16, lambda j, ps, r: cl_.append((ps, r)), colw=128)
                conv_chunk(40 + g, cl_[0][0], cl_[0][1], CT[:, 0:T], r_CT)
                for tb in range(NTB):
                    ps, r_ps = bank()
                    mm(ps[0:TB, 0:128], BT[:, tb * TB:(tb + 1) * TB], identb[:], True, True, [r_BT, r_idb], [r_ps])
                    actf(I("activation", out=Btm[0:TB, tb, :], in_=ps[0:TB, 0:128], func=AF.Copy), [r_ps], [r_Btm])
                xb = [bank() for _ in range(NTB)]
                for xpair in range(2):
                    xl = []
                    proj_fm(W["w_in"], OFF_X + g * 512 + xpair * 256, 256, T, nsrc, r_n, 16, lambda j, ps, r: xl.append((ps, r)))
                    for xi in range(2):
                        xc = xpair * 2 + xi
                        b = xc % 2
                        conv_chunk(g * 4 + xc, xl[xi][0], xl[xi][1], xsT[b][:, 0:T], r_xsT[b])
                        for tb in range(NTB):
                            mm(xb[tb][0][0:TB, xc * 128:(xc + 1) * 128], xsT[b][:, tb * TB:(tb + 1) * TB], identb[:], True, True,
                               [r_xsT[b], r_idb], [xb[tb][1]])
                for tb in range(NTB):
                    xps, r_xps = xb[tb]
                    hs = slice(g * 8, (g + 1) * 8)
                    bc = lambda q: dts[0:TB, q, tb, hs].unsqueeze(2).to_broadcast([TB, 8, 64])
                    x3 = xps[0:TB, :].rearrange("p (h q) -> p h q", h=8)
                    dve(I("tensor_tensor", out=xdt[0:TB, tb, :].rearrange("p (h q) -> p h q", h=8), in0=x3, in1=bc(DT), op=ALU.mult),
                        [r_xps, r_dts[DT]], [r_xdt])
                    dve(I("tensor_tensor", out=xw[0:TB, tb, :].rearrange("p (h q) -> p h q", h=8), in0=x3, in1=bc(W2), op=ALU.mult),
                        [r_xps, r_dts[W2]], [r_xw])
                    dve(I("tensor_tensor", out=ytm[0:TB, tb, :].rearrange("p (h q) -> p h q", h=8), in0=x3,
                                                                in1=dskip[0:TB, hs].unsqueeze(2).to_broadcast([TB, 8, 64]), op=ALU.mult),
                        [r_xps, r_small], [r_ytm])
                if g == 0:
                    dve(I("memset", MT2[0][:], 0.0), [], [r_MT2[0]])
                    dve(I("memset", MT2[1][:], 0.0), [], [r_MT2[1]])
                hs = slice(g * 8, (g + 1) * 8)

                def stage_a(tb):
                    tsl = slice(tb * TB, (tb + 1) * TB)
                    dve(I("tensor_tensor", out=Rb[0:TB, 0:8 * C].rearrange("p (h i) -> p h i", h=8),
                          in0=dts[0:TB, DTA, tb, hs].unsqueeze(2).to_broadcast([TB, 8, C]),
                          in1=triloc[0:TB, 0:C].unsqueeze(1).to_broadcast([TB, 8, C]), op=ALU.mult),
                        [r_dts[DTA], r_cst], [r_Rb])
                    ps, r_ps = bank()
                    mm(ps[0:TB, 0:TB], BT[:, tsl], CT[:, tsl], True, True, [r_BT, r_CT], [r_ps])
                    pseg, r_pseg = bank()
                    mm(pseg[0:TB, 0:8 * C], bdU[0:TB, 0:TB], Rb[0:TB, 0:8 * C], True, True, [r_cst, r_Rb], [r_pseg])
                    for c in range(CPB):
                        p0 = c * C
                        dve(I("tensor_tensor", out=cbm[p0:p0 + C, 0:C], in0=ps[p0:p0 + C, p0:p0 + C], in1=triloc[p0:p0 + C, 0:C], op=ALU.mult),
                            [r_ps, r_cst], [r_cbm])
                    actf(I("activation", out=Eb[0:TB, 0:8 * C], in_=pseg[0:TB, 0:8 * C], func=AF.Exp), [r_pseg], [r_Eb])

                def stage_b(tb):
                    MT = MT2[tb % 2]
                    r_MT = r_MT2[tb % 2]
                    for c in range(CPB):
                        p0 = c * C
                        dve(I("tensor_tensor", out=MT[p0:p0 + C, :, p0:p0 + C],
                              in0=Eb[p0:p0 + C, 0:8 * C].rearrange("p (h i) -> p h i", h=8),
                              in1=cbm[p0:p0 + C, 0:C].unsqueeze(1).to_broadcast([C, 8, C]), op=ALU.mult),
                            [r_Eb, r_cbm], [r_MT])
                    py, r_py = bank()
                    for hh in range(8):
                        mm(py[0:TB, hh * 64:(hh + 1) * 64], MT[0:TB, hh, 0:TB], xdt[0:TB, tb, hh * 64:(hh + 1) * 64], True, True, [r_MT, r_xdt], [r_py])
                    dve(I("tensor_tensor", out=ytm[0:TB, tb, :], in0=ytm[0:TB, tb, :], in1=py[0:TB, :], op=ALU.add),
                        [r_py, r_ytm], [r_ytm])

                def chain_step(tb, c):
                    tsl = slice(tb * TB, (tb + 1) * TB)
                    p0 = c * C
                    csl = slice(p0, p0 + C)
                    ph, r_ph = bank()
                    mm(ph[:, :], Btm[csl, tb, :], xw[csl, tb, :], True, True, [r_Btm, r_xw], [r_ph])
                    pcl, r_pcl = bank()
                    mm(pcl[:, 0:64], selC[csl, :], dts[csl, ECUM, tb, :], True, True, [r_cst, r_dts[ECUM]], [r_pcl])
                    hb_i = hbi[0] % 2
                    pch, r_pch = bank()
                    mm(pch[0:TB, :], CT[:, tsl], Hb2[hb_i][:], True, True, [r_CT, r_Hb2[hb_i]], [r_pch])
                    dve(I("tensor_tensor", out=Hf[:].rearrange("p (h q) -> p h q", h=8), in0=Hf[:].rearrange("p (h q) -> p h q", h=8),
                          in1=pcl[:, hs].unsqueeze(2).to_broadcast([128, 8, 64]), op=ALU.mult), [r_pcl, r_Hf], [r_Hf])
                    dve(I("tensor_tensor", out=Hf[:], in0=Hf[:], in1=ph[:, :], op=ALU.add), [r_ph, r_Hf], [r_Hf])
                    hbi[0] += 1
                    actf(I("activation", out=Hb2[hbi[0] % 2][:], in_=Hf[:], func=AF.Copy), [r_Hf], [r_Hb2[hbi[0] % 2]])
                    dve(I("tensor_tensor", out=y1[csl, :].rearrange("p (h q) -> p h q", h=8),
                          in0=pch[csl, :].rearrange("p (h q) -> p h q", h=8),
                          in1=dts[csl, ECUM, tb, hs].unsqueeze(2).to_broadcast([C, 8, 64]), op=ALU.mult),
                        [r_pch, r_dts[ECUM]], [r_y1])
                    dve(I("tensor_tensor", out=ytm[csl, tb, :], in0=ytm[csl, tb, :], in1=y1[csl, :], op=ALU.add),
                        [r_y1, r_ytm], [r_ytm])
                    next(zgen, None)

                stage_a(0)
                stage_b(0)
                for tb in range(NTB):
                    nxt = tb + 1 < NTB
                    if nxt:
                        stage_a(tb + 1)
                    chain_step(tb, 0)
                    if nxt:
                        stage_b(tb + 1)
                    for c in range(1, CPB):
                        chain_step(tb, c)
                for _ in zgen:
                    pass
                dst = ssm_o[kind][:, g * 512:(g + 1) * 512]
                P.add(MQ, I("dma_start", out=dst, in_=Hf[:]), reads=[r_Hf], writes=[r_ssm_o[kind][g]], key="ss")
                for tb in range(NTB):
                    actf(I("activation", out=sz[0:TB, tb, :], in_=sz[0:TB, tb, :], func=AF.Silu), [r_sz], [r_sz])
                for tb in range(NTB):
                    dve(I("tensor_tensor", out=ytm[0:TB, tb, :], in0=ytm[0:TB, tb, :], in1=sz[0:TB, tb, :], op=ALU.mult),
                        [r_ytm, r_sz], [r_ytm])
                    actf(I("activation", out=junk[0:TB, :], in_=ytm[0:TB, tb, :], func=AF.Square, accum_out=ssq[0:TB, tb:tb + 1]),
                         [r_ytm], [r_junk, r_ssq])
                actf(I("activation", out=ssq[0:TB, 0:NTB], in_=ssq[0:TB, 0:NTB], func=AF.Sqrt, bias=epsc[0:TB, 0:1], scale=1.0 / 512), [r_ssq, r_idb], [r_ssq])
                dve(I("reciprocal", ssq[0:TB, 0:NTB], ssq[0:TB, 0:NTB]), [r_ssq], [r_ssq])
                for tb in range(NTB):
                    dve(I("tensor_scalar", sz[0:TB, tb, :], ytm[0:TB, tb, :], ssq[0:TB, tb:tb + 1], None, op0=ALU.mult),
                        [r_ytm, r_ssq], [r_sz])
                for fc in range(4):
                    ps, r_ps = bank()
                    for tb in range(NTB):
                        mm(ps[:, tb * TB:(tb + 1) * TB], sz[0:TB, tb, fc * 128:(fc + 1) * 128], identb[0:TB, 0:TB], True, True, [r_sz, r_idb], [r_ps])
                    j = g * 4 + fc
                    actf(I("activation", out=yAB[:, j, 0:T], in_=ps[:, 0:T], func=AF.Copy, scale=ssmw[:, j:j + 1]),
                         [r_ps, r_small], [r_yAB[j]])

            def merge_b(j, ps, r_ps):
                b = j % 2
                P.add("dve", I("tensor_tensor", out=gtmp[b][:, 0:T], in0=gtmp[b][:, 0:T], in1=ps[:, 0:T], op=ALU.mult),
                      reads=[r_ps, r_gtmp[b]], writes=[r_gtmp[b]])
                P.add("dve", I("tensor_tensor", out=mbuf[:, j, 0:T], in0=gtmp[b][:, 0:T], in1=mbuf[:, j, 0:T], op=ALU.add),
                      reads=[r_gtmp[b], r_m[j]], writes=[r_m[j]])

            gate_and_proj(OFF_GB, W["w_out_ssm"], merge_b)

            if is_last:
                P.add(MQ, I("dma_start", out=conv_o[kind].rearrange("p (c i) -> p c i", i=3), in_=hist[:]),
                      reads=r_hist, key="st")

            def ep_wo(j, ps, r_ps):
                P.add("dve", I("tensor_tensor", out=h[:, j, 0:T], in0=h[:, j, 0:T], in1=ps[:, 0:T], op=ALU.add),
                      reads=[r_ps, r_h[j]], writes=[r_h[j]])
            proj_fm(W["w_out"], 0, D, T, lambda kc: mbuf[:, kc, 0:T], r_m, 16, ep_wo)
            if stop == "mix":
                continue
            P.fence()
            rmsnorm(T, 2, lambda kc: (n[:, kc, 0:T], r_n[kc]))
            ffn(T, W["ffn2_w_gate"], W["ffn2_w_up"], W["ffn2_w_down"])
            P.fence()
            first_pass[0] = False
            yo = carve(0, 16 * 512, F32, "p (k t) -> p k t", k=16)
            r_yo = [Res() for _ in range(16)]
            rmsnorm(T, 3, lambda kc: (yo[:, kc, 0:T], r_yo[kc]))
            for kq in range(4):
                P.add(MQ, I("dma_start", out=yT[kq * 512:(kq + 1) * 512, t0:t0 + T].rearrange("(k p) t -> p k t", p=128),
                                                         in_=yo[:, kq * 4:(kq + 1) * 4, 0:T]),
                      reads=r_yo[kq * 4:(kq + 1) * 4], key="y")

        if stop is not None:
            for (t0, T, C, kind) in tiles:
                pass
            P.fence()
            t0, T, C, kind = tiles[-1]
            P.add(MQ, I("dma_start", out=yT[:, t0:t0 + T].rearrange("(k p) t -> p k t", p=128), in_=h[:, :, 0:T]),
                  reads=r_h, key="y")
        P.emit_all(nc)
    return nc


def _consts():
    cst = np.zeros((128, 1088), np.float32)
    j = np.arange(128)[:, None]
    i = np.arange(128)[None, :]
    same = (j // 64) == (i // 64)
    cst[:, 0:128] = (same & (i >= j))
    cst[:, 128:256] = (same & (j > i))
    cst[:, 256:384] = same
    cst[:, 384:448] = ((np.arange(128)[:, None] % 64) <= np.arange(64)[None, :])
    cst[:, 448:576] = np.eye(128)
    cst[:, 576:704] = 1.0
    cst[:, 704:832] = (i >= j)
    cst[63, 832:960] = 1.0
    cst[127, 832:960] = 1.0
    cst[15, 960:1088] = 1.0
    pos = np.concatenate([np.arange(SEQ, dtype=np.float32), np.arange(DEC, dtype=np.float32) + np.float32(PAST)])
    inv = (np.float32(10000.0) ** (-np.arange(128, dtype=np.float32) / np.float32(128))).astype(np.float32)
    ang = (pos[None, :] * inv[:, None]).astype(np.float32)
    rope = np.stack([np.cos(ang), np.sin(ang)], axis=1).astype(np.float32)
    dqk = np.zeros((8, 128, 2, 528), np.float32)
    for hd in range(8):
        for (o_, T_) in ((0, 512), (512, 16)):
            idx = np.arange(T_, dtype=np.float64) + 1.0
            dqk[hd, :, 0, o_:o_ + T_] = np.exp(idx * LG[hd])[None, :]
            dqk[hd, :, 1, o_:o_ + T_] = (np.exp(-idx * LG[hd]) / 16.0)[None, :]
    return cst, rope, dqk


def _fm(v, nch):
    return np.ascontiguousarray(np.asarray(v, np.float32).reshape(nch, 128).T)


TILES = [(i * 512, 512, 64, "p") for i in range(8)] + [(SEQ, DEC, DEC, "s")]
_CACHE = {}


def make_in_maps(inp, cores):
    cst, rope, dqk = _consts()
    small = np.zeros((128, 640), np.float32)
    small[:, 0:16] = _fm(inp["norm_ffn1"], 16)
    small[:, 16:32] = _fm(inp["norm_mix"], 16)
    small[:, 32:48] = _fm(inp["norm_ffn2"], 16)
    small[:, 48:64] = _fm(inp["norm_final"], 16)
    small[:, 64:96] = _fm(inp["ret_norm_w"], 32)
    small[:, 96:128] = _fm(inp["ssm_norm_w"], 32)
    small[:, 128:160] = _fm(inp["b_gate"], 32)
    cw = np.asarray(inp["conv_w"], np.float32)
    small[:, 160:352] = cw.reshape(4, 48, 128).transpose(2, 1, 0).reshape(128, 192)
    small[:, 352:400] = _fm(inp["conv_b"], 48)
    small[:, 400:464] = np.asarray(inp["dt_bias"], np.float32)[None, :]
    small[:, 464:528] = np.asarray(inp["a_log"], np.float32)[None, :]
    small[:, 528:592] = np.asarray(inp["d_skip"], np.float32)[None, :]
    wnames = ["ffn1_w_gate", "ffn1_w_up", "ffn1_w_down", "w_in", "w_out_ret", "w_out_ssm", "w_out",
              "ffn2_w_gate", "ffn2_w_up", "ffn2_w_down"]
    shared = {k: np.ascontiguousarray(np.asarray(inp[k], np.float32)) for k in wnames}
    shared.update(small=small, cst=cst, rope=rope, dqk=dqk)
    maps = []
    for b in cores:
        xa = np.concatenate([np.asarray(inp["x_prompt"][b], np.float32), np.asarray(inp["x_sample"][b], np.float32)], axis=0)
        m = dict(shared)
        m["xT"] = np.ascontiguousarray(xa.T)
        m["sret"] = np.ascontiguousarray(np.asarray(inp["state_ret"][b], np.float32))
        m["sssmT"] = np.ascontiguousarray(np.asarray(inp["state_ssm"][b], np.float32).reshape(4096, 128).T)
        sc = np.asarray(inp["state_conv"][b], np.float32)
        m["sconv"] = np.ascontiguousarray(sc.reshape(3, 48, 128).transpose(2, 1, 0).reshape(128, 144))
        maps.append(m)
    return maps


def assemble(results, nb):
    y = np.stack([r["yT"].T for r in results])
    y_prompt = np.ascontiguousarray(y[:, :SEQ])
    y_sample = np.ascontiguousarray(y[:, SEQ:])

    def ssm(k):
        return np.stack([r[k].T.reshape(64, 64, 128) for r in results])

    def conv(k):
        return np.stack([r[k].reshape(128, 48, 3).transpose(2, 1, 0).reshape(3, 6144) for r in results])

    return (y_prompt, y_sample,
            np.stack([r["ret_p"] for r in results]), ssm("ssm_pT"), conv("conv_p"),
            np.stack([r["ret_s"] for r in results]), ssm("ssm_sT"), conv("conv_s"))


def kernel(**inputs):
    if "nc" not in _CACHE:
        _CACHE["nc"] = build(TILES)
    nc = _CACHE["nc"]
    maps = make_in_maps(inputs, list(range(8)))
    res = run_bass_kernel_spmd(nc, maps, core_ids=list(range(8)))
    return assemble(res.results, 8)
```

```python
import math
from contextlib import ExitStack
import numpy as np
import concourse.bass as bass
import concourse.mybir as mybir
from concourse.bass_utils import run_bass_kernel_spmd

F32 = mybir.dt.float32
BF16 = mybir.dt.bfloat16
AF = mybir.ActivationFunctionType
ALU = mybir.AluOpType

D = 2048
DFF = 5632
NKC = 16
SEQ = 4096
DEC = 16
PAST = 1024
NTOK = SEQ + DEC
EPS = 1e-6
OFF_Q, OFF_K, OFF_V, OFF_G, OFF_Z, OFF_X, OFF_DT, OFF_GA, OFF_GB = 0, 2048, 4096, 8192, 12288, 16384, 22528, 22592, 24640
LG = [math.log1p(-2.0 ** (-5.0 - h)) for h in range(8)]
DBG_NOILV = False
DBG_ONEHB = False
DBG_OLDECL = True
WQ = "sp"
MQ = "pool"
WORDER = ["ffn1_w_gate", "ffn1_w_up", "ffn1_w_down", "w_in", "w_out_ret", "w_out_ssm", "w_out",
          "ffn2_w_gate", "ffn2_w_up", "ffn2_w_down"]


class Res:
    __slots__ = ("name", "last_w", "readers")

    def __init__(self, name=""):
        self.name = name
        self.last_w = None
        self.readers = {}


class Op:
    __slots__ = ("eng", "emit", "deps", "signal", "count", "key", "pos", "is_dma")

    def __init__(self, eng, emit, key):
        self.eng = eng
        self.emit = emit
        self.deps = []
        self.signal = False
        self.count = 0
        self.key = key
        self.is_dma = key is not None
        self.pos = 0


class Prog:
    ENGS = ("pe", "act", "dve", "pool", "sp")

    def __init__(self):
        self.ops = []
        self.by_eng = {e: [] for e in self.ENGS}
        self.seen = {e: {} for e in self.ENGS}
        self.keycnt = {}
        self.cpos = {e: 0 for e in self.ENGS}
        self.last_compute = {}
        self.dma_last = {}

    def _k(self, p):
        return ("k", p.key) if p.is_dma else ("e", p.eng)

    def add(self, eng, emit, reads=(), writes=(), key=None, extra_deps=()):
        op = Op(eng, emit, key)
        idx = len(self.ops)
        self.ops.append(op)
        self.by_eng[eng].append(op)
        if key is not None:
            self.keycnt[key] = self.keycnt.get(key, 0) + 1
            op.pos = self.keycnt[key]
            self.dma_last[key] = idx
        elif emit is not None:
            self.cpos[eng] += 1
            op.pos = self.cpos[eng]
            self.last_compute[eng] = idx
        best = {}

        def cand(d):
            p = self.ops[d]
            k = self._k(p)
            if k not in best or self.ops[best[k]].pos < p.pos:
                best[k] = d

        for r in reads:
            if r.last_w is not None:
                cand(r.last_w)
        for r in writes:
            if r.last_w is not None:
                cand(r.last_w)
            for d in r.readers.values():
                cand(d)
        for d in extra_deps:
            cand(d)
        seen = self.seen[eng]
        for k, d in best.items():
            p = self.ops[d]
            if d == idx:
                continue
            if (not p.is_dma) and p.eng == "pe" and eng == "pe" and not op.is_dma and emit is not None:
                continue
            if seen.get(k, 0) >= p.pos:
                continue
            seen[k] = p.pos
            op.deps.append(d)
            p.signal = True
        if emit is not None:
            k = self._k(op)
            for r in reads:
                r.readers[k] = idx
            for r in writes:
                r.last_w = idx
                r.readers = {}
        return op

    def fence(self, engs=("pe", "act", "dve", "pool")):
        deps = [self.last_compute[e] for e in engs if e in self.last_compute]
        deps += [d for k, d in self.dma_last.items() if not (str(k).startswith("w") or str(k).startswith("cv"))]
        for e in engs:
            self.add(e, None, extra_deps=deps)

    def emit_all(self, nc, final_wait_eng="pool"):
        with ExitStack() as st:
            esem = {e: st.enter_context(nc.semaphore("s_" + e)) for e in self.ENGS}
            ksem = {k: st.enter_context(nc.semaphore("k_" + str(k))) for k in self.keycnt}
            for e in self.ENGS:
                c = 0
                for op in self.by_eng[e]:
                    if op.is_dma:
                        op.count = 16 * op.pos
                    elif op.signal:
                        c += 1
                        op.count = c
            block = st.enter_context(nc.Block())

            def run(e, eng):
                waited = {}
                for op in self.by_eng[e]:
                    for d in op.deps:
                        p = self.ops[d]
                        s = ksem[p.key] if p.is_dma else esem[p.eng]
                        wk = self._k(p)
                        if waited.get(wk, 0) >= p.count:
                            continue
                        waited[wk] = p.count
                        eng.wait_ge(s, p.count)
                    if op.emit is None:
                        continue
                    ins = op.emit(eng)
                    if op.is_dma:
                        ins.then_inc(ksem[op.key], 16)
                    elif op.signal:
                        ins.then_inc(esem[e], 1)
                if e == final_wait_eng:
                    for k, n in self.keycnt.items():
                        eng.wait_ge(ksem[k], 16 * n)

            block.tensor(lambda eng: run("pe", eng))
            block.scalar(lambda eng: run("act", eng))
            block.vector(lambda eng: run("dve", eng))
            block.gpsimd(lambda eng: run("pool", eng))
            block.sync(lambda eng: run("sp", eng))


def I(method, *a, **kw):
    return lambda e: getattr(e, method)(*a, **kw)


def build(tiles, stop=None):
    nc = bass.Bass("TRN2", target_bir_lowering=False)
    P = Prog()

    def din(name, shape, dt=F32):
        return nc.dram_tensor(name, list(shape), dt, kind="ExternalInput").ap()

    def dout(name, shape, dt=F32):
        return nc.dram_tensor(name, list(shape), dt, kind="ExternalOutput").ap()

    xT = din("xT", [D, NTOK])
    W = {}
    for nm, shp in (("ffn1_w_gate", [D, DFF]), ("ffn1_w_up", [D, DFF]), ("ffn1_w_down", [DFF, D]),
                    ("w_in", [D, 26688]), ("w_out_ret", [4096, D]), ("w_out_ssm", [4096, D]), ("w_out", [D, D]),
                    ("ffn2_w_gate", [D, DFF]), ("ffn2_w_up", [D, DFF]), ("ffn2_w_down", [DFF, D])):
        W[nm] = {"f32": din(nm, shp), "shape": shp, "res": [],
                 "bf": nc.dram_tensor(nm + "_bf", list(shp), BF16, kind="Internal").ap()}
    small_d = din("small", [128, 640])
    cst_d = din("cst", [128, 1088])
    rope_d = din("rope", [128, 2, NTOK])
    dqk_d = din("dqk", [8, 128, 2, 528])
    sret_d = din("sret", [8, 256, 512])
    sssm_d = din("sssmT", [128, 4096])
    sconv_d = din("sconv", [128, 144])
    yT = dout("yT", [D, NTOK])
    ret_o = {"p": dout("ret_p", [8, 256, 512]), "s": dout("ret_s", [8, 256, 512])}
    ssm_o = {"p": dout("ssm_pT", [128, 4096]), "s": dout("ssm_sT", [128, 4096])}
    conv_o = {"p": dout("conv_p", [128, 144]), "s": dout("conv_s", [128, 144])}
    r_ret_o = {k: [Res() for _ in range(8)] for k in "ps"}
    r_ssm_o = {k: [Res() for _ in range(8)] for k in "ps"}

    with ExitStack() as st:
        def sb(name, shape, dt):
            return st.enter_context(nc.sbuf_tensor("sb_" + name, list(shape), dt))

        h = sb("h", [128, 16, 512], F32)
        r_h = [Res("h%d" % i) for i in range(16)]
        n = sb("n", [128, 16, 512], BF16)
        r_n = [Res("n%d" % i) for i in range(16)]
        NS = 4
        wsl = [sb("ws%d" % i, [128, 4096], BF16) for i in range(NS)]
        r_ws = [Res("ws%d" % i) for i in range(NS)]
        small = sb("small", [128, 640], F32)
        r_small = Res("small")
        cst = sb("cst", [128, 1088], F32)
        r_cst = Res("cst")
        identb = sb("identb", [128, 128], BF16)
        onesb = sb("onesb", [128, 128], BF16)
        epsc = sb("epsc", [128, 2], F32)
        rope = sb("rope", [128, 2, 512], F32)
        r_rope = Res("rope")
        dqk2 = [sb("dqk%d" % i, [128, 2, 512], F32) for i in range(2)]
        r_dqk2 = [Res(), Res()]
        hist = sb("hist", [128, 48, 3], F32)
        r_hist = [Res() for _ in range(48)]
        abc = sb("abc", [128, 64], F32)
        sqb = [sb("sqb%d" % i, [128, 512], BF16) for i in range(2)]
        r_sqb = [Res(), Res()]
        rs = sb("rs", [128, 512], F32)
        r_rs = Res("rs")
        AW = 26000
        ar = sb("arena", [128, AW], F32)
        pb = [st.enter_context(nc.psum_tensor("pb%d" % i, [128, 512], F32)) for i in range(8)]
        r_pb = [Res("pb%d" % i) for i in range(8)]
        bank_i = [0]

        def bank():
            i = bank_i[0] % 8
            bank_i[0] += 1
            return pb[i], r_pb[i]

        def carve(off, nelem, dt, pat=None, **kw):
            nb = nelem * (4 if dt == F32 else 2)
            assert off % 4 == 0 and nb % 4 == 0 and off + nb <= AW * 4, (off, nb)
            v = ar[:, off // 4:(off + nb) // 4]
            if dt != F32:
                v = v.bitcast(dt)
            if pat:
                v = v.rearrange(pat, **kw)
            return v

        normw = small[:, 0:64].rearrange("p (a k) -> p a k", a=4)
        retw = small[:, 64:96]
        ssmw = small[:, 96:128]
        bgate = small[:, 128:160]
        convw = small[:, 160:352].rearrange("p (c i) -> p c i", i=4)
        convb = small[:, 352:400]
        dtb = small[:, 400:464]
        alog = small[:, 464:528]
        dskip = small[:, 528:592]
        bdmask = cst[:, 0:128]
        bdU = cst[:, 128:256]
        bdones = cst[:, 256:384]
        triloc = cst[:, 384:448]
        identf = cst[:, 448:576]
        onesf = cst[:, 576:704]
        causal = cst[:, 704:832]
        sel64 = cst[:, 832:960]
        sel16 = cst[:, 960:1088]

        P.add(MQ, I("dma_start", out=small[:], in_=small_d), writes=[r_small], key="c0")
        P.add(MQ, I("dma_start", out=cst[:], in_=cst_d), writes=[r_cst], key="c1")
        r_idb = Res("identb")
        P.add("act", I("activation", out=identb[:], in_=identf, func=AF.Copy), reads=[r_cst], writes=[r_idb])
        P.add("dve", I("tensor_copy", onesb[:], onesf), reads=[r_cst], writes=[r_idb])
        P.add("dve", I("memset", epsc[:, 0:1], EPS), writes=[r_idb])
        P.add("dve", I("memset", epsc[:, 1:2], 1.0), writes=[r_idb])
        r_abc = Res("abc")
        P.add("act", I("activation", out=abc[:], in_=alog, func=AF.Exp), reads=[r_small], writes=[r_abc])
        P.add("dve", I("tensor_scalar", abc[:], abc[:], -1.0, None, op0=ALU.mult), reads=[r_abc], writes=[r_abc])

        ws_i = [0]

        wregion = {}
        first_pass = [True]

        def wload(wsrc, k0c, nkc, c0, ncols):
            s = ws_i[0] % NS
            ws_i[0] += 1
            assert nkc * ncols <= 4096
            view = wsl[s][:, 0:nkc * ncols].rearrange("p (k c) -> p k c", k=nkc)
            rk = (id(wsrc), k0c, nkc, c0, ncols)
            bfv = wsrc["bf"][k0c * 128:(k0c + nkc) * 128, c0:c0 + ncols].rearrange("(k p) c -> p k c", p=128)
            if first_pass[0]:
                assert rk not in wregion
                src = wsrc["f32"][k0c * 128:(k0c + nkc) * 128, c0:c0 + ncols].rearrange("(k p) c -> p k c", p=128)
                P.add("pool", I("dma_start", out=view, in_=src), writes=[r_ws[s]], key="w%d" % s)
                rr = Res()
                wregion[rk] = rr
                P.add(WQ, I("dma_start", out=bfv, in_=view), reads=[r_ws[s]], writes=[rr], key="wb%d" % s)
            else:
                P.add(WQ, I("dma_start", out=view, in_=bfv), reads=[wregion[rk]], writes=[r_ws[s]], key="w%d" % s)
            return view, r_ws[s]

        def mm(out, lhsT, rhs, start, stop, rd, wr):
            P.add("pe", I("matmul", out, lhsT, rhs, start=start, stop=stop, skip_group_check=True), reads=rd, writes=wr)

        def proj_fm(wap, c0, ncols_total, T, src, r_src, nk, epilogue, colw=256):
            kpl = 4096 // colw
            nkh = (nk + kpl - 1) // kpl
            for cb in range(0, ncols_total, colw):
                cw = min(colw, ncols_total - cb)
                slabs = []
                for kh in range(nkh):
                    k0 = kh * kpl
                    kn = min(kpl, nk - k0)
                    slabs.append((k0, kn) + wload(wap, k0, kn, c0 + cb, cw))
                for jj in range(0, cw, 128):
                    jw = min(128, cw - jj)
                    ps, r_ps = bank()
                    for (k0, kn, view, r_w) in slabs:
                        for kk in range(kn):
                            kc = k0 + kk
                            mm(ps[0:jw, 0:T], view[:, kk, jj:jj + jw], src(kc), kc == 0, kc == nk - 1,
                               [r_w, r_src[kc]], [r_ps])
                    epilogue((cb + jj) // 128, ps, r_ps)

        def proj_tm(wap, c0, ncols, T, TB, NTB, epilogue):
            view, r_w = wload(wap, 0, 16, c0, ncols)
            for tb in range(NTB):
                ps, r_ps = bank()
                for kc in range(16):
                    mm(ps[0:TB, 0:ncols], n[:, kc, tb * TB:(tb + 1) * TB], view[:, kc, :], kc == 0, kc == 15,
                       [r_w, r_n[kc]], [r_ps])
                epilogue(tb, ps, r_ps)

        def proj_tm_gen(wap, c0, ncols, T, TB, NTB, epilogue):
            view, r_w = wload(wap, 0, 16, c0, ncols)
            for tb in range(NTB):
                ps, r_ps = bank()
                for kc in range(16):
                    mm(ps[0:TB, 0:ncols], n[:, kc, tb * TB:(tb + 1) * TB], view[:, kc, :], kc == 0, kc == 15,
                       [r_w, r_n[kc]], [r_ps])
                epilogue(tb, ps, r_ps)
                yield

        def rmsnorm(T, wi, outf):
            ps, r_ps = bank()
            for kc in range(16):
                b = kc % 2
                P.add("act", I("activation", out=sqb[b][:, 0:T], in_=h[:, kc, 0:T], func=AF.Square),
                      reads=[r_h[kc]], writes=[r_sqb[b]])
                mm(ps[:, 0:T], onesb[:], sqb[b][:, 0:T], kc == 0, kc == 15, [r_sqb[b], r_idb], [r_ps])
            P.add("act", I("activation", out=rs[:, 0:T], in_=ps[:, 0:T], func=AF.Sqrt, bias=epsc[:, 0:1], scale=1.0 / D),
                  reads=[r_ps, r_idb], writes=[r_rs])
            P.add("dve", I("reciprocal", rs[:, 0:T], rs[:, 0:T]), reads=[r_rs], writes=[r_rs])
            for kc in range(16):
                o, r_o = outf(kc)
                P.add("dve", I("scalar_tensor_tensor", out=o, in0=h[:, kc, 0:T], scalar=normw[:, wi, kc:kc + 1],
                                                                          in1=rs[:, 0:T], op0=ALU.mult, op1=ALU.mult),
                      reads=[r_h[kc], r_rs, r_small], writes=[r_o])

        def ffn(T, wg, wu, wd):
            act = carve(0, 44 * 512, BF16, "p (j t) -> p j t", j=44)
            r_act = [Res() for _ in range(44)]
            sgt = [carve(45056 + 2048 * i, 512, F32) for i in range(2)]
            r_sgt = [Res(), Res()]
            nsrc = lambda kc: n[:, kc, 0:T]
            pend = {}

            def ep_gate(j, ps, r_ps):
                b = j % 2
                P.add("act", I("activation", out=sgt[b][:, 0:T], in_=ps[:, 0:T], func=AF.Silu),
                      reads=[r_ps], writes=[r_sgt[b]])

            def ep_up(j, ps, r_ps):
                b = j % 2
                P.add("dve", I("tensor_tensor", out=act[:, j, 0:T], in0=sgt[b][:, 0:T], in1=ps[:, 0:T], op=ALU.mult),
                      reads=[r_ps, r_sgt[b]], writes=[r_act[j]])

            for cb in range(0, DFF, 256):
                proj_fm(wg, cb, 256, T, nsrc, r_n, 16, lambda j, ps, r, cb=cb: ep_gate(cb // 128 + j, ps, r))
                proj_fm(wu, cb, 256, T, nsrc, r_n, 16, lambda j, ps, r, cb=cb: ep_up(cb // 128 + j, ps, r))

            def ep_down(j, ps, r_ps):
                P.add("dve", I("scalar_tensor_tensor", out=h[:, j, 0:T], in0=ps[:, 0:T], scalar=0.5, in1=h[:, j, 0:T],
                                                              op0=ALU.mult, op1=ALU.add),
                      reads=[r_ps, r_h[j]], writes=[r_h[j]])

            proj_fm(wd, 0, D, T, lambda kc: act[:, kc, 0:T], r_act, 44, ep_down)

        converted = [False]
        for (t0, T, C, kind) in tiles:
            TB = min(128, T)
            NTB = T // TB
            CPB = TB // C
            first = (t0 == 0) or kind == "s"
            is_last = (kind == "s") or (t0 + T == SEQ)
            gC = [math.exp(C * LG[hd]) for hd in range(8)]
            P.fence()
            for kq in range(4):
                P.add(MQ, I("dma_start", out=h[:, kq * 4:(kq + 1) * 4, 0:T],
                                                         in_=xT[kq * 512:(kq + 1) * 512, t0:t0 + T].rearrange("(k p) t -> p k t", p=128)),
                      writes=r_h[kq * 4:(kq + 1) * 4], key="x%d" % kq)
            P.add(MQ, I("dma_start", out=rope[:, :, 0:T], in_=rope_d[:, :, t0:t0 + T]), writes=[r_rope], key="c2")
            dqo = 0 if kind == "p" else 512
            if first:
                if kind == "p":
                    P.add("dve", I("memset", hist[:], 0.0), writes=r_hist)
                else:
                    P.add(MQ, I("dma_start", out=hist[:], in_=sconv_d.rearrange("p (c i) -> p c i", i=3)),
                          writes=r_hist, key="c4")
            rmsnorm(T, 0, lambda kc: (n[:, kc, 0:T], r_n[kc]))
            ffn(T, W["ffn1_w_gate"], W["ffn1_w_up"], W["ffn1_w_down"])
            if stop == "ffn1":
                continue
            rmsnorm(T, 1, lambda kc: (n[:, kc, 0:T], r_n[kc]))
            nsrc = lambda kc: n[:, kc, 0:T]
            yAB = carve(0, 32 * 512, BF16, "p (j t) -> p j t", j=32)
            r_yAB = [Res() for _ in range(32)]
            mbuf = carve(32768, 16 * 512, BF16, "p (j t) -> p j t", j=16)
            r_m = [Res() for _ in range(16)]
            TOFF = 49152
            cosT = rope[:, 0, 0:T]
            sinT = rope[:, 1, 0:T]

            o = [TOFF]

            def alloc(nelem, dt, pat=None, **kw):
                nb = nelem * (4 if dt == F32 else 2)
                nb = (nb + 31) // 32 * 32
                v = carve(o[0], nb // (4 if dt == F32 else 2), dt)
                v = v[:, 0:nelem]
                if pat:
                    v = v.rearrange(pat, **kw)
                o[0] += nb
                return v

            qh = alloc(2 * 512, BF16, "p (a t) -> p a t", a=2); r_qh = Res()
            kh = alloc(2 * 512, BF16, "p (a t) -> p a t", a=2); r_kh = Res()
            t1 = alloc(512, F32); t2 = alloc(512, F32); t3 = alloc(512, F32)
            r_t1, r_t2, r_t3 = Res(), Res(), Res()
            vtm = alloc(4 * 512, BF16, "p (b c) -> p b c", b=4); r_vtm = Res()
            ktm = alloc(4 * 256, BF16, "p (b c) -> p b c", b=4); r_ktm = Res()
            sg = alloc(4 * 512, F32, "p (a t) -> p a t", a=4); r_sg = [Res() for _ in range(4)]
            oT = alloc(4 * 512, F32, "p (a t) -> p a t", a=4); r_oT = [Res() for _ in range(4)]
            Sf = alloc(2 * 512, F32, "p (a v) -> p a v", a=2); r_Sf = Res()
            Sb = alloc(2 * 512, BF16, "p (a v) -> p a v", a=2); r_Sb = Res()
            pT = [alloc(512, BF16) for _ in range(2)]; r_pT = [Res(), Res()]
            obf = alloc(512, BF16); r_obf = Res()
            mean = alloc(512, F32); msq = alloc(512, F32); r_mean, r_msq = Res(), Res()
            gtmp = [alloc(512, F32) for _ in range(2)]; r_gtmp = [Res(), Res()]

            for hd in range(8):
                dqkb = dqk2[hd % 2]
                r_dq = r_dqk2[hd % 2]
                P.add(MQ, I("dma_start", out=dqkb[:, :, 0:T], in_=dqk_d[hd, :, :, dqo:dqo + T]), writes=[r_dq], key="dq%d" % (hd % 2))
                dq = dqkb[:, 0, 0:T]
                dk = dqkb[:, 1, 0:T]
                gT = math.exp(T * LG[hd])
                if first and kind == "p":
                    P.add("dve", I("memset", Sf[:], 0.0), writes=[r_Sf])
                    P.add("dve", I("memset", Sb[:], 0.0), writes=[r_Sb])
                else:
                    src = (sret_d if first else ret_o[kind])[hd].rearrange("(a p) v -> p a v", p=128)
                    P.add(MQ, I("dma_start", out=Sf[:], in_=src), reads=[r_ret_o[kind][hd]], writes=[r_Sf], key="sr")
                    P.add("act", I("activation", out=Sb[:], in_=Sf[:], func=AF.Copy), reads=[r_Sf], writes=[r_Sb])
                for (off, dst, r_dst, dtab) in ((OFF_Q, qh, r_qh, dq), (OFF_K, kh, r_kh, dk)):
                    pss = []
                    proj_fm(W["w_in"], off + hd * 256, 256, T, nsrc, r_n, 16, lambda j, ps, r: pss.append((ps, r)))
                    (p1, r1), (p2, r2) = pss
                    dve = lambda f, rd, wr: P.add("dve", f, reads=rd, writes=wr)
                    dve(I("tensor_tensor", out=t1[:, 0:T], in0=p1[:, 0:T], in1=cosT, op=ALU.mult), [r1, r_rope], [r_t1])
                    dve(I("tensor_tensor", out=t2[:, 0:T], in0=p2[:, 0:T], in1=sinT, op=ALU.mult), [r2, r_rope], [r_t2])
                    dve(I("tensor_tensor", out=t3[:, 0:T], in0=t1[:, 0:T], in1=t2[:, 0:T], op=ALU.subtract), [r_t1, r_t2], [r_t3])
                    dve(I("tensor_tensor", out=dst[:, 0, 0:T], in0=t3[:, 0:T], in1=dtab, op=ALU.mult), [r_t3, r_dq], [r_dst])
                    dve(I("tensor_tensor", out=t1[:, 0:T], in0=p1[:, 0:T], in1=sinT, op=ALU.mult), [r1, r_rope], [r_t1])
                    dve(I("tensor_tensor", out=t2[:, 0:T], in0=p2[:, 0:T], in1=cosT, op=ALU.mult), [r2, r_rope], [r_t2])
                    dve(I("tensor_tensor", out=t3[:, 0:T], in0=t1[:, 0:T], in1=t2[:, 0:T], op=ALU.add), [r_t1, r_t2], [r_t3])
                    dve(I("tensor_tensor", out=dst[:, 1, 0:T], in0=t3[:, 0:T], in1=dtab, op=ALU.mult), [r_t3, r_dq], [r_dst])
                for half in range(2):
                    def ep_v(tb, ps, r_ps, half=half):
                        P.add("act", I("activation", out=vtm[0:TB, tb, half * 256:(half + 1) * 256], in_=ps[0:TB, 0:256], func=AF.Copy),
                              reads=[r_ps], writes=[r_vtm])
                    proj_tm(W["w_in"], OFF_V + hd * 512 + half * 256, 256, T, TB, NTB, ep_v)
                for tb in range(NTB):
                    ps, r_ps = bank()
                    for a in range(2):
                        mm(ps[0:TB, a * 128:(a + 1) * 128], kh[:, a, tb * TB:(tb + 1) * TB], identb[:], True, True, [r_kh, r_idb], [r_ps])
                    P.add("act", I("activation", out=ktm[0:TB, tb, :], in_=ps[0:TB, 0:256], func=AF.Copy, scale=gT),
                          reads=[r_ps], writes=[r_ktm])
                for bi in range(NTB):
                    isl = slice(bi * TB, (bi + 1) * TB)
                    ps, r_ps = bank()
                    for bj in range(bi + 1):
                        jsl = slice(bj * TB, (bj + 1) * TB)
                        for a in range(2):
                            mm(ps[0:TB, bj * TB:(bj + 1) * TB], kh[:, a, jsl], qh[:, a, isl], (bj == 0 and a == 0), a == 1, [r_kh, r_qh], [r_ps])
                    pb_ = bi % 2
                    if bi > 0:
                        P.add("act", I("activation", out=pT[pb_][0:TB, 0:bi * TB], in_=ps[0:TB, 0:bi * TB], func=AF.Copy),
                              reads=[r_ps], writes=[r_pT[pb_]])
                    P.add("dve", I("tensor_tensor", out=pT[pb_][0:TB, bi * TB:(bi + 1) * TB], in0=ps[0:TB, bi * TB:(bi + 1) * TB],
                                   in1=causal[0:TB, 0:TB], op=ALU.mult), reads=[r_ps, r_cst], writes=[r_pT[pb_]])
                    po_, r_po = bank()
                    for vc in range(4):
                        for bj in range(bi + 1):
                            mm(po_[:, vc * 128:vc * 128 + TB], vtm[0:TB, bj, vc * 128:(vc + 1) * 128], pT[pb_][0:TB, bj * TB:(bj + 1) * TB],
                               (vc == 0 and bj == 0), False, [r_vtm, r_pT[pb_]], [r_po])
                        for a in range(2):
                            mm(po_[:, vc * 128:vc * 128 + TB], Sb[:, a, vc * 128:(vc + 1) * 128], qh[:, a, isl], False, (a == 1), [r_Sb, r_qh], [r_po])
                    for vc in range(4):
                        P.add("act", I("activation", out=oT[:, vc, isl], in_=po_[:, vc * 128:vc * 128 + TB], func=AF.Copy),
                              reads=[r_po], writes=[r_oT[vc]])
                def ep_g(j, ps, r_ps):
                    P.add("act", I("activation", out=sg[:, j, 0:T], in_=ps[:, 0:T], func=AF.Silu), reads=[r_ps], writes=[r_sg[j]])
                for half in range(2):
                    proj_fm(W["w_in"], OFF_G + hd * 512 + half * 256, 256, T, nsrc, r_n, 16,
                            lambda j, ps, r, half=half: ep_g(half * 2 + j, ps, r))
                for a in range(2):
                    pu, r_pu = bank()
                    for tb in range(NTB):
                        mm(pu[:, :], ktm[0:TB, tb, a * 128:(a + 1) * 128], vtm[0:TB, tb, :], tb == 0, tb == NTB - 1, [r_ktm, r_vtm], [r_pu])
                    P.add("dve", I("scalar_tensor_tensor", out=Sf[:, a, :], in0=Sf[:, a, :], scalar=gT, in1=pu[:, :],
                                   op0=ALU.mult, op1=ALU.add), reads=[r_pu, r_Sf], writes=[r_Sf])
                dst = ret_o[kind][hd].rearrange("(a p) v -> p a v", p=128)
                P.add(MQ, I("dma_start", out=dst, in_=Sf[:]), reads=[r_Sf], writes=[r_ret_o[kind][hd]], key="sr")
                psm, r_psm = bank()
                pss2, r_pss2 = bank()
                for vc in range(4):
                    P.add("act", I("activation", out=obf[:, 0:T], in_=oT[:, vc, 0:T], func=AF.Copy), reads=[r_oT[vc]], writes=[r_obf])
                    mm(psm[:, 0:T], onesb[:], obf[:, 0:T], vc == 0, vc == 3, [r_obf, r_idb], [r_psm])
                    b = vc % 2
                    P.add("act", I("activation", out=sqb[b][:, 0:T], in_=oT[:, vc, 0:T], func=AF.Square), reads=[r_oT[vc]], writes=[r_sqb[b]])
                    mm(pss2[:, 0:T], onesb[:], sqb[b][:, 0:T], vc == 0, vc == 3, [r_sqb[b], r_idb], [r_pss2])
                dve = lambda f, rd, wr: P.add("dve", f, reads=rd, writes=wr)
                dve(I("tensor_scalar", mean[:, 0:T], psm[:, 0:T], 1.0 / 512, None, op0=ALU.mult), [r_psm], [r_mean])
                dve(I("tensor_tensor", out=msq[:, 0:T], in0=mean[:, 0:T], in1=mean[:, 0:T], op=ALU.mult), [r_mean], [r_msq])
                dve(I("scalar_tensor_tensor", out=msq[:, 0:T], in0=pss2[:, 0:T], scalar=1.0 / 512, in1=msq[:, 0:T],
                                                                op0=ALU.mult, op1=ALU.subtract), [r_pss2, r_msq], [r_msq])
                P.add("act", I("activation", out=msq[:, 0:T], in_=msq[:, 0:T], func=AF.Sqrt, bias=epsc[:, 0:1], scale=1.0), reads=[r_msq, r_idb], writes=[r_msq])
                dve(I("reciprocal", msq[:, 0:T], msq[:, 0:T]), [r_msq], [r_msq])
                for vc in range(4):
                    j = hd * 4 + vc
                    dve(I("tensor_tensor", out=oT[:, vc, 0:T], in0=oT[:, vc, 0:T], in1=mean[:, 0:T], op=ALU.subtract),
                        [r_oT[vc], r_mean], [r_oT[vc]])
                    dve(I("tensor_tensor", out=oT[:, vc, 0:T], in0=oT[:, vc, 0:T], in1=msq[:, 0:T], op=ALU.mult),
                        [r_oT[vc], r_msq], [r_oT[vc]])
                    dve(I("scalar_tensor_tensor", out=yAB[:, j, 0:T], in0=oT[:, vc, 0:T], scalar=retw[:, j:j + 1], in1=sg[:, vc, 0:T],
                                                                     op0=ALU.mult, op1=ALU.mult),
                        [r_oT[vc], r_sg[vc], r_small], [r_yAB[j]])

            def gate_and_proj(goff, wout, merge):
                for jp in range(0, 16, 2):
                    gl = []
                    proj_fm(W["w_in"], goff + jp * 128, 256, T, nsrc, r_n, 16, lambda j, ps, r: gl.append((ps, r)))
                    for jj, (ps, r_ps) in enumerate(gl):
                        j = jp + jj
                        b = j % 2
                        gcol = (goff - OFF_GA) // 128 + j
                        P.add("act", I("activation", out=gtmp[b][:, 0:T], in_=ps[:, 0:T], func=AF.Sigmoid,
                                                                                bias=bgate[:, gcol:gcol + 1], scale=1.0),
                              reads=[r_ps, r_small], writes=[r_gtmp[b]])
                    proj_fm(wout, jp * 128, 256, T, lambda kc: yAB[:, kc, 0:T], r_yAB, 32, lambda j, ps, r, jp=jp: merge(jp + j, ps, r))

            def merge_a(j, ps, r_ps):
                b = j % 2
                P.add("dve", I("tensor_tensor", out=mbuf[:, j, 0:T], in0=gtmp[b][:, 0:T], in1=ps[:, 0:T], op=ALU.mult),
                      reads=[r_ps, r_gtmp[b]], writes=[r_m[j]])

            gate_and_proj(OFF_GA, W["w_out_ret"], merge_a)
            if stop == "ret":
                continue
            P.fence()

            o[0] = TOFF
            cin = alloc(520, F32); r_cin = Res()
            acc = alloc(512, F32); r_acc = Res()
            xsT = [alloc(512, BF16) for _ in range(2)]; r_xsT = [Res(), Res()]
            xdt = alloc(4 * 512, BF16, "p (b c) -> p b c", b=4); r_xdt = Res()
            xw = alloc(4 * 512, BF16, "p (b c) -> p b c", b=4); r_xw = Res()
            sz = alloc(4 * 512, BF16, "p (b c) -> p b c", b=4); r_sz = Res()
            BT = alloc(512, BF16); CT = alloc(512, BF16); r_BT, r_CT = Res(), Res()
            Btm = alloc(4 * 128, BF16, "p (b c) -> p b c", b=4); r_Btm = Res()
            Hf = alloc(512, F32); r_Hf = Res()
            Hb2 = [alloc(512, BF16) for _ in range(2)]; r_Hb2 = [Res(), Res()]
            hbi = [0]
            Rb = alloc(512, F32); r_Rb = Res()
            Eb = alloc(512, F32); r_Eb = Res()
            MT2 = [alloc(8 * 128, BF16, "p (h i) -> p h i", h=8) for _ in range(2)]; r_MT2 = [Res(), Res()]
            cbm = alloc(64, F32); r_cbm = Res()
            y1 = alloc(512, F32); r_y1 = Res()
            ytm = alloc(4 * 512, F32, "p (b c) -> p b c", b=4); r_ytm = Res()
            dts = alloc(8 * 256, F32, "p (q b c) -> p q b c", q=8, b=4)
            r_dts = [Res() for _ in range(8)]
            DT, DTA, CUM, CL, DEND, ECUM, W2 = range(7)

            ssq = alloc(8, F32); r_ssq = Res()
            junk = y1; r_junk = r_y1
            dve = lambda f, rd, wr: P.add("dve", f, reads=rd, writes=wr)
            actf = lambda f, rd, wr: P.add("act", f, reads=rd, writes=wr)
            selC = sel64 if C == 64 else sel16

            def ep_dt(tb, ps, r_ps):
                dve(I("tensor_tensor", out=dts[0:TB, DT, tb, :], in0=ps[0:TB, 0:64], in1=dtb[0:TB, :], op=ALU.add), [r_ps, r_small], [r_dts[DT]])
            proj_tm(W["w_in"], OFF_DT, 64, T, TB, NTB, ep_dt)
            actf(I("activation", out=dts[0:TB, DT, 0:NTB, :], in_=dts[0:TB, DT, 0:NTB, :], func=AF.Exp), [r_dts[DT]], [r_dts[DT]])
            actf(I("activation", out=dts[0:TB, DT, 0:NTB, :], in_=dts[0:TB, DT, 0:NTB, :], func=AF.Ln, bias=epsc[0:TB, 1:2], scale=1.0), [r_dts[DT]], [r_dts[DT]])
            dve(I("tensor_tensor", out=dts[0:TB, DTA, 0:NTB, :], in0=dts[0:TB, DT, 0:NTB, :],
                                          in1=abc[0:TB, :].unsqueeze(1).to_broadcast([TB, NTB, 64]), op=ALU.mult), [r_dts[DT], r_abc], [r_dts[DTA]])
            for tb in range(NTB):
                pc, r_pc = bank()
                mm(pc[0:TB, 0:64], bdmask[0:TB, 0:TB], dts[0:TB, DTA, tb, :], True, True, [r_cst, r_dts[DTA]], [r_pc])
                mm(pc[0:TB, 64:128], bdones[0:TB, 0:TB], dts[0:TB, DTA, tb, :], True, True, [r_cst, r_dts[DTA]], [r_pc])
                actf(I("activation", out=dts[0:TB, CUM, tb, :], in_=pc[0:TB, 0:64], func=AF.Copy), [r_pc], [r_dts[CUM]])
                actf(I("activation", out=dts[0:TB, ECUM, tb, :], in_=pc[0:TB, 0:64], func=AF.Exp), [r_pc], [r_dts[ECUM]])
                dve(I("tensor_tensor", out=dts[0:TB, DEND, tb, :], in0=pc[0:TB, 64:128], in1=dts[0:TB, CUM, tb, :], op=ALU.subtract),
                    [r_pc, r_dts[CUM]], [r_dts[DEND]])
            actf(I("activation", out=dts[0:TB, DEND, 0:NTB, :], in_=dts[0:TB, DEND, 0:NTB, :], func=AF.Exp), [r_dts[DEND]], [r_dts[DEND]])
            dve(I("tensor_tensor", out=dts[0:TB, W2, 0:NTB, :], in0=dts[0:TB, DEND, 0:NTB, :], in1=dts[0:TB, DT, 0:NTB, :], op=ALU.mult),
                [r_dts[DEND], r_dts[DT]], [r_dts[W2]])

            def conv_chunk(cidx, ps, r_ps, outap, r_out):
                actf(I("activation", out=cin[:, 3:3 + T], in_=ps[:, 0:T], func=AF.Copy), [r_ps], [r_cin])
                dve(I("tensor_copy", cin[:, 0:3], hist[:, cidx, :]), [r_hist[cidx], r_cin], [r_cin])
                dve(I("tensor_scalar", acc[:, 0:T], cin[:, 0:T], convw[:, cidx, 0:1], None, op0=ALU.mult), [r_cin, r_small], [r_acc])
                for i in range(1, 4):
                    dve(I("scalar_tensor_tensor", out=acc[:, 0:T], in0=cin[:, i:i + T], scalar=convw[:, cidx, i:i + 1], in1=acc[:, 0:T],
                                                              op0=ALU.mult, op1=ALU.add), [r_cin, r_acc, r_small], [r_acc])
                dve(I("tensor_copy", hist[:, cidx, :], cin[:, T:T + 3]), [r_cin], [r_hist[cidx]])
                actf(I("activation", out=outap, in_=acc[:, 0:T], func=AF.Silu, bias=convb[:, cidx:cidx + 1], scale=1.0), [r_acc, r_small], [r_out])

            for g in range(8):
                if first and kind == "p":
                    dve(I("memset", Hf[:], 0.0), [], [r_Hf])
                    dve(I("memset", Hb2[hbi[0] % 2][:], 0.0), [], [r_Hb2[hbi[0] % 2]])
                else:
                    src = (sssm_d if first else ssm_o[kind])[:, g * 512:(g + 1) * 512]
                    P.add(MQ, I("dma_start", out=Hf[:], in_=src), reads=[r_ssm_o[kind][g]], writes=[r_Hf], key="ss")
                    actf(I("activation", out=Hb2[hbi[0] % 2][:], in_=Hf[:], func=AF.Copy), [r_Hf], [r_Hb2[hbi[0] % 2]])
                def zgen_f(g=g):
                    for half in range(2):
                        def ep_z(tb, ps, r_ps, half=half):
                            actf(I("activation", out=sz[0:TB, tb, half * 256:(half + 1) * 256], in_=ps[0:TB, 0:256], func=AF.Copy), [r_ps], [r_sz])
                        yield from proj_tm_gen(W["w_in"], OFF_Z + g * 512 + half * 256, 256, T, TB, NTB, ep_z)
                zgen = zgen_f()
                if DBG_NOILV:
                    for _ in zgen:
                        pass
                bl = []
                proj_fm(W["w_in"], OFF_X + 4096 + g * 128, 128, T, nsrc, r_n, 16, lambda j, ps, r: bl.append((ps, r)), colw=128)
                conv_chunk(32 + g, bl[0][0], bl[0][1], BT[:, 0:T], r_BT)
                cl_ = []
                proj_fm(W["w_in"], OFF_X + 5120 + g * 128, 128, T, nsrc, r_n, 16, lambda j, ps, r: cl_.append((ps, r)), colw=128)
                conv_chunk(40 + g, cl_[0][0], cl_[0][1], CT[:, 0:T], r_CT)
                for tb in range(NTB):
                    ps, r_ps = bank()
                    mm(ps[0:TB, 0:128], BT[:, tb * TB:(tb + 1) * TB], identb[:], True, True, [r_BT, r_idb], [r_ps])
                    actf(I("activation", out=Btm[0:TB, tb, :], in_=ps[0:TB, 0:128], func=AF.Copy), [r_ps], [r_Btm])
                xb = [bank() for _ in range(NTB)]
                for xpair in range(2):
                    xl = []
                    proj_fm(W["w_in"], OFF_X + g * 512 + xpair * 256, 256, T, nsrc, r_n, 16, lambda j, ps, r: xl.append((ps, r)))
                    for xi in range(2):
                        xc = xpair * 2 + xi
                        b = xc % 2
                        conv_chunk(g * 4 + xc, xl[xi][0], xl[xi][1], xsT[b][:, 0:T], r_xsT[b])
                        for tb in range(NTB):
                            mm(xb[tb][0][0:TB, xc * 128:(xc + 1) * 128], xsT[b][:, tb * TB:(tb + 1) * TB], identb[:], True, True,
                               [r_xsT[b], r_idb], [xb[tb][1]])
                for tb in range(NTB):
                    xps, r_xps = xb[tb]
                    hs = slice(g * 8, (g + 1) * 8)
                    bc = lambda q: dts[0:TB, q, tb, hs].unsqueeze(2).to_broadcast([TB, 8, 64])
                    x3 = xps[0:TB, :].rearrange("p (h q) -> p h q", h=8)
                    dve(I("tensor_tensor", out=xdt[0:TB, tb, :].rearrange("p (h q) -> p h q", h=8), in0=x3, in1=bc(DT), op=ALU.mult),
                        [r_xps, r_dts[DT]], [r_xdt])
                    dve(I("tensor_tensor", out=xw[0:TB, tb, :].rearrange("p (h q) -> p h q", h=8), in0=x3, in1=bc(W2), op=ALU.mult),
                        [r_xps, r_dts[W2]], [r_xw])
                    dve(I("tensor_tensor", out=ytm[0:TB, tb, :].rearrange("p (h q) -> p h q", h=8), in0=x3,
                                                                in1=dskip[0:TB, hs].unsqueeze(2).to_broadcast([TB, 8, 64]), op=ALU.mult),
                        [r_xps, r_small], [r_ytm])
                if g == 0:
                    dve(I("memset", MT2[0][:], 0.0), [], [r_MT2[0]])
                    dve(I("memset", MT2[1][:], 0.0), [], [r_MT2[1]])
                hs = slice(g * 8, (g + 1) * 8)

                def stage_a(tb):
                    tsl = slice(tb * TB, (tb + 1) * TB)
                    dve(I("tensor_tensor", out=Rb[0:TB, 0:8 * C].rearrange("p (h i) -> p h i", h=8),
                          in0=dts[0:TB, DTA, tb, hs].unsqueeze(2).to_broadcast([TB, 8, C]),
                          in1=triloc[0:TB, 0:C].unsqueeze(1).to_broadcast([TB, 8, C]), op=ALU.mult),
                        [r_dts[DTA], r_cst], [r_Rb])
                    ps, r_ps = bank()
                    mm(ps[0:TB, 0:TB], BT[:, tsl], CT[:, tsl], True, True, [r_BT, r_CT], [r_ps])
                    pseg, r_pseg = bank()
                    mm(pseg[0:TB, 0:8 * C], bdU[0:TB, 0:TB], Rb[0:TB, 0:8 * C], True, True, [r_cst, r_Rb], [r_pseg])
                    for c in range(CPB):
                        p0 = c * C
                        dve(I("tensor_tensor", out=cbm[p0:p0 + C, 0:C], in0=ps[p0:p0 + C, p0:p0 + C], in1=triloc[p0:p0 + C, 0:C], op=ALU.mult),
                            [r_ps, r_cst], [r_cbm])
                    actf(I("activation", out=Eb[0:TB, 0:8 * C], in_=pseg[0:TB, 0:8 * C], func=AF.Exp), [r_pseg], [r_Eb])

                def stage_b(tb):
                    MT = MT2[tb % 2]
                    r_MT = r_MT2[tb % 2]
                    for c in range(CPB):
                        p0 = c * C
                        dve(I("tensor_tensor", out=MT[p0:p0 + C, :, p0:p0 + C],
                              in0=Eb[p0:p0 + C, 0:8 * C].rearrange("p (h i) -> p h i", h=8),
                              in1=cbm[p0:p0 + C, 0:C].unsqueeze(1).to_broadcast([C, 8, C]), op=ALU.mult),
                            [r_Eb, r_cbm], [r_MT])
                    py, r_py = bank()
                    for hh in range(8):
                        mm(py[0:TB, hh * 64:(hh + 1) * 64], MT[0:TB, hh, 0:TB], xdt[0:TB, tb, hh * 64:(hh + 1) * 64], True, True, [r_MT, r_xdt], [r_py])
                    dve(I("tensor_tensor", out=ytm[0:TB, tb, :], in0=ytm[0:TB, tb, :], in1=py[0:TB, :], op=ALU.add),
                        [r_py, r_ytm], [r_ytm])

                def chain_step(tb, c):
                    tsl = slice(tb * TB, (tb + 1) * TB)
                    p0 = c * C
                    csl = slice(p0, p0 + C)
                    ph, r_ph = bank()
                    mm(ph[:, :], Btm[csl, tb, :], xw[csl, tb, :], True, True, [r_Btm, r_xw], [r_ph])
                    pcl, r_pcl = bank()
                    mm(pcl[:, 0:64], selC[csl, :], dts[csl, ECUM, tb, :], True, True, [r_cst, r_dts[ECUM]], [r_pcl])
                    hb_i = hbi[0] % 2
                    pch, r_pch = bank()
                    mm(pch[0:TB, :], CT[:, tsl], Hb2[hb_i][:], True, True, [r_CT, r_Hb2[hb_i]], [r_pch])
                    dve(I("tensor_tensor", out=Hf[:].rearrange("p (h q) -> p h q", h=8), in0=Hf[:].rearrange("p (h q) -> p h q", h=8),
                          in1=pcl[:, hs].unsqueeze(2).to_broadcast([128, 8, 64]), op=ALU.mult), [r_pcl, r_Hf], [r_Hf])
                    dve(I("tensor_tensor", out=Hf[:], in0=Hf[:], in1=ph[:, :], op=ALU.add), [r_ph, r_Hf], [r_Hf])
                    hbi[0] += 1
                    actf(I("activation", out=Hb2[hbi[0] % 2][:], in_=Hf[:], func=AF.Copy), [r_Hf], [r_Hb2[hbi[0] % 2]])
                    dve(I("tensor_tensor", out=y1[csl, :].rearrange("p (h q) -> p h q", h=8),
                          in0=pch[csl, :].rearrange("p (h q) -> p h q", h=8),
                          in1=dts[csl, ECUM, tb, hs].unsqueeze(2).to_broadcast([C, 8, 64]), op=ALU.mult),
                        [r_pch, r_dts[ECUM]], [r_y1])
                    dve(I("tensor_tensor", out=ytm[csl, tb, :], in0=ytm[csl, tb, :], in1=y1[csl, :], op=ALU.add),
                        [r_y1, r_ytm], [r_ytm])
                    next(zgen, None)

                stage_a(0)
                stage_b(0)
                for tb in range(NTB):
                    nxt = tb + 1 < NTB
                    if nxt:
                        stage_a(tb + 1)
                    chain_step(tb, 0)
                    if nxt:
                        stage_b(tb + 1)
                    for c in range(1, CPB):
                        chain_step(tb, c)
                for _ in zgen:
                    pass
                dst = ssm_o[kind][:, g * 512:(g + 1) * 512]
                P.add(MQ, I("dma_start", out=dst, in_=Hf[:]), reads=[r_Hf], writes=[r_ssm_o[kind][g]], key="ss")
                for tb in range(NTB):
                    actf(I("activation", out=sz[0:TB, tb, :], in_=sz[0:TB, tb, :], func=AF.Silu), [r_sz], [r_sz])
                for tb in range(NTB):
                    dve(I("tensor_tensor", out=ytm[0:TB, tb, :], in0=ytm[0:TB, tb, :], in1=sz[0:TB, tb, :], op=ALU.mult),
                        [r_ytm, r_sz], [r_ytm])
                    actf(I("activation", out=junk[0:TB, :], in_=ytm[0:TB, tb, :], func=AF.Square, accum_out=ssq[0:TB, tb:tb + 1]),
                         [r_ytm], [r_junk, r_ssq])
                actf(I("activation", out=ssq[0:TB, 0:NTB], in_=ssq[0:TB, 0:NTB], func=AF.Sqrt, bias=epsc[0:TB, 0:1], scale=1.0 / 512), [r_ssq, r_idb], [r_ssq])
                dve(I("reciprocal", ssq[0:TB, 0:NTB], ssq[0:TB, 0:NTB]), [r_ssq], [r_ssq])
                for tb in range(NTB):
                    dve(I("tensor_scalar", sz[0:TB, tb, :], ytm[0:TB, tb, :], ssq[0:TB, tb:tb + 1], None, op0=ALU.mult),
                        [r_ytm, r_ssq], [r_sz])
                for fc in range(4):
                    ps, r_ps = bank()
                    for tb in range(NTB):
                        mm(ps[:, tb * TB:(tb + 1) * TB], sz[0:TB, tb, fc * 128:(fc + 1) * 128], identb[0:TB, 0:TB], True, True, [r_sz, r_idb], [r_ps])
                    j = g * 4 + fc
                    actf(I("activation", out=yAB[:, j, 0:T], in_=ps[:, 0:T], func=AF.Copy, scale=ssmw[:, j:j + 1]),
                         [r_ps, r_small], [r_yAB[j]])

            def merge_b(j, ps, r_ps):
                b = j % 2
                P.add("dve", I("tensor_tensor", out=gtmp[b][:, 0:T], in0=gtmp[b][:, 0:T], in1=ps[:, 0:T], op=ALU.mult),
                      reads=[r_ps, r_gtmp[b]], writes=[r_gtmp[b]])
                P.add("dve", I("tensor_tensor", out=mbuf[:, j, 0:T], in0=gtmp[b][:, 0:T], in1=mbuf[:, j, 0:T], op=ALU.add),
                      reads=[r_gtmp[b], r_m[j]], writes=[r_m[j]])

            gate_and_proj(OFF_GB, W["w_out_ssm"], merge_b)

            if is_last:
                P.add(MQ, I("dma_start", out=conv_o[kind].rearrange("p (c i) -> p c i", i=3), in_=hist[:]),
                      reads=r_hist, key="st")

            def ep_wo(j, ps, r_ps):
                P.add("dve", I("tensor_tensor", out=h[:, j, 0:T], in0=h[:, j, 0:T], in1=ps[:, 0:T], op=ALU.add),
                      reads=[r_ps, r_h[j]], writes=[r_h[j]])
            proj_fm(W["w_out"], 0, D, T, lambda kc: mbuf[:, kc, 0:T], r_m, 16, ep_wo)
            if stop == "mix":
                continue
            rmsnorm(T, 2, lambda kc: (n[:, kc, 0:T], r_n[kc]))
            ffn(T, W["ffn2_w_gate"], W["ffn2_w_up"], W["ffn2_w_down"])
            first_pass[0] = False
            yo = carve(0, 16 * 512, F32, "p (k t) -> p k t", k=16)
            r_yo = [Res() for _ in range(16)]
            rmsnorm(T, 3, lambda kc: (yo[:, kc, 0:T], r_yo[kc]))
            for kq in range(4):
                P.add(MQ, I("dma_start", out=yT[kq * 512:(kq + 1) * 512, t0:t0 + T].rearrange("(k p) t -> p k t", p=128),
                                                         in_=yo[:, kq * 4:(kq + 1) * 4, 0:T]),
                      reads=r_yo[kq * 4:(kq + 1) * 4], key="y")

        if stop is not None:
            for (t0, T, C, kind) in tiles:
                pass
            P.fence()
            t0, T, C, kind = tiles[-1]
            P.add(MQ, I("dma_start", out=yT[:, t0:t0 + T].rearrange("(k p) t -> p k t", p=128), in_=h[:, :, 0:T]),
                  reads=r_h, key="y")
        P.emit_all(nc)
    return nc


def _consts():
    cst = np.zeros((128, 1088), np.float32)
    j = np.arange(128)[:, None]
    i = np.arange(128)[None, :]
    same = (j // 64) == (i // 64)
    cst[:, 0:128] = (same & (i >= j))
    cst[:, 128:256] = (same & (j > i))
    cst[:, 256:384] = same
    cst[:, 384:448] = ((np.arange(128)[:, None] % 64) <= np.arange(64)[None, :])
    cst[:, 448:576] = np.eye(128)
    cst[:, 576:704] = 1.0
    cst[:, 704:832] = (i >= j)
    cst[63, 832:960] = 1.0
    cst[127, 832:960] = 1.0
    cst[15, 960:1088] = 1.0
    pos = np.concatenate([np.arange(SEQ, dtype=np.float32), np.arange(DEC, dtype=np.float32) + np.float32(PAST)])
    inv = (np.float32(10000.0) ** (-np.arange(128, dtype=np.float32) / np.float32(128))).astype(np.float32)
    ang = (pos[None, :] * inv[:, None]).astype(np.float32)
    rope = np.stack([np.cos(ang), np.sin(ang)], axis=1).astype(np.float32)
    dqk = np.zeros((8, 128, 2, 528), np.float32)
    for hd in range(8):
        for (o_, T_) in ((0, 512), (512, 16)):
            idx = np.arange(T_, dtype=np.float64) + 1.0
            dqk[hd, :, 0, o_:o_ + T_] = np.exp(idx * LG[hd])[None, :]
            dqk[hd, :, 1, o_:o_ + T_] = (np.exp(-idx * LG[hd]) / 16.0)[None, :]
    return cst, rope, dqk


def _fm(v, nch):
    return np.ascontiguousarray(np.asarray(v, np.float32).reshape(nch, 128).T)


TILES = [(i * 512, 512, 64, "p") for i in range(8)] + [(SEQ, DEC, DEC, "s")]
_CACHE = {}


def make_in_maps(inp, cores):
    cst, rope, dqk = _consts()
    small = np.zeros((128, 640), np.float32)
    small[:, 0:16] = _fm(inp["norm_ffn1"], 16)
    small[:, 16:32] = _fm(inp["norm_mix"], 16)
    small[:, 32:48] = _fm(inp["norm_ffn2"], 16)
    small[:, 48:64] = _fm(inp["norm_final"], 16)
    small[:, 64:96] = _fm(inp["ret_norm_w"], 32)
    small[:, 96:128] = _fm(inp["ssm_norm_w"], 32)
    small[:, 128:160] = _fm(inp["b_gate"], 32)
    cw = np.asarray(inp["conv_w"], np.float32)
    small[:, 160:352] = cw.reshape(4, 48, 128).transpose(2, 1, 0).reshape(128, 192)
    small[:, 352:400] = _fm(inp["conv_b"], 48)
    small[:, 400:464] = np.asarray(inp["dt_bias"], np.float32)[None, :]
    small[:, 464:528] = np.asarray(inp["a_log"], np.float32)[None, :]
    small[:, 528:592] = np.asarray(inp["d_skip"], np.float32)[None, :]
    wnames = ["ffn1_w_gate", "ffn1_w_up", "ffn1_w_down", "w_in", "w_out_ret", "w_out_ssm", "w_out",
              "ffn2_w_gate", "ffn2_w_up", "ffn2_w_down"]
    shared = {k: np.ascontiguousarray(np.asarray(inp[k], np.float32)) for k in wnames}
    shared.update(small=small, cst=cst, rope=rope, dqk=dqk)
    maps = []
    for b in cores:
        xa = np.concatenate([np.asarray(inp["x_prompt"][b], np.float32), np.asarray(inp["x_sample"][b], np.float32)], axis=0)
        m = dict(shared)
        m["xT"] = np.ascontiguousarray(xa.T)
        m["sret"] = np.ascontiguousarray(np.asarray(inp["state_ret"][b], np.float32))
        m["sssmT"] = np.ascontiguousarray(np.asarray(inp["state_ssm"][b], np.float32).reshape(4096, 128).T)
        sc = np.asarray(inp["state_conv"][b], np.float32)
        m["sconv"] = np.ascontiguousarray(sc.reshape(3, 48, 128).transpose(2, 1, 0).reshape(128, 144))
        maps.append(m)
    return maps


def assemble(results, nb):
    y = np.stack([r["yT"].T for r in results])
    y_prompt = np.ascontiguousarray(y[:, :SEQ])
    y_sample = np.ascontiguousarray(y[:, SEQ:])

    def ssm(k):
        return np.stack([r[k].T.reshape(64, 64, 128) for r in results])

    def conv(k):
        return np.stack([r[k].reshape(128, 48, 3).transpose(2, 1, 0).reshape(3, 6144) for r in results])

    return (y_prompt, y_sample,
            np.stack([r["ret_p"] for r in results]), ssm("ssm_pT"), conv("conv_p"),
            np.stack([r["ret_s"] for r in results]), ssm("ssm_sT"), conv("conv_s"))


def kernel(**inputs):
    if "nc" not in _CACHE:
        _CACHE["nc"] = build(TILES)
    nc = _CACHE["nc"]
    maps = make_in_maps(inputs, list(range(8)))
    res = run_bass_kernel_spmd(nc, maps, core_ids=list(range(8)))
    return assemble(res.results, 8)
```
